# Optimizing a Trainium2 kernel written in Bass

```python
import math
import functools
import jax
import jax.numpy as jnp
from jax import lax
import numpy as np

D_MODEL = 1024
BATCH = 32
SEQ = 2048
DEPTH = 2
DEC_BATCH = 16
DEC_SEQ = 64
PAST_LEN = 1024

CHUNK = 64
Q_BLOCK = 128
PLE_DIM = 256
D_FF = 2816
LN_EPS = 1e-5
RMS_EPS = 1e-6
DEEPNORM_ALPHA = (2 * DEPTH) ** 0.25
DEEPNORM_BETA = (8 * DEPTH) ** -0.25
A_HEADS = 8
A_HD = 64
FORGET_BIAS_INIT = 3.0
B_HEADS = 4
B_HD = 64
B_ROT = B_HD // 4
ROPE_THETA = 500000.0
C_HEADS = 8
C_NOPE = 64
C_ROPE = 32
C_VD = 64
C_Q_LORA = 384
C_KV_LORA = 256
MLA_ROPE_THETA = 10000.0
N_BRANCH = 3
BRANCH_W = 512
MIX_SPLITS = (A_HEADS * A_HD, A_HEADS * A_HD, A_HEADS * A_HD, A_HEADS,
              2 * B_HEADS * B_HD, 2 * B_HEADS * B_HD, 2 * B_HEADS * B_HD,
              C_Q_LORA, C_KV_LORA, C_ROPE)
MIX_IN = sum(MIX_SPLITS)

kernel_name = 'hybrid_fox_diff_mla_streaming_step'


def _layer_norm(x, g, b):
    xf = x.astype(jnp.float32)
    mu = jnp.mean(xf, axis=-1, keepdims=True)
    var = jnp.mean(jnp.square(xf - mu), axis=-1, keepdims=True)
    y = (xf - mu) * lax.rsqrt(var + LN_EPS) * g.astype(jnp.float32) + b.astype(jnp.float32)
    return y.astype(x.dtype)


def _rms_norm(x, g):
    xf = x.astype(jnp.float32)
    y = xf * lax.rsqrt(jnp.mean(jnp.square(xf), axis=-1, keepdims=True) + RMS_EPS)
    return (y * g.astype(jnp.float32)).astype(x.dtype)


def _swiglu(x, w_in, w_out):
    gate, up = jnp.split(x @ w_in, 2, axis=-1)
    return (jax.nn.silu(gate) * up) @ w_out


def _rope(x, pos, theta):
    d = x.shape[-1]
    half = d // 2
    inv = jnp.exp(-math.log(theta) * jnp.arange(half, dtype=jnp.float32) * (2.0 / d))
    ang = pos.astype(jnp.float32)[:, None] * inv[None, :]
    shape = (1, pos.shape[0]) + (1,) * (x.ndim - 3) + (half,)
    cos = jnp.cos(ang).reshape(shape)
    sin = jnp.sin(ang).reshape(shape)
    xf = x.astype(jnp.float32)
    x1, x2 = xf[..., :half], xf[..., half:]
    return jnp.concatenate([x1 * cos - x2 * sin, x2 * cos + x1 * sin], axis=-1).astype(x.dtype)


def _partial_rope(x, pos):
    return jnp.concatenate([_rope(x[..., :B_ROT], pos, ROPE_THETA), x[..., B_ROT:]], axis=-1)


def _masked_softmax(logits, mask):
    return jax.nn.softmax(jnp.where(mask, logits.astype(jnp.float32), -jnp.inf), axis=-1)


def _chunk_visible(q_pos, k_pos):
    return (k_pos[None, :] // CHUNK) <= (q_pos[:, None] // CHUNK)


def _frame_visible(q_pos, k_pos):
    return k_pos[None, :] <= q_pos[:, None]


def _fox_core(q, fq, k, v, fk, q_pos, k_pos):
    s = jnp.einsum('bqhd,bkhd->bhqk', q, k).astype(jnp.float32) * (A_HD ** -0.5)
    s = s + jnp.swapaxes(fq, 1, 2)[..., :, None] - jnp.swapaxes(fk, 1, 2)[..., None, :]
    pr = _masked_softmax(s, _frame_visible(q_pos, k_pos))
    return jnp.einsum('bhqk,bkhd->bqhd', pr.astype(v.dtype), v)


def _diff_core(q, k, v, q_pos, k_pos, lam):
    s = jnp.einsum('bqhmd,bkhmd->bmhqk', q, k).astype(jnp.float32) * (B_HD ** -0.5)
    pr = _masked_softmax(s, _chunk_visible(q_pos, k_pos))
    a = pr[:, 0] - lam * pr[:, 1]
    return jnp.einsum('bhqk,bkhe->bqhe', a.astype(v.dtype), v)


def _mla_core(q_lat, q_pe, ckv, kpe, q_pos, k_pos):
    s = (jnp.einsum('bqhl,bkl->bhqk', q_lat, ckv) + jnp.einsum('bqhr,bkr->bhqk', q_pe, kpe))
    s = s.astype(jnp.float32) * ((C_NOPE + C_ROPE) ** -0.5)
    pr = _masked_softmax(s, _chunk_visible(q_pos, k_pos))
    return jnp.einsum('bhqk,bkl->bqhl', pr.astype(ckv.dtype), ckv)


def _attend(core, q_args, k_args, q_pos, k_pos, sweep):
    if not sweep:
        return core(*q_args, *k_args, q_pos, k_pos)
    n = q_args[0].shape[1]
    outs = []
    for s in range(0, n, Q_BLOCK):
        e = s + Q_BLOCK
        outs.append(core(*[a[:, s:e] for a in q_args], *[a[:, :e] for a in k_args], q_pos[s:e], k_pos[:e]))
    return jnp.concatenate(outs, axis=1)


def _token_mixing(h, pos, i, W, cache):
    bsz, t, _ = h.shape
    offsets = [int(o) for o in np.cumsum(MIX_SPLITS)[:-1]]
    qa, ka, va, fa, qb, kb, vb, dq, dkv, kr = jnp.split(h @ W['w_in_mix'][i], offsets, axis=-1)
    qa = qa.reshape(bsz, t, A_HEADS, A_HD)
    ka = ka.reshape(bsz, t, A_HEADS, A_HD)
    va = va.reshape(bsz, t, A_HEADS, A_HD)
    logf = jax.nn.log_sigmoid(fa.astype(jnp.float32) + W['b_forget'][i].astype(jnp.float32))
    qb = _partial_rope(qb.reshape(bsz, t, B_HEADS, 2, B_HD), pos)
    kb = _partial_rope(kb.reshape(bsz, t, B_HEADS, 2, B_HD), pos)
    vb = vb.reshape(bsz, t, B_HEADS, 2 * B_HD)
    lq1, lk1, lq2, lk2 = [W['diff_lambda'][i, j].astype(jnp.float32) for j in range(4)]
    lambda_init = 0.8 - 0.6 * math.exp(-0.3 * i)
    lam = jnp.exp(jnp.sum(lq1 * lk1)) - jnp.exp(jnp.sum(lq2 * lk2)) + lambda_init
    cq = (_rms_norm(dq, W['mla_q_norm_g'][i]) @ W['mla_w_uq'][i]).reshape(bsz, t, C_HEADS, C_NOPE + C_ROPE)
    w_ukv = W['mla_w_ukv'][i]
    w_uk, w_uv = w_ukv[..., :C_NOPE], w_ukv[..., C_NOPE:]
    q_lat = jnp.einsum('bthn,lhn->bthl', cq[..., :C_NOPE], w_uk)
    q_pe = _rope(cq[..., C_NOPE:], pos, MLA_ROPE_THETA)
    ckv = _rms_norm(dkv, W['mla_kv_norm_g'][i])
    kpe = _rope(kr[:, :, None, :], pos, MLA_ROPE_THETA)[:, :, 0, :]

    new_state = (ka, va, logf, kb, vb, ckv, kpe)
    if cache is None:
        sweep = True
        ka_all, va_all, kb_all, vb_all, ckv_all, kpe_all = ka, va, kb, vb, ckv, kpe
        fk = jnp.cumsum(logf, axis=1)
        fq = fk
        k_pos = pos
    else:
        sweep = False
        c_ka, c_va, c_logf, c_kb, c_vb, c_ckv, c_kpe = cache
        past = c_ka.shape[1]
        ka_all = jnp.concatenate([c_ka, ka], axis=1)
        va_all = jnp.concatenate([c_va, va], axis=1)
        kb_all = jnp.concatenate([c_kb, kb], axis=1)
        vb_all = jnp.concatenate([c_vb, vb], axis=1)
        ckv_all = jnp.concatenate([c_ckv, ckv], axis=1)
        kpe_all = jnp.concatenate([c_kpe, kpe], axis=1)
        fk = jnp.cumsum(jnp.concatenate([c_logf.astype(jnp.float32), logf], axis=1), axis=1)
        fq = fk[:, past:]
        k_pos = jnp.arange(past + t, dtype=jnp.int32)

    oa = _attend(_fox_core, (qa, fq), (ka_all, va_all, fk), pos, k_pos, sweep)
    ob = _attend(functools.partial(_diff_core, lam=lam), (qb,), (kb_all, vb_all), pos, k_pos, sweep)
    oc = _attend(_mla_core, (q_lat, q_pe), (ckv_all, kpe_all), pos, k_pos, sweep)

    oa = oa.reshape(bsz, t, BRANCH_W)
    ob = (_rms_norm(ob, W['diff_norm_g'][i]) * (1.0 - lambda_init)).reshape(bsz, t, BRANCH_W)
    oc = jnp.einsum('bthl,lhv->bthv', oc, w_uv).reshape(bsz, t, BRANCH_W)

    gates = jnp.split(jax.nn.sigmoid(h @ W['w_gate'][i] + W['b_gate'][i]), N_BRANCH, axis=-1)
    merged = (gates[0] * (oa @ W['w_branch'][i, 0])
              + gates[1] * (ob @ W['w_branch'][i, 1])
              + gates[2] * (oc @ W['w_branch'][i, 2]))
    return merged @ W['w_out'][i], new_state


def _layer(x, p, pos, i, W, cache):
    x = _layer_norm(DEEPNORM_ALPHA * x + 0.5 * _swiglu(x, W['ffn1_w_in'][i], W['ffn1_w_out'][i]),
                    W['ln_g'][i, 0], W['ln_b'][i, 0])
    mix, new_state = _token_mixing(x, pos, i, W, cache)
    x = _layer_norm(DEEPNORM_ALPHA * x + mix, W['ln_g'][i, 1], W['ln_b'][i, 1])
    x = _layer_norm(DEEPNORM_ALPHA * x + 0.5 * _swiglu(x, W['ffn2_w_in'][i], W['ffn2_w_out'][i]),
                    W['ln_g'][i, 2], W['ln_b'][i, 2])
    gate = jax.nn.sigmoid(x @ W['ple_w_gate'][i] + W['ple_b_gate'][i])
    x = _layer_norm(DEEPNORM_ALPHA * x + gate * (p @ W['ple_w_proj'][i]), W['ln_g'][i, 3], W['ln_b'][i, 3])
    return x, new_state


def setup_inputs(seed: int = 0) -> dict:
    key = jax.random.key(seed)
    k = jax.random.split(key, 32)
    D = D_MODEL

    def nrm(kk, shape, scale=1.0):
        return jax.random.normal(kk, shape, jnp.float32) * scale

    return {
        'x_prompt': nrm(k[0], (BATCH, SEQ, D)),
        'x_sample': nrm(k[1], (DEC_BATCH, DEC_SEQ, D)),
        'p_prompt': nrm(k[2], (DEPTH, BATCH, SEQ, PLE_DIM)),
        'p_sample': nrm(k[3], (DEPTH, DEC_BATCH, DEC_SEQ, PLE_DIM)),
        'cache_fox_k': nrm(k[4], (DEPTH, DEC_BATCH, PAST_LEN, A_HEADS, A_HD)),
        'cache_fox_v': nrm(k[5], (DEPTH, DEC_BATCH, PAST_LEN, A_HEADS, A_HD)),
        'cache_fox_logf': jax.nn.log_sigmoid(FORGET_BIAS_INIT + nrm(k[6], (DEPTH, DEC_BATCH, PAST_LEN, A_HEADS))),
        'cache_diff_k': nrm(k[7], (DEPTH, DEC_BATCH, PAST_LEN, B_HEADS, 2, B_HD)),
        'cache_diff_v': nrm(k[8], (DEPTH, DEC_BATCH, PAST_LEN, B_HEADS, 2 * B_HD)),
        'cache_mla_ckv': nrm(k[9], (DEPTH, DEC_BATCH, PAST_LEN, C_KV_LORA)),
        'cache_mla_kpe': nrm(k[10], (DEPTH, DEC_BATCH, PAST_LEN, C_ROPE)),
        'ffn1_w_in': nrm(k[11], (DEPTH, D, 2 * D_FF), D ** -0.5),
        'ffn1_w_out': nrm(k[12], (DEPTH, D_FF, D), DEEPNORM_BETA * D_FF ** -0.5),
        'ffn2_w_in': nrm(k[13], (DEPTH, D, 2 * D_FF), D ** -0.5),
        'ffn2_w_out': nrm(k[14], (DEPTH, D_FF, D), DEEPNORM_BETA * D_FF ** -0.5),
        'ln_g': 1.0 + nrm(k[15], (DEPTH, 4, D), 0.02),
        'ln_b': nrm(k[16], (DEPTH, 4, D), 0.02),
        'w_in_mix': nrm(k[17], (DEPTH, D, MIX_IN), D ** -0.5),
        'b_forget': FORGET_BIAS_INIT + nrm(k[18], (DEPTH, A_HEADS), 0.1),
        'diff_lambda': nrm(k[19], (DEPTH, 4, B_HD), 0.1),
        'diff_norm_g': 1.0 + nrm(k[20], (DEPTH, 2 * B_HD), 0.02),
        'mla_q_norm_g': 1.0 + nrm(k[21], (DEPTH, C_Q_LORA), 0.02),
        'mla_w_uq': nrm(k[22], (DEPTH, C_Q_LORA, C_HEADS * (C_NOPE + C_ROPE)), C_Q_LORA ** -0.5),
        'mla_kv_norm_g': 1.0 + nrm(k[23], (DEPTH, C_KV_LORA), 0.02),
        'mla_w_ukv': nrm(k[24], (DEPTH, C_KV_LORA, C_HEADS, C_NOPE + C_VD), C_KV_LORA ** -0.5),
        'w_branch': nrm(k[25], (DEPTH, N_BRANCH, BRANCH_W, D), BRANCH_W ** -0.5),
        'w_gate': nrm(k[26], (DEPTH, D, N_BRANCH * D), D ** -0.5),
        'b_gate': nrm(k[27], (DEPTH, N_BRANCH * D), 0.02),
        'w_out': nrm(k[28], (DEPTH, D, D), DEEPNORM_BETA * D ** -0.5),
        'ple_w_gate': nrm(k[29], (DEPTH, D, D), D ** -0.5),
        'ple_b_gate': nrm(k[30], (DEPTH, D), 0.02),
        'ple_w_proj': nrm(k[31], (DEPTH, PLE_DIM, D), DEEPNORM_BETA * PLE_DIM ** -0.5),
    }


def reference(x_prompt, x_sample, p_prompt, p_sample, cache_fox_k, cache_fox_v, cache_fox_logf,
              cache_diff_k, cache_diff_v, cache_mla_ckv, cache_mla_kpe,
              ffn1_w_in, ffn1_w_out, ffn2_w_in, ffn2_w_out, ln_g, ln_b, w_in_mix, b_forget,
              diff_lambda, diff_norm_g, mla_q_norm_g, mla_w_uq, mla_kv_norm_g, mla_w_ukv,
              w_branch, w_gate, b_gate, w_out, ple_w_gate, ple_b_gate, ple_w_proj):
    W = dict(ffn1_w_in=ffn1_w_in, ffn1_w_out=ffn1_w_out, ffn2_w_in=ffn2_w_in, ffn2_w_out=ffn2_w_out,
             ln_g=ln_g, ln_b=ln_b, w_in_mix=w_in_mix, b_forget=b_forget, diff_lambda=diff_lambda,
             diff_norm_g=diff_norm_g, mla_q_norm_g=mla_q_norm_g, mla_w_uq=mla_w_uq,
             mla_kv_norm_g=mla_kv_norm_g, mla_w_ukv=mla_w_ukv, w_branch=w_branch, w_gate=w_gate,
             b_gate=b_gate, w_out=w_out, ple_w_gate=ple_w_gate, ple_b_gate=ple_b_gate,
             ple_w_proj=ple_w_proj)
    pos_p = jnp.arange(x_prompt.shape[1], dtype=jnp.int32)
    past = cache_fox_k.shape[2]
    pos_s = past + jnp.arange(x_sample.shape[1], dtype=jnp.int32)

    xp, xs = x_prompt, x_sample
    states_p, states_s = [], []
    for i in range(DEPTH):
        xp, st_p = _layer(xp, p_prompt[i], pos_p, i, W, None)
        states_p.append(st_p)
        cache_i = (cache_fox_k[i], cache_fox_v[i], cache_fox_logf[i], cache_diff_k[i], cache_diff_v[i],
                   cache_mla_ckv[i], cache_mla_kpe[i])
        xs, st_s = _layer(xs, p_sample[i], pos_s, i, W, cache_i)
        states_s.append(st_s)

    fox_k_p, fox_v_p, fox_logf_p, diff_k_p, diff_v_p, mla_ckv_p, mla_kpe_p = [
        jnp.stack([st[j] for st in states_p], axis=0) for j in range(7)]
    fox_k_s, fox_v_s, fox_logf_s, diff_k_s, diff_v_s, mla_ckv_s, mla_kpe_s = [
        jnp.stack([st[j] for st in states_s], axis=0) for j in range(7)]
    return (xp, xs, fox_k_p, fox_k_s, fox_v_p, fox_v_s, fox_logf_p, fox_logf_s,
            diff_k_p, diff_k_s, diff_v_p, diff_v_s, mla_ckv_p, mla_ckv_s, mla_kpe_p, mla_kpe_s)
```

```python
import math
from collections import deque
import numpy as np
import concourse.bass as bass
import concourse.mybir as mybir
from concourse.bass_utils import run_bass_kernel_spmd

F32 = mybir.dt.float32
BF16 = mybir.dt.bfloat16
AF = mybir.ActivationFunctionType
ALU = mybir.AluOpType

ENGINES = ["pe", "act", "dve", "pool", "sp"]
ALPHA = 4.0 ** 0.25
LN_EPS = 1e-5
RMS_EPS = 1e-6
NEG = -30000.0


class T:
    __slots__ = ("h", "name", "w", "r", "sem", "cnt", "excl")

    def __init__(self, h, name=""):
        self.excl = False
        self.h = h
        self.name = name
        self.w = None
        self.r = {}
        self.sem = None
        self.cnt = 0

    def __getitem__(self, k):
        return self.h[k]


class Rec:
    __slots__ = ("fn", "waits", "inc", "dma")

    def __init__(self, fn):
        self.fn = fn
        self.waits = []
        self.inc = False
        self.dma = None


class Prog:
    def __init__(self, nc):
        self.nc = nc
        self.q = {e: [] for e in ENGINES}
        self.seen = {e: {} for e in ENGINES}
        self.esem = {}
        self._ctx = []
        self.owners = []
        for e in ENGINES:
            self.esem[e] = self._sem("s_" + e)

    def _sem(self, name):
        cm = self.nc.semaphore(name)
        s = cm.__enter__()
        self._ctx.append(cm)
        return s

    def sbuf(self, name, shape, dt):
        cm = self.nc.sbuf_tensor(name, list(shape), dt)
        h = cm.__enter__()
        self._ctx.append(cm)
        return T(h, name)

    def psum(self, name, shape, dt):
        cm = self.nc.psum_tensor(name, list(shape), dt)
        h = cm.__enter__()
        self._ctx.append(cm)
        t = T(h, name)
        t.excl = True
        return t

    def _need(self, eng, rec, ev):
        if ev is None:
            return
        if ev[0] == "eng":
            _, e2, idx = ev
            if e2 == eng and eng in ("pe", "sp"):
                return
            key = ("eng", e2)
            if self.seen[eng].get(key, -1) >= idx:
                return
            self.seen[eng][key] = idx
            self.q[e2][idx].inc = True
            rec.waits.append(ev)
        else:
            _, sem, val = ev
            key = ("dma", id(sem))
            if self.seen[eng].get(key, -1) >= val:
                return
            self.seen[eng][key] = val
            rec.waits.append(ev)

    def _deps(self, eng, rec, reads, writes):
        for t in reads:
            self._need(eng, rec, t.w)
            if t.excl:
                for k, ev in t.r.items():
                    if k != eng:
                        self._need(eng, rec, ev)
        for t in writes:
            self._need(eng, rec, t.w)
            for ev in t.r.values():
                self._need(eng, rec, ev)

    def op(self, eng, fn, reads=(), writes=()):
        rec = Rec(fn)
        self._deps(eng, rec, reads, writes)
        idx = len(self.q[eng])
        self.q[eng].append(rec)
        ev = ("eng", eng, idx)
        for t in reads:
            t.r[eng] = ev
        for t in writes:
            t.w = ev
            t.r = {}
        return ev

    def dma(self, queue, fn, reads=(), writes=(), extra=()):
        rec = Rec(fn)
        for ev in extra:
            self._need(queue, rec, ev)
        owner = (list(writes) + list(reads))[0]
        if owner.sem is None:
            owner.sem = {}
            owner.cnt = {}
        if queue not in owner.sem:
            owner.sem[queue] = self._sem("d%s_%s" % (queue[0], owner.name))
            owner.cnt[queue] = 0
            self.owners.append((owner, queue))
        sem = owner.sem[queue]
        for t in reads:
            self._need(queue, rec, t.w)
        for t in writes:
            if not (t.w is not None and t.w[0] == "dma" and t.w[1] is sem):
                self._need(queue, rec, t.w)
            for ev in t.r.values():
                self._need(queue, rec, ev)
        owner.cnt[queue] += 16
        ev = ("dma", sem, owner.cnt[queue])
        rec.dma = (sem, 16)
        self.q[queue].append(rec)
        for t in reads:
            t.r["dma%d%s" % (id(owner), queue)] = ev
        for t in writes:
            t.w = ev
            t.r = {}
        return ev

    def wait_all_dma(self, eng):
        rec = Rec(None)
        for o, qn in self.owners:
            self._need(eng, rec, ("dma", o.sem[qn], o.cnt[qn]))
        self.q[eng].append(rec)

    def finish(self):
        nc = self.nc
        pref = {}
        for e in ENGINES:
            c = 0
            arr = []
            for rec in self.q[e]:
                if rec.inc:
                    c += 1
                arr.append(c)
            pref[e] = arr
        esem = self.esem
        q = self.q

        def run(e, engobj):
            for rec in q[e]:
                for ev in rec.waits:
                    if ev[0] == "eng":
                        engobj.wait_ge(esem[ev[1]], pref[ev[1]][ev[2]])
                    else:
                        engobj.wait_ge(ev[1], ev[2])
                if rec.fn is None:
                    continue
                ins = rec.fn(engobj)
                if rec.dma is not None:
                    ins.then_inc(rec.dma[0], rec.dma[1])
                if rec.inc:
                    ins.then_inc(esem[e], 1)

        with nc.Block() as block:
            @block.tensor
            def _(eng):
                run("pe", eng)

            @block.scalar
            def _(eng):
                run("act", eng)

            @block.vector
            def _(eng):
                run("dve", eng)

            @block.gpsimd
            def _(eng):
                run("pool", eng)

            @block.sync
            def _(eng):
                run("sp", eng)
        for cm in reversed(self._ctx):
            cm.__exit__(None, None, None)
        self._ctx = []


class Ring:
    def __init__(self, tiles):
        self.t = tiles
        self.i = 0

    def next(self):
        t = self.t[self.i % len(self.t)]
        self.i += 1
        return t


OUT_SPECS = [
    ("y", 1024, False), ("fox_k", 512, True), ("fox_v", 512, True), ("fox_logf", 8, True),
    ("diff_k", 512, True), ("diff_v", 512, True), ("mla_ckv", 256, True), ("mla_kpe", 32, True)]

MIXOFF = dict(qa=0, ka=512, va=1024, qb=1536, kb=2048, vb=2560, dq=3072, dkv=3456, kr=3712, fa=3744)


class KB:
    def __init__(self, NP, NS, L=2, WSLOTS=3, stop_after=None):
        self.NP, self.NS, self.L = NP, NS, L
        self.stop_after = stop_after
        nc = bass.Bass("TRN2", target_bir_lowering=False)
        self.nc = nc
        P = Prog(nc)
        self.P = P
        d = {}
        self.d = d

        def din(name, shape):
            d[name] = nc.dram_tensor(name, list(shape), F32, kind="ExternalInput").ap()

        def dout(name, shape):
            d[name] = nc.dram_tensor(name, list(shape), F32, kind="ExternalOutput").ap()

        din("xp", [NP, 2048, 1024])
        din("ppT", [2, NP, 2, 128, 2048])
        if NS:
            din("xs", [NS, 64, 1024])
            din("psT", [2, NS, 2, 128, 64])
            din("cka", [2, NS, 4, 128, 1024])
            din("cva", [2, NS, 1024, 512])
            din("clf", [2, NS, 1024, 8])
            din("ckb", [2, NS, 4, 128, 1024])
            din("cvb", [2, NS, 1024, 512])
            din("cckT", [2, NS, 2, 128, 1024])
            din("ckpT", [2, NS, 128, 1024])
        din("w1i", [2, 1024, 5632]); din("w1o", [2, 2816, 1024])
        din("w2i", [2, 1024, 5632]); din("w2o", [2, 2816, 1024])
        din("lng", [2, 4, 1024]); din("lnb", [2, 4, 1024])
        din("wmix", [2, 1024, 3752]); din("bfg", [2, 8]); din("dlam", [2, 256]); din("dng", [2, 128])
        din("gq", [2, 384]); din("wuq", [2, 384, 768]); din("gkv", [2, 256])
        din("wuk", [2, 256, 512]); din("wuv", [2, 256, 512])
        din("wbr", [2, 3, 512, 1024]); din("wg", [2, 1024, 3072]); din("bg", [2, 128, 24])
        din("wo", [2, 1024, 1024]); din("wpg", [2, 1024, 1024]); din("bpg", [2, 1024]); din("wpp", [2, 256, 1024])
        din("c_ident", [128, 128]); din("c_maskc", [128, 128]); din("c_maskd", [128, 128]); din("c_tri", [128, 128])
        din("c_ropeB", [128, 16, 2, 8]); din("c_ropeC", [128, 16, 2, 16])
        for nm, wd, hasl in OUT_SPECS:
            dout(nm + "_p", ([2] if hasl else []) + [NP, 2048, wd])
            if NS:
                dout(nm + "_s", ([2] if hasl else []) + [NS, 64, wd])
        if stop_after == "mla":
            dout("dbg_ot", [128, 12, 512])
        self.WNAMES = ["w1i", "w1o", "wmix", "wuq", "wuk", "wuv", "wbr", "wg", "wo", "w2i", "w2o", "wpg", "wpp"]
        self.wb = {}
        self.wT = {}
        for nm in self.WNAMES:
            self.wb[nm] = nc.dram_tensor(nm + "_bf", list(d[nm].shape), BF16).ap()
            for l in range(2):
                self.wT[(nm, l)] = T(None, "c_%s%d" % (nm, l))
        self.xmid_p = nc.dram_tensor("xmid_p", [NP, 2048, 1024], F32).ap()
        self.xmid_ev = {}
        if NS:
            self.xmid_s = nc.dram_tensor("xmid_s", [NS, 64, 1024], F32).ap()

        sb = P.sbuf
        self.x = [sb("x%d" % i, [128, 1024], F32) for i in range(4)]
        self.XT = sb("XT", [128, 8, 512], BF16)
        self.QT = sb("QT", [128, 8, 512], BF16)
        self.OT = sb("OT", [128, 12, 512], BF16)
        self.tokb = Ring([sb("tokb%d" % i, [128, 1024], BF16) for i in range(2)])
        self.stg = Ring([sb("stg%d" % i, [128, 512], F32) for i in range(3)])
        self.ftr = self.stg
        self.pool_tiles = [sb("pl%d" % i, [128, 512], BF16) for i in range(12)]
        self.free = deque(self.pool_tiles)
        self.wring = Ring([sb("wr%d" % i, [128, 4096], BF16) for i in range(WSLOTS)])
        self.gb = sb("gb", [128, 1024], F32)
        self.KTa = sb("KTa", [128, 4, 2048], BF16)
        self.KTb = sb("KTb", [128, 4, 2048], BF16)
        self.KTc = sb("KTc", [128, 4, 2048], BF16)
        self.KPT = sb("KPT", [128, 2048], BF16)
        self.Va = sb("Va", [128, 16, 8, 66], BF16)
        self.Vb = sb("Vb", [128, 16, 4, 130], BF16)
        self.Vc = sb("Vc", [128, 16, 8, 66], BF16)
        self.FK = sb("FK", [128, 16, 8], F32)
        self.BK = sb("BK", [128, 16, 8], F32)
        self.carry = sb("carry", [128, 8], F32)
        self.cref = sb("cref", [128, 8], F32)
        self.lfc = sb("lfc", [128, 8, 8], F32)
        self.bf_t = sb("bf_t", [128, 8], F32)
        self.dng_t = sb("dng_t", [128, 128], F32)
        self.gq_t = sb("gq_t", [128, 384], F32)
        self.gkv_t = sb("gkv_t", [128, 256], F32)
        self.bg_t = sb("bg_t", [128, 24], F32)
        self.bhi = sb("bhi", [1, 1024], BF16)
        self.dl_t = sb("dl_t", [128, 256], F32)
        self.dl2 = sb("dl2", [128, 2, 64], F32)
        self.lam2 = sb("lam2", [128, 2], F32)
        self.neglam = sb("neglam", [128, 1], F32)
        self.ident = sb("ident", [128, 128], BF16)
        self.maskc = sb("maskc", [128, 128], BF16)
        self.maskd = sb("maskd", [128, 128], BF16)
        self.tri = sb("tri", [128, 128], F32)
        self.ones = sb("ones", [128, 128], F32)
        self.onesb = sb("onesb", [1, 128], BF16)
        self.ropeB = sb("ropeB", [128, 16, 2, 8], F32)
        self.ropeC = sb("ropeC", [128, 16, 2, 16], F32)
        self.st = sb("st", [128, 2, 6], F32)
        self.mv = sb("mv", [128, 2], F32)
        self.rstd = sb("rstd", [128, 1], F32)
        self.nmr = sb("nmr", [128, 1], F32)
        self.rc = Ring([sb("rc%d" % i, [128, 1], F32) for i in range(4)])
        self.ss = Ring([sb("ss%d" % i, [128, 1], F32) for i in range(2)])
        self.t1 = [sb("t1_%d" % i, [128, 128], F32) for i in range(4)]
        self.obf = Ring([sb("obf%d" % i, [128, 128], F32) for i in range(2)])
        self.junk = sb("junk", [128, 384], F32)
        self.e8 = Ring([sb("e8_%d" % i, [128, 8], F32) for i in range(2)])
        self.lf8 = Ring([sb("lf8_%d" % i, [128, 8], F32) for i in range(2)])
        self.kp32 = Ring([sb("kp32_%d" % i, [128, 32], F32) for i in range(2)])
        self.rt = [sb("rt%d" % i, [128, 8, 16], F32) for i in range(4)]
        self.psA = Ring([P.psum("psA%d" % i, [128, 512], F32) for i in range(4)])
        self.psS = Ring([P.psum("psS%d" % i, [128, 512], F32) for i in range(2)])
        self.psB = Ring([P.psum("psB%d" % i, [128, 1024], BF16) for i in range(2)])
        self.cp_i = 0
        print("sbuf bytes remaining:", nc.sbuf_bytes_remaining)

    def E(self, fn, reads, writes):
        return self.P.op("pe", fn, reads, writes)

    def A(self, fn, reads, writes):
        return self.P.op("act", fn, reads, writes)

    def V(self, fn, reads, writes):
        return self.P.op("dve", fn, reads, writes)

    def cp(self, oT, oap, iT, iap, eng=None):
        if eng is None:
            eng = "act" if (self.cp_i % 2 == 0) else "dve"
            self.cp_i += 1
        if eng == "act":
            self.A(lambda e: e.activation(out=oap, in_=iap, func=AF.Copy), [iT], [oT])
        else:
            self.V(lambda e: e.tensor_copy(out=oap, in_=iap), [iT], [oT])

    def mm(self, oT, oap, lT, lap, rT, rap, start, stop):
        self.E(lambda e: e.matmul(oap, lhsT=lap, rhs=rap, start=start, stop=stop), [lT, rT], [oT])

    def tr(self, oT, oap, iT, iap, n):
        ident = self.ident
        self.E(lambda e: e.transpose(out=oap, in_=iap, identity=ident[:n, :n]), [iT, ident], [oT])

    def palloc(self):
        return self.free.popleft()

    def pfree(self, t):
        self.free.append(t)

    def wget(self, src, shape, dep):
        slot = self.wring.next()
        n = int(np.prod(shape))
        assert n <= 4096
        dst = slot.h[:src.shape[0], 0:n]
        if len(shape) == 2:
            dst = dst.rearrange("p (a b) -> p a b", a=shape[0])
        elif len(shape) == 3:
            dst = dst.rearrange("p (a b c) -> p a b c", a=shape[0], b=shape[1])
        self.P.dma("sp", lambda e: e.dma_start(out=dst, in_=src), reads=[dep], writes=[slot])
        return slot, dst

    def load_gb(self, src_row):
        gb = self.gb
        self.P.dma("pool", lambda e: e.dma_start(out=gb[:, :], in_=src_row.to_broadcast([128, 1024])), writes=[gb])

    def out_dma(self, name, l, s, tok0, n, sT, sap):
        dst = self.d[name + ("_p" if self.kind == "p" else "_s")]
        dst = dst[l, s, tok0:tok0 + n, :] if l is not None else dst[s, tok0:tok0 + n, :]
        self.P.dma("pool", lambda e: e.dma_start(out=dst, in_=sap), reads=[sT])

    def setup(self):
        P, d = self.P, self.d
        for nm, t in (("c_ident", self.ident), ("c_maskc", self.maskc), ("c_maskd", self.maskd)):
            P.dma("pool", lambda e, nm=nm, t=t: e.dma_start(out=t[:, :], in_=d[nm]), writes=[t])
        for nm, t in (("c_tri", self.tri), ("c_ropeB", self.ropeB), ("c_ropeC", self.ropeC)):
            P.dma("sp", lambda e, nm=nm, t=t: e.dma_start(out=t[:], in_=d[nm]), writes=[t])
        for l in range(self.L):
            for nm in self.WNAMES:
                src = d[nm][l]
                dst = self.wb[nm][l]
                if len(src.shape) == 3:
                    src = src.rearrange("b k n -> (b k) n")
                    dst = dst.rearrange("b k n -> (b k) n")
                P.dma("pool", lambda e, src=src, dst=dst: e.dma_start(out=dst, in_=src, max_dma_last_dim=8192), writes=[self.wT[(nm, l)]])
        self.V(lambda e: e.memset(self.ones[:, :], 1.0), [], [self.ones])
        self.V(lambda e: e.memset(self.onesb[:, :], 1.0), [], [self.onesb])
        self.V(lambda e: e.memset(self.QT[:, :, :], 0.0), [], [self.QT])
        self.V(lambda e: e.memset(self.Va[:, :, :, 64:65], 1.0), [], [self.Va])
        self.V(lambda e: e.memset(self.Vb[:, :, :, 128:129], 1.0), [], [self.Vb])
        self.V(lambda e: e.memset(self.Vc[:, :, :, 64:65], 1.0), [], [self.Vc])

    def layer_params(self, l):
        P, d = self.P, self.d
        ld = lambda t, src: P.dma("sp", lambda e: e.dma_start(out=t[:], in_=src), writes=[t])
        ld(self.bf_t, d["bfg"][l:l + 1, :].to_broadcast([128, 8]))
        ld(self.dng_t, d["dng"][l:l + 1, :].to_broadcast([128, 128]))
        ld(self.gq_t, d["gq"][l:l + 1, :].to_broadcast([128, 384]))
        ld(self.gkv_t, d["gkv"][l:l + 1, :].to_broadcast([128, 256]))
        ld(self.bg_t, d["bg"][l])
        r32 = self.x[0]
        P.dma("sp", lambda e: e.dma_start(out=r32[0:1, :], in_=d["bpg"][l:l + 1, :]), writes=[r32])
        ld(self.dl_t, d["dlam"][l:l + 1, :].to_broadcast([128, 256]))
        lam_init = 0.8 - 0.6 * math.exp(-0.3 * l)
        self.lam_init = lam_init
        self.V(lambda e: e.tensor_scalar(self.dng_t[:, :], self.dng_t[:, :], 1.0 - lam_init, None, ALU.mult), [self.dng_t], [self.dng_t])
        self.V(lambda e: e.tensor_copy(out=self.bhi[:, :], in_=r32[0:1, :]), [r32], [self.bhi])
        dl = self.dl_t
        self.V(lambda e: e.tensor_tensor(out=self.dl2[:, 0, :], in0=dl[:, 0:64], in1=dl[:, 64:128], op=ALU.mult), [dl], [self.dl2])
        self.V(lambda e: e.tensor_tensor(out=self.dl2[:, 1, :], in0=dl[:, 128:192], in1=dl[:, 192:256], op=ALU.mult), [dl, self.dl2], [self.dl2])
        self.V(lambda e: e.reduce_sum(out=self.lam2[:, :], in_=self.dl2[:, :, :], axis=mybir.AxisListType.X), [self.dl2], [self.lam2])
        self.A(lambda e: e.activation(out=self.lam2[:, :], in_=self.lam2[:, :], func=AF.Exp), [self.lam2], [self.lam2])
        self.V(lambda e: e.tensor_tensor(out=self.neglam[:, :], in0=self.lam2[:, 1:2], in1=self.lam2[:, 0:1], op=ALU.subtract), [self.lam2], [self.neglam])
        self.V(lambda e: e.tensor_scalar(self.neglam[:, :], self.neglam[:, :], -lam_init, None, ALU.add), [self.neglam], [self.neglam])

    def make_XT(self):
        nt, ntl = self.nt, self.ntl
        XT = self.XT
        for t in range(ntl):
            tb = self.tokb.next()
            x = self.x[t]
            self.cp(tb, tb[:nt, :], x, x[:nt, :])
            pb = self.psB.next()
            for c in range(8):
                self.tr(pb, pb[:, c * nt:(c + 1) * nt], tb, tb[:nt, c * 128:(c + 1) * 128], nt)
            self.cp(XT, XT[:, :, t * nt:(t + 1) * nt], pb, pb[:, 0:8 * nt].rearrange("p (c n) -> p c n", c=8))

    def layernorm(self, l, k):
        nt, ntl = self.nt, self.ntl
        d = self.d
        gb = self.gb
        self.load_gb(d["lng"][l, k:k + 1, :])
        for t in range(ntl):
            x = self.x[t]
            st, mv, rstd, nmr = self.st, self.mv, self.rstd, self.nmr
            for i in range(2):
                self.V(lambda e, i=i, x=x: e.bn_stats(out=st[:nt, i, :], in_=x[:nt, i * 512:(i + 1) * 512]), [x], [st])
            self.V(lambda e: e.bn_aggr(out=mv[:nt, :], in_=st[:nt, :, :].rearrange("p a b -> p (a b)")), [st], [mv])
            self.A(lambda e: e.activation(out=rstd[:nt, :], in_=mv[:nt, 1:2], func=AF.Sqrt, bias=LN_EPS, scale=1.0), [mv], [rstd])
            self.V(lambda e: e.reciprocal(out=rstd[:nt, :], in_=rstd[:nt, :]), [rstd], [rstd])
            self.V(lambda e: e.tensor_scalar(nmr[:nt, :], mv[:nt, 0:1], -1.0, rstd[:nt, 0:1], ALU.mult, ALU.mult), [mv, rstd], [nmr])
            self.A(lambda e, x=x: e.activation(out=x[:nt, :], in_=x[:nt, :], func=AF.Identity, bias=nmr[:nt, 0:1], scale=rstd[:nt, 0:1]), [x, nmr, rstd], [x])
            self.V(lambda e, x=x: e.tensor_tensor(out=x[:nt, :], in0=x[:nt, :], in1=gb[:nt, :], op=ALU.mult), [x, gb], [x])
        self.load_gb(d["lnb"][l, k:k + 1, :])
        for t in range(ntl):
            x = self.x[t]
            self.V(lambda e, x=x: e.tensor_tensor(out=x[:nt, :], in0=x[:nt, :], in1=gb[:nt, :], op=ALU.add), [x, gb], [x])

    def ffn(self, l, which):
        nt, ntl, NT = self.nt, self.ntl, self.NT
        wn_i = "w1i" if which == 1 else "w2i"
        wn_o = "w1o" if which == 1 else "w2o"
        w_in = self.wb[wn_i][l].rearrange("(c p) (g f) -> p c g f", p=128, g=2)
        w_out = self.wb[wn_o][l].rearrange("(c p) n -> p c n", p=128)
        XT = self.XT
        self.make_XT()
        for half in range(2):
            HT = [self.palloc() for _ in range(11)]
            for jj in range(0, 11, 2):
                nj = min(2, 11 - jj)
                j0 = half * 11 + jj
                slot = self.wring.next()
                wv = slot.h[:, 0:16 * nj * 128].rearrange("p (a b c) -> p a b c", a=8, b=2)
                for g in range(2):
                    self.P.dma("sp", lambda e, g=g, wv=wv, j0=j0, nj=nj: e.dma_start(out=wv[:, :, g, :], in_=w_in[:, :, g, j0 * 128:(j0 + nj) * 128]), reads=[self.wT[(wn_i, l)]], writes=[slot])
                for q in range(nj):
                    pg = self.psA.next()
                    pu = self.psA.next()
                    for g, ps in ((0, pg), (1, pu)):
                        for c in range(8):
                            self.mm(ps, ps[:, :NT], slot, wv[:, c, g, q * 128:(q + 1) * 128], XT, XT[:, c, :NT], c == 0, c == 7)
                    ft = self.ftr.next()
                    self.A(lambda e, ft=ft, pg=pg: e.activation(out=ft[:, :NT], in_=pg[:, :NT], func=AF.Silu), [pg], [ft])
                    h = HT[jj + q]
                    self.V(lambda e, h=h, ft=ft, pu=pu: e.scalar_tensor_tensor(out=h[:, :NT], in0=ft[:, :NT], scalar=0.5, in1=pu[:, :NT], op0=ALU.mult, op1=ALU.mult), [ft, pu], [h])
            for nh in range(2):
                accs = [self.psA.next() for _ in range(ntl)]
                for k0, nk in ((0, 8), (8, 3)):
                    slot, wv = self.wget(w_out[:, half * 11 + k0:half * 11 + k0 + nk, nh * 512:(nh + 1) * 512], [nk, 512], self.wT[(wn_o, l)])
                    for t in range(ntl):
                        for k in range(nk):
                            h = HT[k0 + k]
                            self.mm(accs[t], accs[t][:nt, :], h, h[:, t * nt:(t + 1) * nt], slot, wv[:, k, :], k0 + k == 0, k0 + k == 10)
                for t in range(ntl):
                    x = self.x[t]
                    xa = x[:nt, nh * 512:(nh + 1) * 512]
                    acc = accs[t]
                    if half == 0:
                        self.V(lambda e, xa=xa, acc=acc: e.scalar_tensor_tensor(out=xa, in0=xa, scalar=ALPHA, in1=acc[:nt, :], op0=ALU.mult, op1=ALU.add), [x, acc], [x])
                    else:
                        self.V(lambda e, xa=xa, acc=acc: e.tensor_tensor(out=xa, in0=xa, in1=acc[:nt, :], op=ALU.add), [x, acc], [x])
            for h in HT:
                self.pfree(h)
        self.layernorm(l, 0 if which == 1 else 2)

    def proj_tm(self, l, col0, ncols, consume):
        nt, ntl = self.nt, self.ntl
        XT = self.XT
        src = self.wb["wmix"][l].rearrange("(c p) n -> p c n", p=128)[:, :, col0:col0 + ncols]
        slot, wv = self.wget(src, [8, ncols], self.wT[("wmix", l)])
        pending = None
        for t in range(ntl):
            ps = self.psA.next()
            for c in range(8):
                self.mm(ps, ps[:nt, :ncols], XT, XT[:, c, t * nt:(t + 1) * nt], slot, wv[:, c, :], c == 0, c == 7)
            nxt = consume(t, ps)
            if pending is not None:
                pending()
            pending = nxt
        if pending is not None:
            pending()

    def to_fm(self, tb, nchunks, dT, dap_fn, t, widths=None):
        nt = self.nt
        pb = self.psB.next()
        off = 0
        for c in range(nchunks):
            w = 128 if widths is None else widths[c]
            self.tr(pb, pb[:w, c * nt:(c + 1) * nt], tb, tb[:nt, off:off + w], nt)
            off += w
        dap_fn(pb)

    def rope(self, sg, view, H, half, tab, tile):
        nt = self.nt
        cos = tab[:nt, tile, 0, :].unsqueeze(1).to_broadcast([nt, H, half])
        sin = tab[:nt, tile, 1, :].unsqueeze(1).to_broadcast([nt, H, half])
        x1 = view[:, :, 0:half]
        x2 = view[:, :, half:2 * half]
        a, b, c, dd = [r[:nt, 0:H, 0:half] for r in self.rt]
        rT = self.rt
        self.V(lambda e: e.tensor_tensor(out=a, in0=x1, in1=cos, op=ALU.mult), [sg, tab], [rT[0]])
        self.V(lambda e: e.tensor_tensor(out=b, in0=x2, in1=sin, op=ALU.mult), [sg, tab], [rT[1]])
        self.V(lambda e: e.tensor_tensor(out=c, in0=x2, in1=cos, op=ALU.mult), [sg, tab], [rT[2]])
        self.V(lambda e: e.tensor_tensor(out=dd, in0=x1, in1=sin, op=ALU.mult), [sg, tab], [rT[3]])
        self.V(lambda e: e.tensor_tensor(out=x1, in0=a, in1=b, op=ALU.subtract), [rT[0], rT[1]], [sg])
        self.V(lambda e: e.tensor_tensor(out=x2, in0=c, in1=dd, op=ALU.add), [rT[2], rT[3]], [sg])

    def cumsum(self, kti, lfT, lfap, n):
        ps = self.psA.next()
        tri, ones, FK, carry = self.tri, self.ones, self.FK, self.carry
        self.mm(ps, ps[:n, 0:8], tri, tri[:n, :n], lfT, lfap, True, True)
        self.mm(ps, ps[:, 8:16], ones, ones[:n, :], lfT, lfap, True, True)
        self.V(lambda e: e.tensor_tensor(out=FK[:n, kti, :], in0=ps[:n, 0:8], in1=carry[:n, :], op=ALU.add), [ps, carry], [FK])
        self.V(lambda e: e.tensor_tensor(out=carry[:, :], in0=carry[:, :], in1=ps[:, 8:16], op=ALU.add), [ps, carry], [carry])

    def attention(self, nheads, kq_fn, Vbuf, vap_fn, vdim, scale, use_bias, mask, finish_fn):
        nt, ntl, NT, kt0 = self.nt, self.ntl, self.NT, self.kt0
        nkt = kt0 + ntl
        ident = self.ident
        BK = self.BK
        for h in range(nheads):
            accs = [self.psA.next() for _ in range(ntl)]
            ops = kq_fn(h)

            def emit_s(kt):
                diag = kt >= kt0
                i = kt - kt0 if diag else 0
                nk = nt if diag else 128
                kc0 = kt * 128
                q0 = i * nt if diag else 0
                sp = self.psS.next()
                use_mask = diag and (mask is not None)
                for oi, (KTt, kap, QTt, qap) in enumerate(ops):
                    self.mm(sp, sp[:nk, q0:NT], KTt, kap(kc0, nk), QTt, qap(q0, NT), oi == 0, (oi == len(ops) - 1) and not use_mask)
                if use_mask:
                    self.mm(sp, sp[:nk, q0:q0 + nt], ident, ident[:, :nk], mask, mask[:, :nt], False, True)
                pt = self.palloc()
                if use_bias:
                    bap = BK[:nk, kt, h:h + 1]
                    self.A(lambda e: e.activation(out=pt[:nk, q0:NT], in_=sp[:nk, q0:NT], func=AF.Exp, scale=scale, bias=bap), [sp, BK], [pt])
                else:
                    self.A(lambda e: e.activation(out=pt[:nk, q0:NT], in_=sp[:nk, q0:NT], func=AF.Exp, scale=scale), [sp], [pt])
                return (kt, i if diag else 0, nk, pt)

            def emit_pv(item):
                kt, j0, nk, pt = item
                for j in range(j0, ntl):
                    self.mm(accs[j], accs[j][:nt, 0:vdim + 1], pt, pt[:nk, j * nt:(j + 1) * nt], Vbuf, vap_fn(kt, h, nk), kt == 0, kt == kt0 + j)
                self.pfree(pt)

            prev = None
            for kt in range(nkt):
                cur = emit_s(kt)
                if prev is not None:
                    emit_pv(prev)
                prev = cur
            emit_pv(prev)
            for j in range(ntl):
                finish_fn(h, j, accs[j])

    def ot_from(self, Otok, b):
        nt, ntl = self.nt, self.ntl
        OT = self.OT
        for j in range(ntl):
            ob = Otok[j]
            self.to_fm(ob, 4, OT, lambda pb, j=j: self.cp(OT, OT[:, b * 4:(b + 1) * 4, j * nt:(j + 1) * nt], pb, pb[:, 0:4 * nt].rearrange("p (c n) -> p c n", c=4)), j)
            self.pfree(ob)

    def mix(self, l, s):
        nt, ntl, NT, kt0, pos0 = self.nt, self.ntl, self.NT, self.kt0, self.pos0
        tok0 = self.tok0
        d = self.d
        XT, QT, OT = self.XT, self.QT, self.OT
        self.make_XT()
        KTa, Va, KTb, Vb, KTc, Vc, KPT = self.KTa, self.Va, self.KTb, self.Vb, self.KTc, self.Vc, self.KPT

        QT4 = QT[:, :, :].rearrange("p (c two) n -> p c two n", two=2)

        def qt_store(pb, t):
            self.cp(QT, QT4[0:64, :, 0, t * nt:(t + 1) * nt], pb, pb[0:64, 0:4 * nt].rearrange("p (c n) -> p c n", c=4), "act")
            self.cp(QT, QT4[64:128, :, 1, t * nt:(t + 1) * nt], pb, pb[64:128, 0:4 * nt].rearrange("p (c n) -> p c n", c=4), "act")

        def c_q(t, ps):
            tb = self.tokb.next()
            self.cp(tb, tb[:nt, :512], ps, ps[:nt, :512])
            return lambda: self.to_fm(tb, 4, QT, lambda pb: qt_store(pb, t), t)

        def c_k(name, KT):
            def f(t, ps):
                sg = self.stg.next()
                self.cp(sg, sg[:nt, :512], ps, ps[:nt, :512], "act")
                if name == "diff_k":
                    self.rope(sg, sg[:nt, :512].rearrange("p (h d) -> p h d", h=8), 8, 8, self.ropeB, kt0 + t)
                self.out_dma(name, l, s, tok0 + t * nt, nt, sg, sg[:nt, :512])
                tb = self.tokb.next()
                self.cp(tb, tb[:nt, :512], sg, sg[:nt, :512], "dve")
                return lambda: self.to_fm(tb, 4, KT, lambda pb: self.cp(KT, KT[:, :, pos0 + t * nt:pos0 + (t + 1) * nt], pb, pb[:, 0:4 * nt].rearrange("p (c n) -> p c n", c=4)), t)
            return f

        def c_v(name, Vb_, H, D):
            def f(t, ps):
                sg = self.stg.next()
                self.cp(sg, sg[:nt, :512], ps, ps[:nt, :512], "act")
                self.out_dma(name, l, s, tok0 + t * nt, nt, sg, sg[:nt, :512])
                self.cp(Vb_, Vb_[:nt, kt0 + t, :, 0:D], sg, sg[:nt, :512].rearrange("p (h d) -> p h d", h=H), "dve")
            return f

        def c_fa(t, ps):
            e8 = self.e8.next()
            lf = self.lf8.next()
            bf_t = self.bf_t
            self.V(lambda e: e.tensor_tensor(out=e8[:nt, :], in0=ps[:nt, 0:8], in1=bf_t[:nt, :], op=ALU.add), [ps, bf_t], [e8])
            self.A(lambda e: e.activation(out=e8[:nt, :], in_=e8[:nt, :], func=AF.Exp, scale=-1.0), [e8], [e8])
            self.A(lambda e: e.activation(out=e8[:nt, :], in_=e8[:nt, :], func=AF.Ln, bias=1.0, scale=1.0), [e8], [e8])
            self.V(lambda e: e.tensor_scalar(lf[:nt, :], e8[:nt, :], -1.0, None, ALU.mult), [e8], [lf])
            self.out_dma("fox_logf", l, s, tok0 + t * nt, nt, lf, lf[:nt, :])
            return lambda: self.cumsum(kt0 + t, lf, lf[:nt, :], nt)

        carry, cref, FK, BK = self.carry, self.cref, self.FK, self.BK
        self.V(lambda e: e.tensor_copy(out=cref[:, :], in_=carry[:, :]), [carry], [cref])
        self.proj_tm(l, MIXOFF["fa"], 8, c_fa)
        for kt in range(kt0 + ntl):
            n = 128 if kt < kt0 else nt
            self.V(lambda e, kt=kt, n=n: e.tensor_tensor(out=BK[:n, kt, :], in0=cref[:n, :], in1=FK[:n, kt, :], op=ALU.subtract), [cref, FK], [BK])
        if self.stop_after == "fa":
            return
        self.proj_tm(l, MIXOFF["qa"], 512, c_q)
        if self.stop_after == "fq":
            return
        self.proj_tm(l, MIXOFF["ka"], 512, c_k("fox_k", KTa))
        if self.stop_after == "fk":
            return
        self.proj_tm(l, MIXOFF["va"], 512, c_v("fox_v", Va, 8, 64))

        if self.stop_after == "fproj":
            return
        Otok = [self.palloc() for _ in range(ntl)]

        def kq_a(h):
            c, b = h // 2, (h % 2) * 64
            return [(KTa, lambda kc0, nk: KTa[:, c, kc0:kc0 + nk], QT, lambda q0, q1: QT[:, h, q0:q1])]

        def fin_simple(h, j, acc):
            rc = self.rc.next()
            ot = Otok[j]
            self.V(lambda e: e.reciprocal(out=rc[:nt, :], in_=acc[:nt, 64:65]), [acc], [rc])
            self.V(lambda e: e.tensor_scalar(ot[:nt, h * 64:(h + 1) * 64], acc[:nt, 0:64], rc[:nt, 0:1], None, ALU.mult), [acc, rc], [ot])

        self.attention(8, kq_a, Va, lambda kt, h, nk: Va[:nk, kt, h, 0:65], 64, 0.125, True, self.maskc, fin_simple)
        self.ot_from(Otok, 0)
        if self.stop_after == "fox":
            return

        def c_qrope(t, ps):
            sg = self.stg.next()
            self.cp(sg, sg[:nt, :512], ps, ps[:nt, :512], "act")
            self.rope(sg, sg[:nt, :512].rearrange("p (h d) -> p h d", h=8), 8, 8, self.ropeB, kt0 + t)
            tb = self.tokb.next()
            self.cp(tb, tb[:nt, :512], sg, sg[:nt, :512], "dve")
            return lambda: self.to_fm(tb, 4, QT, lambda pb: qt_store(pb, t), t)

        self.proj_tm(l, MIXOFF["qb"], 512, c_qrope)
        if self.stop_after == "dq":
            return
        self.proj_tm(l, MIXOFF["kb"], 512, c_k("diff_k", KTb))
        self.proj_tm(l, MIXOFF["vb"], 512, c_v("diff_v", Vb, 4, 128))
        if self.stop_after == "dproj":
            return
        Otok = [self.palloc() for _ in range(ntl)]

        def kq_b(v):
            c, b = v // 2, (v % 2) * 64
            return [(KTb, lambda kc0, nk: KTb[:, c, kc0:kc0 + nk], QT, lambda q0, q1: QT[:, v, q0:q1])]

        neglam, dng_t = self.neglam, self.dng_t

        def fin_b(v, j, acc):
            hh, m = v // 2, v % 2
            rc = self.rc.next()
            self.V(lambda e: e.reciprocal(out=rc[:nt, :], in_=acc[:nt, 128:129]), [acc], [rc])
            t1 = self.t1[j]
            if m == 0:
                self.V(lambda e: e.tensor_scalar(t1[:nt, :], acc[:nt, 0:128], rc[:nt, 0:1], None, ALU.mult), [acc, rc], [t1])
                return
            obf = self.obf.next()
            ss = self.ss.next()
            junk = self.junk
            ot = Otok[j]
            self.V(lambda e: e.tensor_tensor(out=rc[:nt, :], in0=rc[:nt, :], in1=neglam[:nt, :], op=ALU.mult), [rc, neglam], [rc])
            self.V(lambda e: e.scalar_tensor_tensor(out=obf[:nt, :], in0=acc[:nt, 0:128], scalar=rc[:nt, 0:1], in1=t1[:nt, :], op0=ALU.mult, op1=ALU.add), [acc, rc, t1], [obf])
            self.V(lambda e: e.tensor_tensor(out=junk[:nt, 0:128], in0=obf[:nt, :], in1=obf[:nt, :], op=ALU.mult), [obf], [junk])
            self.V(lambda e: e.reduce_sum(out=ss[:nt, 0:1], in_=junk[:nt, 0:128], axis=mybir.AxisListType.X), [junk], [ss])
            self.A(lambda e: e.activation(out=ss[:nt, :], in_=ss[:nt, :], func=AF.Sqrt, bias=RMS_EPS, scale=1.0 / 128), [ss], [ss])
            self.V(lambda e: e.reciprocal(out=ss[:nt, :], in_=ss[:nt, :]), [ss], [ss])
            self.V(lambda e: e.scalar_tensor_tensor(out=ot[:nt, hh * 128:(hh + 1) * 128], in0=obf[:nt, :], scalar=ss[:nt, 0:1], in1=dng_t[:nt, :], op0=ALU.mult, op1=ALU.mult), [obf, ss, dng_t], [ot])

        self.attention(8, kq_b, Vb, lambda kt, v, nk: Vb[:nk, kt, v // 2, 0:129], 128, 0.125, False, self.maskd if self.kind == "p" else None, fin_b)
        self.ot_from(Otok, 1)
        if self.stop_after == "diff":
            return

        dqnT = [self.palloc() for _ in range(3)]
        gq_t, gkv_t = self.gq_t, self.gkv_t

        def rms_to(ps, n, gt, oT, oap):
            ss = self.ss.next()
            junk = self.junk
            self.A(lambda e: e.activation(out=junk[:nt, 0:n], in_=ps[:nt, 0:n], func=AF.Copy), [ps], [junk])
            self.V(lambda e: e.tensor_tensor(out=junk[:nt, 0:n], in0=junk[:nt, 0:n], in1=junk[:nt, 0:n], op=ALU.mult), [junk], [junk])
            self.V(lambda e: e.reduce_sum(out=ss[:nt, 0:1], in_=junk[:nt, 0:n], axis=mybir.AxisListType.X), [junk], [ss])
            self.A(lambda e: e.activation(out=ss[:nt, :], in_=ss[:nt, :], func=AF.Sqrt, bias=RMS_EPS, scale=1.0 / n), [ss], [ss])
            self.V(lambda e: e.reciprocal(out=ss[:nt, :], in_=ss[:nt, :]), [ss], [ss])
            self.V(lambda e: e.scalar_tensor_tensor(out=oap, in0=ps[:nt, 0:n], scalar=ss[:nt, 0:1], in1=gt[:nt, 0:n], op0=ALU.mult, op1=ALU.mult), [ps, ss, gt], [oT])

        def c_dq(t, ps):
            tb = self.tokb.next()
            rms_to(ps, 384, gq_t, tb, tb[:nt, 0:384])

            def dst(pb):
                for c in range(3):
                    self.cp(dqnT[c], dqnT[c][:, t * nt:(t + 1) * nt], pb, pb[:, c * nt:(c + 1) * nt])
            return lambda: self.to_fm(tb, 3, None, dst, t)

        self.proj_tm(l, MIXOFF["dq"], 384, c_dq)
        if self.stop_after == "cdq":
            return
        QPT = [self.palloc() for _ in range(4)]
        slot, wv = self.wget(self.wb["wuq"][l].rearrange("(c p) n -> p c n", p=128), [3, 768], self.wT[("wuq", l)])
        for t in range(ntl):
            ps1 = self.psA.next()
            for c in range(3):
                self.mm(ps1, ps1[:nt, :512], dqnT[c], dqnT[c][:, t * nt:(t + 1) * nt], slot, wv[:, c, 0:512], c == 0, c == 2)
            lat = c_q(t, ps1)
            ps2 = self.psA.next()
            for c in range(3):
                self.mm(ps2, ps2[:nt, :256], dqnT[c], dqnT[c][:, t * nt:(t + 1) * nt], slot, wv[:, c, 512:768], c == 0, c == 2)
            sg = self.stg.next()
            self.cp(sg, sg[:nt, :256], ps2, ps2[:nt, :256], "act")
            self.rope(sg, sg[:nt, :256].rearrange("p (h d) -> p h d", h=8), 8, 16, self.ropeC, kt0 + t)
            tb = self.tokb.next()
            self.V(lambda e, tb=tb: e.memset(tb[:nt, 0:512], 0.0), [], [tb])
            self.cp(tb, tb[:nt, 0:512].rearrange("p (h d) -> p h d", h=8)[:, :, 0:32], sg, sg[:nt, :256].rearrange("p (h d) -> p h d", h=8), "dve")

            def dst(pb, t=t):
                for g in range(4):
                    self.cp(QPT[g], QPT[g][:, t * nt:(t + 1) * nt], pb, pb[:, g * nt:(g + 1) * nt])
            lat()
            self.to_fm(tb, 4, None, dst, t)
        for c in range(3):
            self.pfree(dqnT[c])
        if self.stop_after == "cq":
            return
        ckvT = [self.palloc() for _ in range(2)]

        def c_kv(t, ps):
            kp = self.kp32.next()
            self.cp(kp, kp[:nt, :], ps, ps[:nt, 256:288], "act")
            sg = self.stg.next()
            rms_to(ps, 256, gkv_t, sg, sg[:nt, 0:256])
            self.out_dma("mla_ckv", l, s, tok0 + t * nt, nt, sg, sg[:nt, 0:256])
            tb = self.tokb.next()
            self.cp(tb, tb[:nt, :256], sg, sg[:nt, :256], "dve")
            self.rope(kp, kp[:nt, :].rearrange("p (h d) -> p h d", h=1), 1, 16, self.ropeC, kt0 + t)
            self.out_dma("mla_kpe", l, s, tok0 + t * nt, nt, kp, kp[:nt, :])
            self.V(lambda e, tb=tb: e.memset(tb[:nt, 256:384], 0.0), [], [tb])
            for r in range(2):
                self.cp(tb, tb[:nt, 256 + r * 64:256 + r * 64 + 32], kp, kp[:nt, :], "dve")

            def dst(pb):
                for c in range(2):
                    self.cp(ckvT[c], ckvT[c][:, t * nt:(t + 1) * nt], pb, pb[:, c * nt:(c + 1) * nt])
                self.cp(KPT, KPT[:, pos0 + t * nt:pos0 + (t + 1) * nt], pb, pb[:, 2 * nt:3 * nt])
            return lambda: self.to_fm(tb, 3, None, dst, t)

        self.proj_tm(l, MIXOFF["dkv"], 288, c_kv)
        self.mla_kv(l, ckvT, [(0, NT)], pos0, kt0, nt, ntl)
        for c in range(2):
            self.pfree(ckvT[c])
        if self.stop_after == "ckv":
            return
        Otok = [self.palloc() for _ in range(ntl)]

        def kq_c(h):
            c, b = h // 2, (h % 2) * 64
            return [(KTc, lambda kc0, nk: KTc[:, c, kc0:kc0 + nk], QT, lambda q0, q1: QT[:, h, q0:q1]),
                    (KPT, lambda kc0, nk: KPT[b:b + 64, kc0:kc0 + nk], QPT[c], lambda q0, q1: QPT[c][b:b + 64, q0:q1])]

        self.attention(8, kq_c, Vc, lambda kt, h, nk: Vc[:nk, kt, h, 0:65], 64, 96.0 ** -0.5, False, self.maskd if self.kind == "p" else None, fin_simple)
        for g in range(4):
            self.pfree(QPT[g])
        self.ot_from(Otok, 2)
        if self.stop_after == "mla":
            for cc in range(12):
                self.P.dma("pool", lambda e, cc=cc: e.dma_start(out=d["dbg_ot"][:, cc, :], in_=OT[:, cc, :]), reads=[OT])
            return

        MG = [self.palloc() for _ in range(8)]
        bg_t = self.bg_t
        for b in range(3):
            for half in range(2):
                slot_b, wb = self.wget(self.wb["wbr"][l, b].rearrange("(c p) n -> p c n", p=128)[:, :, half * 512:(half + 1) * 512], [4, 512], self.wT[("wbr", l)])
                slot_g, wgv = self.wget(self.wb["wg"][l].rearrange("(c p) n -> p c n", p=128)[:, :, b * 1024 + half * 512:b * 1024 + (half + 1) * 512], [8, 512], self.wT[("wg", l)])
                for o4 in range(4):
                    oc = half * 4 + o4
                    pgt = self.psA.next()
                    for c in range(8):
                        self.mm(pgt, pgt[:, :NT], slot_g, wgv[:, c, o4 * 128:(o4 + 1) * 128], XT, XT[:, c, :NT], c == 0, c == 7)
                    gsb = self.ftr.next()
                    bcol = b * 8 + oc
                    self.A(lambda e, gsb=gsb, pgt=pgt, bcol=bcol: e.activation(out=gsb[:, :NT], in_=pgt[:, :NT], func=AF.Sigmoid, bias=bg_t[:, bcol:bcol + 1], scale=1.0), [pgt, bg_t], [gsb])
                    pbr = self.psA.next()
                    for kc in range(4):
                        self.mm(pbr, pbr[:, :NT], slot_b, wb[:, kc, o4 * 128:(o4 + 1) * 128], OT, OT[:, b * 4 + kc, :NT], kc == 0, kc == 3)
                    mg = MG[oc]
                    if b == 0:
                        self.V(lambda e, mg=mg, gsb=gsb, pbr=pbr: e.tensor_tensor(out=mg[:, :NT], in0=gsb[:, :NT], in1=pbr[:, :NT], op=ALU.mult), [gsb, pbr], [mg])
                    else:
                        self.V(lambda e, gsb=gsb, pbr=pbr: e.tensor_tensor(out=gsb[:, :NT], in0=gsb[:, :NT], in1=pbr[:, :NT], op=ALU.mult), [gsb, pbr], [gsb])
                        self.V(lambda e, mg=mg, gsb=gsb: e.tensor_tensor(out=mg[:, :NT], in0=mg[:, :NT], in1=gsb[:, :NT], op=ALU.add), [mg, gsb], [mg])
        for nh in range(2):
            slot, wv = self.wget(self.wb["wo"][l].rearrange("(c p) n -> p c n", p=128)[:, :, nh * 512:(nh + 1) * 512], [8, 512], self.wT[("wo", l)])
            for t in range(ntl):
                ps = self.psA.next()
                for kc in range(8):
                    self.mm(ps, ps[:nt, :], MG[kc], MG[kc][:, t * nt:(t + 1) * nt], slot, wv[:, kc, :], kc == 0, kc == 7)
                x = self.x[t]
                xa = x[:nt, nh * 512:(nh + 1) * 512]
                self.V(lambda e, xa=xa, ps=ps: e.scalar_tensor_tensor(out=xa, in0=xa, scalar=ALPHA, in1=ps[:nt, :], op0=ALU.mult, op1=ALU.add), [x, ps], [x])
        for m in MG:
            self.pfree(m)
        self.layernorm(l, 1)

    def mla_kv(self, l, ckvT, ranges, pos0, kt0, nt, ntl):
        d = self.d
        KTc, Vc = self.KTc, self.Vc
        slot, wv = self.wget(self.wb["wuk"][l].rearrange("(c p) n -> p c n", p=128), [2, 512], self.wT[("wuk", l)])
        for (c0, n) in ranges:
            for c in range(4):
                ps = self.psA.next()
                for lc in range(2):
                    self.mm(ps, ps[:, :n], slot, wv[:, lc, c * 128:(c + 1) * 128], ckvT[lc], ckvT[lc][:, c0:c0 + n], lc == 0, lc == 1)
                self.cp(KTc, KTc[:, c, pos0 + c0:pos0 + c0 + n], ps, ps[:, :n])
        slot, wv = self.wget(self.wb["wuv"][l].rearrange("(c p) n -> p c n", p=128), [2, 512], self.wT[("wuv", l)])
        for t in range(ntl):
            ps = self.psA.next()
            for lc in range(2):
                self.mm(ps, ps[:nt, :], ckvT[lc], ckvT[lc][:, t * nt:(t + 1) * nt], slot, wv[:, lc, :], lc == 0, lc == 1)
            tb = self.tokb.next()
            self.cp(tb, tb[:nt, 0:512], ps, ps[:nt, :], "act")
            self.cp(Vc, Vc[:nt, kt0 + t, :, 0:64], tb, tb[:nt, 0:512].rearrange("p (h d) -> p h d", h=8), "dve")

    def ple(self, l, s):
        nt, ntl, NT, pos0 = self.nt, self.ntl, self.NT, self.pos0
        d = self.d
        XT = self.XT
        self.make_XT()
        pt = [self.palloc() for _ in range(2)]
        src = d["ppT" if self.kind == "p" else "psT"]
        for kc in range(2):
            sap = src[l, s, kc, :, self.tok0:self.tok0 + NT]
            self.P.dma("pool", lambda e, kc=kc, sap=sap: e.dma_start(out=pt[kc][:, :NT], in_=sap), writes=[pt[kc]])
        onesb, bhi = self.onesb, self.bhi
        for nh in range(2):
            slot, wv = self.wget(self.wb["wpg"][l].rearrange("(c p) n -> p c n", p=128)[:, :, nh * 512:(nh + 1) * 512], [8, 512], self.wT[("wpg", l)])
            slot2, wp = self.wget(self.wb["wpp"][l].rearrange("(c p) n -> p c n", p=128)[:, :, nh * 512:(nh + 1) * 512], [2, 512], self.wT[("wpp", l)])
            for t in range(ntl):
                pg = self.psA.next()
                for c in range(8):
                    self.mm(pg, pg[:nt, :], XT, XT[:, c, t * nt:(t + 1) * nt], slot, wv[:, c, :], c == 0, False)
                self.mm(pg, pg[:nt, :], onesb, onesb[0:1, :nt], bhi, bhi[0:1, nh * 512:(nh + 1) * 512], False, True)
                gs = self.ftr.next()
                self.A(lambda e, gs=gs, pg=pg: e.activation(out=gs[:nt, :], in_=pg[:nt, :], func=AF.Sigmoid), [pg], [gs])
                pp = self.psA.next()
                for kc in range(2):
                    self.mm(pp, pp[:nt, :], pt[kc], pt[kc][:, t * nt:(t + 1) * nt], slot2, wp[:, kc, :], kc == 0, kc == 1)
                self.V(lambda e, gs=gs, pp=pp: e.tensor_tensor(out=gs[:nt, :], in0=gs[:nt, :], in1=pp[:nt, :], op=ALU.mult), [gs, pp], [gs])
                x = self.x[t]
                xa = x[:nt, nh * 512:(nh + 1) * 512]
                self.V(lambda e, xa=xa, gs=gs: e.scalar_tensor_tensor(out=xa, in0=xa, scalar=ALPHA, in1=gs[:nt, :], op0=ALU.mult, op1=ALU.add), [x, gs], [x])
        for p_ in pt:
            self.pfree(p_)
        self.layernorm(l, 3)

    def block(self, l, kind, s, B):
        self.kind = kind
        if kind == "p":
            self.nt, self.ntl = 128, 4
            self.kt0 = 4 * B
            self.tok0 = 512 * B
            self.pos0 = 512 * B
        else:
            self.nt, self.ntl = 64, 1
            self.kt0 = 8
            self.tok0 = 0
            self.pos0 = 1024
        self.NT = self.nt * self.ntl
        nt, ntl, tok0 = self.nt, self.ntl, self.tok0
        d, P = self.d, self.P
        xmid = self.xmid_p if kind == "p" else self.xmid_s
        xin = d["xp" if kind == "p" else "xs"]
        for t in range(ntl):
            x = self.x[t]
            if l == 0:
                sap = xin[s, tok0 + t * nt:tok0 + (t + 1) * nt, :]
                P.dma("sp", lambda e, x=x, sap=sap: e.dma_start(out=x[:nt, :], in_=sap), writes=[x])
            else:
                sap = xmid[s, tok0 + t * nt:tok0 + (t + 1) * nt, :]
                P.dma("sp", lambda e, x=x, sap=sap: e.dma_start(out=x[:nt, :], in_=sap), writes=[x], extra=list(self.xmid_ev.values()))
        sa = self.stop_after
        self.ffn(l, 1)
        if sa != "ffn1":
            self.mix(l, s)
            if sa is None or sa == "ffn2":
                self.ffn(l, 2)
                if sa is None:
                    self.ple(l, s)
        last = (l == self.L - 1)
        for t in range(ntl):
            x = self.x[t]
            if last:
                self.out_dma("y", None, s, tok0 + t * nt, nt, x, x[:nt, :])
            else:
                dap = xmid[s, tok0 + t * nt:tok0 + (t + 1) * nt, :]
                ev = P.dma("pool", lambda e, x=x, dap=dap: e.dma_start(out=dap, in_=x[:nt, :]), reads=[x])
                self.xmid_ev[id(ev[1])] = ev

    def sample_prep(self, l, s):
        d, P = self.d, self.P
        KTa, Va, KTb, Vb, KPT, lfc = self.KTa, self.Va, self.KTb, self.Vb, self.KPT, self.lfc
        P.dma("pool", lambda e: e.dma_start(out=KTa[:, :, 0:1024], in_=d["cka"][l, s].rearrange("c p n -> p c n")), writes=[KTa])
        P.dma("pool", lambda e: e.dma_start(out=KTb[:, :, 0:1024], in_=d["ckb"][l, s].rearrange("c p n -> p c n")), writes=[KTb])
        P.dma("pool", lambda e: e.dma_start(out=KPT[:, 0:1024], in_=d["ckpT"][l, s]), writes=[KPT])
        for kt in range(8):
            P.dma("pool", lambda e, kt=kt: e.dma_start(out=Va[:, kt, :, 0:64], in_=d["cva"][l, s, 128 * kt:128 * (kt + 1), :].rearrange("p (h d) -> p h d", h=8)), writes=[Va])
            P.dma("pool", lambda e, kt=kt: e.dma_start(out=Vb[:, kt, :, 0:128], in_=d["cvb"][l, s, 128 * kt:128 * (kt + 1), :].rearrange("p (h d) -> p h d", h=4)), writes=[Vb])
        P.dma("sp", lambda e: e.dma_start(out=lfc[:, :, :], in_=d["clf"][l, s].rearrange("(t p) h -> p t h", p=128)), writes=[lfc])
        ck = [[self.palloc() for _ in range(2)] for _ in range(2)]
        for lc in range(2):
            for hf in range(2):
                P.dma("pool", lambda e, lc=lc, hf=hf: e.dma_start(out=ck[lc][hf][:, :], in_=d["cckT"][l, s, lc, :, hf * 512:(hf + 1) * 512]), writes=[ck[lc][hf]])
        for hf in range(2):
            self.mla_kv(l, [ck[0][hf], ck[1][hf]], [(0, 512)], hf * 512, hf * 4, 128, 4)
        for lc in range(2):
            for hf in range(2):
                self.pfree(ck[lc][hf])
        for kt in range(8):
            self.cumsum(kt, lfc, lfc[:, kt, :], 128)

    def build(self):
        self.setup()
        NP, NS = self.NP, self.NS
        carry = self.carry
        for l in range(self.L):
            self.layer_params(l)
            for s in range(NP):
                self.V(lambda e: e.memset(carry[:, :], 0.0), [], [carry])
                for B in range(self.nblk):
                    self.block(l, "p", s, B)
            for s in range(NS):
                self.V(lambda e: e.memset(carry[:, :], 0.0), [], [carry])
                self.kind = "s"
                self.sample_prep(l, s)
                self.block(l, "s", s, 0)
        self.P.wait_all_dma("sp")
        self.P.finish()
        return self.nc


def build_program(NP=4, NS=2, L=2, nblk=4, stop_after=None, WSLOTS=2):
    kb = KB(NP, NS, L, WSLOTS=WSLOTS, stop_after=stop_after)
    kb.nblk = nblk
    nc = kb.build()
    n = {e: len(kb.P.q[e]) for e in ENGINES}
    print("instr counts:", n, "dma sems:", len(kb.P.owners))
    return nc


def _rope_tab(half, d, theta):
    inv = np.exp(np.float32(-math.log(theta)) * np.arange(half, dtype=np.float32) * np.float32(2.0 / d)).astype(np.float32)
    pos = np.arange(2048, dtype=np.float32)
    ang = (pos[:, None] * inv[None, :]).astype(np.float32)
    tab = np.stack([np.cos(ang), np.sin(ang)], axis=1).astype(np.float32)
    return np.ascontiguousarray(tab.reshape(16, 128, 2, half).transpose(1, 0, 2, 3))


def shared_inputs(inp):
    f = np.float32
    A = lambda a: np.ascontiguousarray(a, dtype=f)
    w = inp["w_in_mix"]
    sp = np.cumsum([0, 512, 512, 512, 8, 512, 512, 512, 384, 256, 32])
    seg = {n: w[:, :, sp[i]:sp[i + 1]] for i, n in enumerate(["qa", "ka", "va", "fa", "qb", "kb", "vb", "dq", "dkv", "kr"])}
    wmix = np.concatenate([seg[n] for n in ["qa", "ka", "va", "qb", "kb", "vb", "dq", "dkv", "kr", "fa"]], axis=2)
    wuq = inp["mla_w_uq"].reshape(2, 384, 8, 96)
    wuq = np.concatenate([wuq[..., :64].reshape(2, 384, 512), wuq[..., 64:].reshape(2, 384, 256)], axis=2)
    wukv = inp["mla_w_ukv"]
    k = np.arange(128)
    sh = {
        "w1i": A(inp["ffn1_w_in"]), "w1o": A(inp["ffn1_w_out"]), "w2i": A(inp["ffn2_w_in"]), "w2o": A(inp["ffn2_w_out"]),
        "lng": A(inp["ln_g"]), "lnb": A(inp["ln_b"]), "wmix": A(wmix), "bfg": A(inp["b_forget"]),
        "dlam": A(inp["diff_lambda"].reshape(2, 256)), "dng": A(inp["diff_norm_g"]), "gq": A(inp["mla_q_norm_g"]),
        "wuq": A(wuq), "gkv": A(inp["mla_kv_norm_g"]),
        "wuk": A(wukv[..., :64].reshape(2, 256, 512)), "wuv": A(wukv[..., 64:].reshape(2, 256, 512)),
        "wbr": A(inp["w_branch"]), "wg": A(inp["w_gate"]), "bg": A(inp["b_gate"].reshape(2, 24, 128).transpose(0, 2, 1)),
        "wo": A(inp["w_out"]), "wpg": A(inp["ple_w_gate"]), "bpg": A(inp["ple_b_gate"]), "wpp": A(inp["ple_w_proj"]),
        "c_ident": np.eye(128, dtype=f),
        "c_maskc": np.where(k[:, None] <= k[None, :], 0.0, NEG).astype(f),
        "c_maskd": np.where((k[:, None] // 64) <= (k[None, :] // 64), 0.0, NEG).astype(f),
        "c_tri": (k[:, None] <= k[None, :]).astype(f),
        "c_ropeB": _rope_tab(8, 16, 500000.0), "c_ropeC": _rope_tab(16, 32, 10000.0),
    }
    return sh


def core_inputs(inp, sh, pseqs, sseqs):
    f = np.float32
    A = lambda a: np.ascontiguousarray(a, dtype=f)
    m = dict(sh)
    m["xp"] = A(inp["x_prompt"][pseqs])
    pp = inp["p_prompt"][:, pseqs]
    m["ppT"] = A(pp.transpose(0, 1, 3, 2).reshape(2, len(pseqs), 2, 128, 2048))
    if len(sseqs):
        ns = len(sseqs)
        m["xs"] = A(inp["x_sample"][sseqs])
        m["psT"] = A(inp["p_sample"][:, sseqs].transpose(0, 1, 3, 2).reshape(2, ns, 2, 128, 64))
        m["cka"] = A(inp["cache_fox_k"][:, sseqs].reshape(2, ns, 1024, 4, 128).transpose(0, 1, 3, 4, 2))
        m["cva"] = A(inp["cache_fox_v"][:, sseqs].reshape(2, ns, 1024, 512))
        m["clf"] = A(inp["cache_fox_logf"][:, sseqs])
        m["ckb"] = A(inp["cache_diff_k"][:, sseqs].reshape(2, ns, 1024, 4, 128).transpose(0, 1, 3, 4, 2))
        m["cvb"] = A(inp["cache_diff_v"][:, sseqs].reshape(2, ns, 1024, 512))
        m["cckT"] = A(inp["cache_mla_ckv"][:, sseqs].reshape(2, ns, 1024, 2, 128).transpose(0, 1, 3, 4, 2))
        kp = inp["cache_mla_kpe"][:, sseqs].transpose(0, 1, 3, 2)
        z = np.zeros_like(kp)
        m["ckpT"] = A(np.concatenate([kp, z, kp, z], axis=2))
    return m


_NC_CACHE = {}


def kernel(**inputs):
    inp = {k: np.asarray(v) for k, v in inputs.items()}
    ncores = 8
    NP, NS = 4, 2
    if "full" not in _NC_CACHE:
        _NC_CACHE["full"] = build_program(NP, NS, 2)
    nc = _NC_CACHE["full"]
    sh = shared_inputs(inp)
    in_maps = []
    for c in range(ncores):
        in_maps.append(core_inputs(inp, sh, list(range(NP * c, NP * (c + 1))), list(range(NS * c, NS * (c + 1)))))
    res = run_bass_kernel_spmd(nc, in_maps, core_ids=list(range(ncores)))
    R = res.results

    def cat(name, axis):
        return np.concatenate([np.asarray(r[name]) for r in R], axis=axis)

    y_p = cat("y_p", 0)
    y_s = cat("y_s", 0)
    outs = [y_p, y_s]
    shp = {"fox_k": (8, 64), "fox_v": (8, 64), "fox_logf": (8,), "diff_k": (4, 2, 64), "diff_v": (4, 128), "mla_ckv": (256,), "mla_kpe": (32,)}
    for nm in ["fox_k", "fox_v", "fox_logf", "diff_k", "diff_v", "mla_ckv", "mla_kpe"]:
        a = cat(nm + "_p", 1)
        b = cat(nm + "_s", 1)
        outs.append(a.reshape(2, 32, 2048, *shp[nm]))
        outs.append(b.reshape(2, 16, 64, *shp[nm]))
    return tuple(np.ascontiguousarray(o, dtype=np.float32) for o in outs)
```

```python
import math
from collections import deque
import numpy as np
import concourse.bass as bass
import concourse.mybir as mybir
from concourse.bass_utils import run_bass_kernel_spmd

F32 = mybir.dt.float32
BF16 = mybir.dt.bfloat16
AF = mybir.ActivationFunctionType
ALU = mybir.AluOpType

ENGINES = ["pe", "act", "dve", "pool", "sp"]
ALPHA = 4.0 ** 0.25
LN_EPS = 1e-5
RMS_EPS = 1e-6
NEG = -30000.0


class T:
    __slots__ = ("h", "name", "w", "r", "sem", "cnt", "excl")

    def __init__(self, h, name=""):
        self.excl = False
        self.h = h
        self.name = name
        self.w = None
        self.r = {}
        self.sem = None
        self.cnt = 0

    def __getitem__(self, k):
        return self.h[k]


class Rec:
    __slots__ = ("fn", "waits", "inc", "dma")

    def __init__(self, fn):
        self.fn = fn
        self.waits = []
        self.inc = False
        self.dma = None


class Prog:
    def __init__(self, nc):
        self.nc = nc
        self.q = {e: [] for e in ENGINES}
        self.seen = {e: {} for e in ENGINES}
        self.esem = {}
        self._ctx = []
        self.owners = []
        for e in ENGINES:
            self.esem[e] = self._sem("s_" + e)

    def _sem(self, name):
        cm = self.nc.semaphore(name)
        s = cm.__enter__()
        self._ctx.append(cm)
        return s

    def sbuf(self, name, shape, dt):
        cm = self.nc.sbuf_tensor(name, list(shape), dt)
        h = cm.__enter__()
        self._ctx.append(cm)
        return T(h, name)

    def psum(self, name, shape, dt):
        cm = self.nc.psum_tensor(name, list(shape), dt)
        h = cm.__enter__()
        self._ctx.append(cm)
        t = T(h, name)
        t.excl = True
        return t

    def _need(self, eng, rec, ev):
        if ev is None:
            return
        if ev[0] == "eng":
            _, e2, idx = ev
            if e2 == eng and eng in ("pe", "sp"):
                return
            key = ("eng", e2)
            if self.seen[eng].get(key, -1) >= idx:
                return
            self.seen[eng][key] = idx
            self.q[e2][idx].inc = True
            rec.waits.append(ev)
        else:
            _, sem, val = ev
            key = ("dma", id(sem))
            if self.seen[eng].get(key, -1) >= val:
                return
            self.seen[eng][key] = val
            rec.waits.append(ev)

    def _deps(self, eng, rec, reads, writes):
        for t in reads:
            self._need(eng, rec, t.w)
            if t.excl:
                for k, ev in t.r.items():
                    if k != eng:
                        self._need(eng, rec, ev)
        for t in writes:
            self._need(eng, rec, t.w)
            for ev in t.r.values():
                self._need(eng, rec, ev)

    def op(self, eng, fn, reads=(), writes=()):
        rec = Rec(fn)
        self._deps(eng, rec, reads, writes)
        idx = len(self.q[eng])
        self.q[eng].append(rec)
        ev = ("eng", eng, idx)
        for t in reads:
            t.r[eng] = ev
        for t in writes:
            t.w = ev
            t.r = {}
        return ev

    def dma(self, queue, fn, reads=(), writes=(), extra=()):
        rec = Rec(fn)
        for ev in extra:
            self._need(queue, rec, ev)
        owner = (list(writes) + list(reads))[0]
        if owner.sem is None:
            owner.sem = {}
            owner.cnt = {}
        if queue not in owner.sem:
            owner.sem[queue] = self._sem("d%s_%s" % (queue[0], owner.name))
            owner.cnt[queue] = 0
            self.owners.append((owner, queue))
        sem = owner.sem[queue]
        for t in reads:
            self._need(queue, rec, t.w)
        for t in writes:
            if not (t.w is not None and t.w[0] == "dma" and t.w[1] is sem):
                self._need(queue, rec, t.w)
            for ev in t.r.values():
                self._need(queue, rec, ev)
        owner.cnt[queue] += 16
        ev = ("dma", sem, owner.cnt[queue])
        rec.dma = (sem, 16)
        self.q[queue].append(rec)
        for t in reads:
            t.r["dma%d%s" % (id(owner), queue)] = ev
        for t in writes:
            t.w = ev
            t.r = {}
        return ev

    def wait_all_dma(self, eng):
        rec = Rec(None)
        for o, qn in self.owners:
            self._need(eng, rec, ("dma", o.sem[qn], o.cnt[qn]))
        self.q[eng].append(rec)

    def finish(self):
        nc = self.nc
        pref = {}
        for e in ENGINES:
            c = 0
            arr = []
            for rec in self.q[e]:
                if rec.inc:
                    c += 1
                arr.append(c)
            pref[e] = arr
        esem = self.esem
        q = self.q

        def run(e, engobj):
            for rec in q[e]:
                for ev in rec.waits:
                    if ev[0] == "eng":
                        engobj.wait_ge(esem[ev[1]], pref[ev[1]][ev[2]])
                    else:
                        engobj.wait_ge(ev[1], ev[2])
                if rec.fn is None:
                    continue
                ins = rec.fn(engobj)
                if rec.dma is not None:
                    ins.then_inc(rec.dma[0], rec.dma[1])
                if rec.inc:
                    ins.then_inc(esem[e], 1)

        with nc.Block() as block:
            @block.tensor
            def _(eng):
                run("pe", eng)

            @block.scalar
            def _(eng):
                run("act", eng)

            @block.vector
            def _(eng):
                run("dve", eng)

            @block.gpsimd
            def _(eng):
                run("pool", eng)

            @block.sync
            def _(eng):
                run("sp", eng)
        for cm in reversed(self._ctx):
            cm.__exit__(None, None, None)
        self._ctx = []


class Ring:
    def __init__(self, tiles):
        self.t = tiles
        self.i = 0

    def next(self):
        t = self.t[self.i % len(self.t)]
        self.i += 1
        return t


OUT_SPECS = [
    ("y", 1024, False), ("fox_k", 512, True), ("fox_v", 512, True), ("fox_logf", 8, True),
    ("diff_k", 512, True), ("diff_v", 512, True), ("mla_ckv", 256, True), ("mla_kpe", 32, True)]

MIXOFF = dict(qa=0, ka=512, va=1024, qb=1536, kb=2048, vb=2560, dq=3072, dkv=3456, kr=3712, fa=3744)


class KB:
    def __init__(self, NP, NS, L=2, WSLOTS=3, stop_after=None):
        self.NP, self.NS, self.L = NP, NS, L
        self.stop_after = stop_after
        nc = bass.Bass("TRN2", target_bir_lowering=False)
        self.nc = nc
        P = Prog(nc)
        self.P = P
        d = {}
        self.d = d

        def din(name, shape):
            d[name] = nc.dram_tensor(name, list(shape), F32, kind="ExternalInput").ap()

        def dout(name, shape):
            d[name] = nc.dram_tensor(name, list(shape), F32, kind="ExternalOutput").ap()

        din("xp", [NP, 2048, 1024])
        din("ppT", [2, NP, 2, 128, 2048])
        if NS:
            din("xs", [NS, 64, 1024])
            din("psT", [2, NS, 2, 128, 64])
            din("cka", [2, NS, 4, 128, 1024])
            din("cva", [2, NS, 1024, 512])
            din("clf", [2, NS, 1024, 8])
            din("ckb", [2, NS, 4, 128, 1024])
            din("cvb", [2, NS, 1024, 512])
            din("cckT", [2, NS, 2, 128, 1024])
            din("ckpT", [2, NS, 128, 1024])
        din("w1i", [2, 1024, 5632]); din("w1o", [2, 2816, 1024])
        din("w2i", [2, 1024, 5632]); din("w2o", [2, 2816, 1024])
        din("lng", [2, 4, 1024]); din("lnb", [2, 4, 1024])
        din("wmix", [2, 1024, 3752]); din("bfg", [2, 8]); din("dlam", [2, 256]); din("dng", [2, 128])
        din("gq", [2, 384]); din("wuq", [2, 384, 768]); din("gkv", [2, 256])
        din("wuk", [2, 256, 512]); din("wuv", [2, 256, 512])
        din("wbr", [2, 3, 512, 1024]); din("wg", [2, 1024, 3072]); din("bg", [2, 128, 24])
        din("wo", [2, 1024, 1024]); din("wpg", [2, 1024, 1024]); din("bpg", [2, 1024]); din("wpp", [2, 256, 1024])
        din("c_ident", [128, 128]); din("c_maskc", [128, 128]); din("c_maskd", [128, 128]); din("c_tri", [128, 128])
        din("c_ropeB", [128, 16, 2, 8]); din("c_ropeC", [128, 16, 2, 16])
        for nm, wd, hasl in OUT_SPECS:
            dout(nm + "_p", ([2] if hasl else []) + [NP, 2048, wd])
            if NS:
                dout(nm + "_s", ([2] if hasl else []) + [NS, 64, wd])
        if stop_after == "mla":
            dout("dbg_ot", [128, 12, 512])
        self.WNAMES = ["w1i", "w1o", "wmix", "wuq", "wuk", "wuv", "wbr", "wg", "wo", "w2i", "w2o", "wpg", "wpp"]
        self.wb = {}
        self.wT = {}
        for nm in self.WNAMES:
            self.wb[nm] = nc.dram_tensor(nm + "_bf", list(d[nm].shape), BF16).ap()
            for l in range(2):
                self.wT[(nm, l)] = T(None, "c_%s%d" % (nm, l))
        self.xmid_p = nc.dram_tensor("xmid_p", [NP, 2048, 1024], F32).ap()
        self.xmid_ev = {}
        if NS:
            self.xmid_s = nc.dram_tensor("xmid_s", [NS, 64, 1024], F32).ap()

        sb = P.sbuf
        self.x = [sb("x%d" % i, [128, 1024], F32) for i in range(4)]
        self.XT = sb("XT", [128, 8, 512], BF16)
        self.QT = sb("QT", [128, 8, 512], BF16)
        self.OT = sb("OT", [128, 12, 512], BF16)
        self.tokb = Ring([sb("tokb%d" % i, [128, 1024], BF16) for i in range(2)])
        self.stg = Ring([sb("stg%d" % i, [128, 512], F32) for i in range(3)])
        self.ftr = self.stg
        self.pool_big = sb("plbig", [128, 12, 512], BF16)
        self.pool_tiles = [T(self.pool_big.h[:, i, :], "pl%d" % i) for i in range(12)]
        self.free = deque(self.pool_tiles)
        self.wring = Ring([sb("wr%d" % i, [128, 4096], BF16) for i in range(WSLOTS)])
        self.gb = sb("gb", [128, 1024], F32)
        self.KTa = sb("KTa", [128, 4, 2048], BF16)
        self.KTb = sb("KTb", [128, 4, 2048], BF16)
        self.KTc = sb("KTc", [128, 4, 2048], BF16)
        self.KPT = sb("KPT", [128, 2048], BF16)
        self.Va = sb("Va", [128, 16, 8, 66], BF16)
        self.Vb = sb("Vb", [128, 16, 4, 130], BF16)
        self.Vc = sb("Vc", [128, 16, 8, 66], BF16)
        self.FK = sb("FK", [128, 16, 8], F32)
        self.BK = sb("BK", [128, 16, 8], F32)
        self.carry = sb("carry", [128, 8], F32)
        self.cref = sb("cref", [128, 8], F32)
        self.lfc = sb("lfc", [128, 8, 8], F32)
        self.bf_t = sb("bf_t", [128, 8], F32)
        self.dng_t = sb("dng_t", [128, 128], F32)
        self.gq_t = sb("gq_t", [128, 384], F32)
        self.gkv_t = sb("gkv_t", [128, 256], F32)
        self.bg_t = sb("bg_t", [128, 24], F32)
        self.bhi = sb("bhi", [1, 1024], BF16)
        self.dl_t = sb("dl_t", [128, 256], F32)
        self.dl2 = sb("dl2", [128, 2, 64], F32)
        self.lam2 = sb("lam2", [128, 2], F32)
        self.neglam = sb("neglam", [128, 1], F32)
        self.ident = sb("ident", [128, 128], BF16)
        self.maskc = sb("maskc", [128, 128], BF16)
        self.maskd = sb("maskd", [128, 128], BF16)
        self.tri = sb("tri", [128, 128], F32)
        self.ones = sb("ones", [128, 128], F32)
        self.onesb = sb("onesb", [1, 128], BF16)
        self.ropeB = sb("ropeB", [128, 16, 2, 8], F32)
        self.ropeC = sb("ropeC", [128, 16, 2, 16], F32)
        self.st = sb("st", [128, 2, 6], F32)
        self.mv = sb("mv", [128, 2], F32)
        self.rstd = sb("rstd", [128, 1], F32)
        self.nmr = sb("nmr", [128, 1], F32)
        self.rc = Ring([sb("rc%d" % i, [128, 1], F32) for i in range(4)])
        self.ss = Ring([sb("ss%d" % i, [128, 1], F32) for i in range(2)])
        self.t1 = [sb("t1_%d" % i, [128, 128], F32) for i in range(4)]
        self.obf = Ring([sb("obf%d" % i, [128, 128], F32) for i in range(2)])
        self.junk = sb("junk", [128, 384], F32)
        self.e8 = Ring([sb("e8_%d" % i, [128, 8], F32) for i in range(2)])
        self.lf8 = Ring([sb("lf8_%d" % i, [128, 8], F32) for i in range(2)])
        self.kp32 = Ring([sb("kp32_%d" % i, [128, 32], F32) for i in range(2)])
        self.rt = [sb("rt%d" % i, [128, 8, 16], F32) for i in range(4)]
        self.psA = Ring([P.psum("psA%d" % i, [128, 512], F32) for i in range(4)])
        self.psS = Ring([P.psum("psS%d" % i, [128, 512], F32) for i in range(2)])
        self.psB = Ring([P.psum("psB%d" % i, [128, 1024], BF16) for i in range(2)])
        self.cp_i = 0
        print("sbuf bytes remaining:", nc.sbuf_bytes_remaining)

    def E(self, fn, reads, writes):
        return self.P.op("pe", fn, reads, writes)

    def A(self, fn, reads, writes):
        return self.P.op("act", fn, reads, writes)

    def V(self, fn, reads, writes):
        return self.P.op("dve", fn, reads, writes)

    def cp(self, oT, oap, iT, iap, eng=None):
        if eng is None:
            eng = "act" if (self.cp_i % 2 == 0) else "dve"
            self.cp_i += 1
        if eng == "act":
            self.A(lambda e: e.activation(out=oap, in_=iap, func=AF.Copy), [iT], [oT])
        else:
            self.V(lambda e: e.tensor_copy(out=oap, in_=iap), [iT], [oT])

    def mm(self, oT, oap, lT, lap, rT, rap, start, stop):
        self.E(lambda e: e.matmul(oap, lhsT=lap, rhs=rap, start=start, stop=stop), [lT, rT], [oT])

    def tr(self, oT, oap, iT, iap, n):
        ident = self.ident
        self.E(lambda e: e.transpose(out=oap, in_=iap, identity=ident[:n, :n]), [iT, ident], [oT])

    def palloc(self):
        return self.free.popleft()

    def pfree(self, t):
        self.free.append(t)

    def wget(self, src, shape, dep):
        slot = self.wring.next()
        n = int(np.prod(shape))
        assert n <= 4096
        dst = slot.h[:src.shape[0], 0:n]
        if len(shape) == 2:
            dst = dst.rearrange("p (a b) -> p a b", a=shape[0])
        elif len(shape) == 3:
            dst = dst.rearrange("p (a b c) -> p a b c", a=shape[0], b=shape[1])
        self.P.dma("sp", lambda e: e.dma_start(out=dst, in_=src), reads=[dep], writes=[slot])
        return slot, dst

    def load_gb(self, src_row):
        gb = self.gb
        self.P.dma("pool", lambda e: e.dma_start(out=gb[:, :], in_=src_row.to_broadcast([128, 1024])), writes=[gb])

    def out_dma(self, name, l, s, tok0, n, sT, sap):
        dst = self.d[name + ("_p" if self.kind == "p" else "_s")]
        dst = dst[l, s, tok0:tok0 + n, :] if l is not None else dst[s, tok0:tok0 + n, :]
        self.P.dma("pool", lambda e: e.dma_start(out=dst, in_=sap), reads=[sT])

    def setup(self):
        P, d = self.P, self.d
        for nm, t in (("c_ident", self.ident), ("c_maskc", self.maskc), ("c_maskd", self.maskd)):
            P.dma("pool", lambda e, nm=nm, t=t: e.dma_start(out=t[:, :], in_=d[nm]), writes=[t])
        for nm, t in (("c_tri", self.tri), ("c_ropeB", self.ropeB), ("c_ropeC", self.ropeC)):
            P.dma("sp", lambda e, nm=nm, t=t: e.dma_start(out=t[:], in_=d[nm]), writes=[t])
        for l in range(self.L):
            for nm in self.WNAMES:
                src = d[nm][l]
                dst = self.wb[nm][l]
                if len(src.shape) == 3:
                    src = src.rearrange("b k n -> (b k) n")
                    dst = dst.rearrange("b k n -> (b k) n")
                P.dma("pool", lambda e, src=src, dst=dst: e.dma_start(out=dst, in_=src, max_dma_last_dim=8192), writes=[self.wT[(nm, l)]])
        self.V(lambda e: e.memset(self.ones[:, :], 1.0), [], [self.ones])
        self.V(lambda e: e.memset(self.onesb[:, :], 1.0), [], [self.onesb])
        self.V(lambda e: e.memset(self.QT[:, :, :], 0.0), [], [self.QT])
        self.V(lambda e: e.memset(self.Va[:, :, :, 64:65], 1.0), [], [self.Va])
        self.V(lambda e: e.memset(self.Vb[:, :, :, 128:129], 1.0), [], [self.Vb])
        self.V(lambda e: e.memset(self.Vc[:, :, :, 64:65], 1.0), [], [self.Vc])

    def layer_params(self, l):
        P, d = self.P, self.d
        ld = lambda t, src: P.dma("sp", lambda e: e.dma_start(out=t[:], in_=src), writes=[t])
        ld(self.bf_t, d["bfg"][l:l + 1, :].to_broadcast([128, 8]))
        ld(self.dng_t, d["dng"][l:l + 1, :].to_broadcast([128, 128]))
        ld(self.gq_t, d["gq"][l:l + 1, :].to_broadcast([128, 384]))
        ld(self.gkv_t, d["gkv"][l:l + 1, :].to_broadcast([128, 256]))
        ld(self.bg_t, d["bg"][l])
        r32 = self.x[0]
        P.dma("sp", lambda e: e.dma_start(out=r32[0:1, :], in_=d["bpg"][l:l + 1, :]), writes=[r32])
        ld(self.dl_t, d["dlam"][l:l + 1, :].to_broadcast([128, 256]))
        lam_init = 0.8 - 0.6 * math.exp(-0.3 * l)
        self.lam_init = lam_init
        self.V(lambda e: e.tensor_scalar(self.dng_t[:, :], self.dng_t[:, :], 1.0 - lam_init, None, ALU.mult), [self.dng_t], [self.dng_t])
        self.V(lambda e: e.tensor_copy(out=self.bhi[:, :], in_=r32[0:1, :]), [r32], [self.bhi])
        dl = self.dl_t
        self.V(lambda e: e.tensor_tensor(out=self.dl2[:, 0, :], in0=dl[:, 0:64], in1=dl[:, 64:128], op=ALU.mult), [dl], [self.dl2])
        self.V(lambda e: e.tensor_tensor(out=self.dl2[:, 1, :], in0=dl[:, 128:192], in1=dl[:, 192:256], op=ALU.mult), [dl, self.dl2], [self.dl2])
        self.V(lambda e: e.reduce_sum(out=self.lam2[:, :], in_=self.dl2[:, :, :], axis=mybir.AxisListType.X), [self.dl2], [self.lam2])
        self.A(lambda e: e.activation(out=self.lam2[:, :], in_=self.lam2[:, :], func=AF.Exp), [self.lam2], [self.lam2])
        self.V(lambda e: e.tensor_tensor(out=self.neglam[:, :], in0=self.lam2[:, 1:2], in1=self.lam2[:, 0:1], op=ALU.subtract), [self.lam2], [self.neglam])
        self.V(lambda e: e.tensor_scalar(self.neglam[:, :], self.neglam[:, :], -lam_init, None, ALU.add), [self.neglam], [self.neglam])

    def make_XT(self):
        nt, ntl = self.nt, self.ntl
        XT = self.XT
        for t in range(ntl):
            tb = self.tokb.next()
            x = self.x[t]
            self.cp(tb, tb[:nt, :], x, x[:nt, :])
            pb = self.psB.next()
            for c in range(8):
                self.tr(pb, pb[:, c * nt:(c + 1) * nt], tb, tb[:nt, c * 128:(c + 1) * 128], nt)
            self.cp(XT, XT[:, :, t * nt:(t + 1) * nt], pb, pb[:, 0:8 * nt].rearrange("p (c n) -> p c n", c=8))

    def layernorm(self, l, k):
        nt, ntl = self.nt, self.ntl
        d = self.d
        gb = self.gb
        bt = self.pool_tiles[8:12]
        for t_ in bt:
            self.free.remove(t_)
        bview = self.pool_big.h[:, 8:12, :].rearrange("p a b -> p (a b)").bitcast(F32)
        self.load_gb(d["lng"][l, k:k + 1, :])
        brow = d["lnb"][l, k:k + 1, :]
        self.P.dma("pool", lambda e: e.dma_start(out=bview, in_=brow.to_broadcast([128, 1024])), writes=bt)
        for t in range(ntl):
            x = self.x[t]
            st, mv, rstd, nmr = self.st, self.mv, self.rstd, self.nmr
            for i in range(2):
                self.V(lambda e, i=i, x=x: e.bn_stats(out=st[:nt, i, :], in_=x[:nt, i * 512:(i + 1) * 512]), [x], [st])
            self.V(lambda e: e.bn_aggr(out=mv[:nt, :], in_=st[:nt, :, :].rearrange("p a b -> p (a b)")), [st], [mv])
            self.A(lambda e: e.activation(out=rstd[:nt, :], in_=mv[:nt, 1:2], func=AF.Sqrt, bias=LN_EPS, scale=1.0), [mv], [rstd])
            self.V(lambda e: e.reciprocal(out=rstd[:nt, :], in_=rstd[:nt, :]), [rstd], [rstd])
            self.V(lambda e: e.tensor_scalar(nmr[:nt, :], mv[:nt, 0:1], -1.0, rstd[:nt, 0:1], ALU.mult, ALU.mult), [mv, rstd], [nmr])
            self.A(lambda e, x=x: e.activation(out=x[:nt, :], in_=x[:nt, :], func=AF.Identity, bias=nmr[:nt, 0:1], scale=rstd[:nt, 0:1]), [x, nmr, rstd], [x])
            self.V(lambda e, x=x: e.tensor_tensor(out=x[:nt, :], in0=x[:nt, :], in1=gb[:nt, :], op=ALU.mult), [x, gb], [x])
            self.P.op("pool", lambda e, x=x: e.tensor_tensor(out=x[:nt, :], in0=x[:nt, :], in1=bview[:nt, :], op=ALU.add), [x] + bt, [x])
        for t_ in bt:
            self.pfree(t_)

    def ffn(self, l, which):
        nt, ntl, NT = self.nt, self.ntl, self.NT
        wn_i = "w1i" if which == 1 else "w2i"
        wn_o = "w1o" if which == 1 else "w2o"
        w_in = self.wb[wn_i][l].rearrange("(c p) (g f) -> p c g f", p=128, g=2)
        w_out = self.wb[wn_o][l].rearrange("(c p) n -> p c n", p=128)
        XT = self.XT
        self.make_XT()
        for half in range(2):
            HT = [self.palloc() for _ in range(11)]
            for jj in range(0, 11, 2):
                nj = min(2, 11 - jj)
                j0 = half * 11 + jj
                slot = self.wring.next()
                wv = slot.h[:, 0:16 * nj * 128].rearrange("p (a b c) -> p a b c", a=8, b=2)
                for g in range(2):
                    self.P.dma("sp", lambda e, g=g, wv=wv, j0=j0, nj=nj: e.dma_start(out=wv[:, :, g, :], in_=w_in[:, :, g, j0 * 128:(j0 + nj) * 128]), reads=[self.wT[(wn_i, l)]], writes=[slot])
                for q in range(nj):
                    pg = self.psA.next()
                    pu = self.psA.next()
                    for g, ps in ((0, pg), (1, pu)):
                        for c in range(8):
                            self.mm(ps, ps[:, :NT], slot, wv[:, c, g, q * 128:(q + 1) * 128], XT, XT[:, c, :NT], c == 0, c == 7)
                    ft = self.ftr.next()
                    self.A(lambda e, ft=ft, pg=pg: e.activation(out=ft[:, :NT], in_=pg[:, :NT], func=AF.Silu), [pg], [ft])
                    h = HT[jj + q]
                    self.V(lambda e, h=h, ft=ft, pu=pu: e.scalar_tensor_tensor(out=h[:, :NT], in0=ft[:, :NT], scalar=0.5, in1=pu[:, :NT], op0=ALU.mult, op1=ALU.mult), [ft, pu], [h])
            for nh in range(2):
                accs = [self.psA.next() for _ in range(ntl)]
                for k0, nk in ((0, 6), (6, 5)):
                    slot, wv = self.wget(w_out[:, half * 11 + k0:half * 11 + k0 + nk, nh * 512:(nh + 1) * 512], [nk, 512], self.wT[(wn_o, l)])
                    for t in range(ntl):
                        for k in range(nk):
                            h = HT[k0 + k]
                            self.mm(accs[t], accs[t][:nt, :], h, h[:, t * nt:(t + 1) * nt], slot, wv[:, k, :], k0 + k == 0, k0 + k == 10)
                for t in range(ntl):
                    x = self.x[t]
                    xa = x[:nt, nh * 512:(nh + 1) * 512]
                    acc = accs[t]
                    if half == 0:
                        self.V(lambda e, xa=xa, acc=acc: e.scalar_tensor_tensor(out=xa, in0=xa, scalar=ALPHA, in1=acc[:nt, :], op0=ALU.mult, op1=ALU.add), [x, acc], [x])
                    else:
                        self.V(lambda e, xa=xa, acc=acc: e.tensor_tensor(out=xa, in0=xa, in1=acc[:nt, :], op=ALU.add), [x, acc], [x])
            for h in HT:
                self.pfree(h)
        self.layernorm(l, 0 if which == 1 else 2)

    def proj_tm(self, l, col0, ncols, consume):
        nt, ntl = self.nt, self.ntl
        XT = self.XT
        src = self.wb["wmix"][l].rearrange("(c p) n -> p c n", p=128)[:, :, col0:col0 + ncols]
        slot, wv = self.wget(src, [8, ncols], self.wT[("wmix", l)])
        pending = None
        for t in range(ntl):
            ps = self.psA.next()
            for c in range(8):
                self.mm(ps, ps[:nt, :ncols], XT, XT[:, c, t * nt:(t + 1) * nt], slot, wv[:, c, :], c == 0, c == 7)
            nxt = consume(t, ps)
            if pending is not None:
                pending()
            pending = nxt
        if pending is not None:
            pending()

    def to_fm(self, tb, nchunks, dT, dap_fn, t, widths=None):
        nt = self.nt
        pb = self.psB.next()
        off = 0
        for c in range(nchunks):
            w = 128 if widths is None else widths[c]
            self.tr(pb, pb[:w, c * nt:(c + 1) * nt], tb, tb[:nt, off:off + w], nt)
            off += w
        dap_fn(pb)

    def rope(self, sg, view, H, half, tab, tile):
        nt = self.nt
        cos = tab[:nt, tile, 0, :].unsqueeze(1).to_broadcast([nt, H, half])
        sin = tab[:nt, tile, 1, :].unsqueeze(1).to_broadcast([nt, H, half])
        x1 = view[:, :, 0:half]
        x2 = view[:, :, half:2 * half]
        a, b, c, dd = [r[:nt, 0:H, 0:half] for r in self.rt]
        rT = self.rt
        self.V(lambda e: e.tensor_tensor(out=a, in0=x1, in1=cos, op=ALU.mult), [sg, tab], [rT[0]])
        self.V(lambda e: e.tensor_tensor(out=b, in0=x2, in1=sin, op=ALU.mult), [sg, tab], [rT[1]])
        self.V(lambda e: e.tensor_tensor(out=c, in0=x2, in1=cos, op=ALU.mult), [sg, tab], [rT[2]])
        self.V(lambda e: e.tensor_tensor(out=dd, in0=x1, in1=sin, op=ALU.mult), [sg, tab], [rT[3]])
        self.V(lambda e: e.tensor_tensor(out=x1, in0=a, in1=b, op=ALU.subtract), [rT[0], rT[1]], [sg])
        self.V(lambda e: e.tensor_tensor(out=x2, in0=c, in1=dd, op=ALU.add), [rT[2], rT[3]], [sg])

    def cumsum(self, kti, lfT, lfap, n):
        ps = self.psA.next()
        tri, ones, FK, carry = self.tri, self.ones, self.FK, self.carry
        self.mm(ps, ps[:n, 0:8], tri, tri[:n, :n], lfT, lfap, True, True)
        self.mm(ps, ps[:, 8:16], ones, ones[:n, :], lfT, lfap, True, True)
        self.V(lambda e: e.tensor_tensor(out=FK[:n, kti, :], in0=ps[:n, 0:8], in1=carry[:n, :], op=ALU.add), [ps, carry], [FK])
        self.V(lambda e: e.tensor_tensor(out=carry[:, :], in0=carry[:, :], in1=ps[:, 8:16], op=ALU.add), [ps, carry], [carry])

    def attention(self, nheads, kq_fn, Vbuf, vap_fn, vdim, scale, use_bias, mask, finish_fn):
        nt, ntl, NT, kt0 = self.nt, self.ntl, self.NT, self.kt0
        nkt = kt0 + ntl
        ident = self.ident
        BK = self.BK
        for h in range(nheads):
            accs = [self.psA.next() for _ in range(ntl)]
            ops = kq_fn(h)

            def emit_s(kt):
                diag = kt >= kt0
                i = kt - kt0 if diag else 0
                nk = nt if diag else 128
                kc0 = kt * 128
                q0 = i * nt if diag else 0
                sp = self.psS.next()
                use_mask = diag and (mask is not None)
                for oi, (KTt, kap, QTt, qap) in enumerate(ops):
                    self.mm(sp, sp[:nk, q0:NT], KTt, kap(kc0, nk), QTt, qap(q0, NT), oi == 0, (oi == len(ops) - 1) and not use_mask)
                if use_mask:
                    self.mm(sp, sp[:nk, q0:q0 + nt], ident, ident[:, :nk], mask, mask[:, :nt], False, True)
                pt = self.palloc()
                if use_bias:
                    bap = BK[:nk, kt, h:h + 1]
                    self.A(lambda e: e.activation(out=pt[:nk, q0:NT], in_=sp[:nk, q0:NT], func=AF.Exp, scale=scale, bias=bap), [sp, BK], [pt])
                else:
                    self.A(lambda e: e.activation(out=pt[:nk, q0:NT], in_=sp[:nk, q0:NT], func=AF.Exp, scale=scale), [sp], [pt])
                return (kt, i if diag else 0, nk, pt)

            def emit_pv(item):
                kt, j0, nk, pt = item
                for j in range(j0, ntl):
                    self.mm(accs[j], accs[j][:nt, 0:vdim + 1], pt, pt[:nk, j * nt:(j + 1) * nt], Vbuf, vap_fn(kt, h, nk), kt == 0, kt == kt0 + j)
                self.pfree(pt)

            prev = None
            for kt in range(nkt):
                cur = emit_s(kt)
                if prev is not None:
                    emit_pv(prev)
                prev = cur
            emit_pv(prev)
            for j in range(ntl):
                finish_fn(h, j, accs[j])

    def ot_from(self, Otok, b):
        nt, ntl = self.nt, self.ntl
        OT = self.OT
        for j in range(ntl):
            ob = Otok[j]
            self.to_fm(ob, 4, OT, lambda pb, j=j: self.cp(OT, OT[:, b * 4:(b + 1) * 4, j * nt:(j + 1) * nt], pb, pb[:, 0:4 * nt].rearrange("p (c n) -> p c n", c=4)), j)
            self.pfree(ob)

    def mix(self, l, s):
        nt, ntl, NT, kt0, pos0 = self.nt, self.ntl, self.NT, self.kt0, self.pos0
        tok0 = self.tok0
        d = self.d
        XT, QT, OT = self.XT, self.QT, self.OT
        self.make_XT()
        KTa, Va, KTb, Vb, KTc, Vc, KPT = self.KTa, self.Va, self.KTb, self.Vb, self.KTc, self.Vc, self.KPT

        QT4 = QT[:, :, :].rearrange("p (c two) n -> p c two n", two=2)

        def qt_store(pb, t):
            self.cp(QT, QT4[0:64, :, 0, t * nt:(t + 1) * nt], pb, pb[0:64, 0:4 * nt].rearrange("p (c n) -> p c n", c=4), "act")
            self.cp(QT, QT4[64:128, :, 1, t * nt:(t + 1) * nt], pb, pb[64:128, 0:4 * nt].rearrange("p (c n) -> p c n", c=4), "act")

        def c_q(t, ps):
            tb = self.tokb.next()
            self.cp(tb, tb[:nt, :512], ps, ps[:nt, :512])
            return lambda: self.to_fm(tb, 4, QT, lambda pb: qt_store(pb, t), t)

        def c_k(name, KT):
            def f(t, ps):
                sg = self.stg.next()
                self.cp(sg, sg[:nt, :512], ps, ps[:nt, :512], "act")
                if name == "diff_k":
                    self.rope(sg, sg[:nt, :512].rearrange("p (h d) -> p h d", h=8), 8, 8, self.ropeB, kt0 + t)
                self.out_dma(name, l, s, tok0 + t * nt, nt, sg, sg[:nt, :512])
                tb = self.tokb.next()
                self.cp(tb, tb[:nt, :512], sg, sg[:nt, :512], "dve")
                return lambda: self.to_fm(tb, 4, KT, lambda pb: self.cp(KT, KT[:, :, pos0 + t * nt:pos0 + (t + 1) * nt], pb, pb[:, 0:4 * nt].rearrange("p (c n) -> p c n", c=4)), t)
            return f

        def c_v(name, Vb_, H, D):
            def f(t, ps):
                sg = self.stg.next()
                self.cp(sg, sg[:nt, :512], ps, ps[:nt, :512], "act")
                self.out_dma(name, l, s, tok0 + t * nt, nt, sg, sg[:nt, :512])
                self.cp(Vb_, Vb_[:nt, kt0 + t, :, 0:D], sg, sg[:nt, :512].rearrange("p (h d) -> p h d", h=H), "dve")
            return f

        def c_fa(t, ps):
            e8 = self.e8.next()
            lf = self.lf8.next()
            bf_t = self.bf_t
            self.V(lambda e: e.tensor_tensor(out=e8[:nt, :], in0=ps[:nt, 0:8], in1=bf_t[:nt, :], op=ALU.add), [ps, bf_t], [e8])
            self.A(lambda e: e.activation(out=e8[:nt, :], in_=e8[:nt, :], func=AF.Exp, scale=-1.0), [e8], [e8])
            self.A(lambda e: e.activation(out=e8[:nt, :], in_=e8[:nt, :], func=AF.Ln, bias=1.0, scale=1.0), [e8], [e8])
            self.V(lambda e: e.tensor_scalar(lf[:nt, :], e8[:nt, :], -1.0, None, ALU.mult), [e8], [lf])
            self.out_dma("fox_logf", l, s, tok0 + t * nt, nt, lf, lf[:nt, :])
            return lambda: self.cumsum(kt0 + t, lf, lf[:nt, :], nt)

        carry, cref, FK, BK = self.carry, self.cref, self.FK, self.BK
        self.V(lambda e: e.tensor_copy(out=cref[:, :], in_=carry[:, :]), [carry], [cref])
        self.proj_tm(l, MIXOFF["fa"], 8, c_fa)
        for kt in range(kt0 + ntl):
            n = 128 if kt < kt0 else nt
            self.V(lambda e, kt=kt, n=n: e.tensor_tensor(out=BK[:n, kt, :], in0=cref[:n, :], in1=FK[:n, kt, :], op=ALU.subtract), [cref, FK], [BK])
        if self.stop_after == "fa":
            return
        self.proj_tm(l, MIXOFF["qa"], 512, c_q)
        if self.stop_after == "fq":
            return
        self.proj_tm(l, MIXOFF["ka"], 512, c_k("fox_k", KTa))
        if self.stop_after == "fk":
            return
        self.proj_tm(l, MIXOFF["va"], 512, c_v("fox_v", Va, 8, 64))

        if self.stop_after == "fproj":
            return
        Otok = [self.palloc() for _ in range(ntl)]

        def kq_a(h):
            c, b = h // 2, (h % 2) * 64
            return [(KTa, lambda kc0, nk: KTa[:, c, kc0:kc0 + nk], QT, lambda q0, q1: QT[:, h, q0:q1])]

        def fin_simple(h, j, acc):
            rc = self.rc.next()
            ot = Otok[j]
            self.V(lambda e: e.reciprocal(out=rc[:nt, :], in_=acc[:nt, 64:65]), [acc], [rc])
            self.V(lambda e: e.tensor_scalar(ot[:nt, h * 64:(h + 1) * 64], acc[:nt, 0:64], rc[:nt, 0:1], None, ALU.mult), [acc, rc], [ot])

        self.attention(8, kq_a, Va, lambda kt, h, nk: Va[:nk, kt, h, 0:65], 64, 0.125, True, self.maskc, fin_simple)
        self.ot_from(Otok, 0)
        if self.stop_after == "fox":
            return

        def c_qrope(t, ps):
            sg = self.stg.next()
            self.cp(sg, sg[:nt, :512], ps, ps[:nt, :512], "act")
            self.rope(sg, sg[:nt, :512].rearrange("p (h d) -> p h d", h=8), 8, 8, self.ropeB, kt0 + t)
            tb = self.tokb.next()
            self.cp(tb, tb[:nt, :512], sg, sg[:nt, :512], "dve")
            return lambda: self.to_fm(tb, 4, QT, lambda pb: qt_store(pb, t), t)

        self.proj_tm(l, MIXOFF["qb"], 512, c_qrope)
        if self.stop_after == "dq":
            return
        self.proj_tm(l, MIXOFF["kb"], 512, c_k("diff_k", KTb))
        self.proj_tm(l, MIXOFF["vb"], 512, c_v("diff_v", Vb, 4, 128))
        if self.stop_after == "dproj":
            return
        Otok = [self.palloc() for _ in range(ntl)]

        def kq_b(v):
            c, b = v // 2, (v % 2) * 64
            return [(KTb, lambda kc0, nk: KTb[:, c, kc0:kc0 + nk], QT, lambda q0, q1: QT[:, v, q0:q1])]

        neglam, dng_t = self.neglam, self.dng_t

        def fin_b(v, j, acc):
            hh, m = v // 2, v % 2
            rc = self.rc.next()
            self.V(lambda e: e.reciprocal(out=rc[:nt, :], in_=acc[:nt, 128:129]), [acc], [rc])
            t1 = self.t1[j]
            if m == 0:
                self.V(lambda e: e.tensor_scalar(t1[:nt, :], acc[:nt, 0:128], rc[:nt, 0:1], None, ALU.mult), [acc, rc], [t1])
                return
            obf = self.obf.next()
            ss = self.ss.next()
            junk = self.junk
            ot = Otok[j]
            self.V(lambda e: e.tensor_tensor(out=rc[:nt, :], in0=rc[:nt, :], in1=neglam[:nt, :], op=ALU.mult), [rc, neglam], [rc])
            self.V(lambda e: e.scalar_tensor_tensor(out=obf[:nt, :], in0=acc[:nt, 0:128], scalar=rc[:nt, 0:1], in1=t1[:nt, :], op0=ALU.mult, op1=ALU.add), [acc, rc, t1], [obf])
            self.V(lambda e: e.tensor_tensor(out=junk[:nt, 0:128], in0=obf[:nt, :], in1=obf[:nt, :], op=ALU.mult), [obf], [junk])
            self.V(lambda e: e.reduce_sum(out=ss[:nt, 0:1], in_=junk[:nt, 0:128], axis=mybir.AxisListType.X), [junk], [ss])
            self.A(lambda e: e.activation(out=ss[:nt, :], in_=ss[:nt, :], func=AF.Sqrt, bias=RMS_EPS, scale=1.0 / 128), [ss], [ss])
            self.V(lambda e: e.reciprocal(out=ss[:nt, :], in_=ss[:nt, :]), [ss], [ss])
            self.V(lambda e: e.scalar_tensor_tensor(out=ot[:nt, hh * 128:(hh + 1) * 128], in0=obf[:nt, :], scalar=ss[:nt, 0:1], in1=dng_t[:nt, :], op0=ALU.mult, op1=ALU.mult), [obf, ss, dng_t], [ot])

        self.attention(8, kq_b, Vb, lambda kt, v, nk: Vb[:nk, kt, v // 2, 0:129], 128, 0.125, False, self.maskd if self.kind == "p" else None, fin_b)
        self.ot_from(Otok, 1)
        if self.stop_after == "diff":
            return

        dqnT = [self.palloc() for _ in range(3)]
        gq_t, gkv_t = self.gq_t, self.gkv_t

        def rms_to(ps, n, gt, oT, oap):
            ss = self.ss.next()
            junk = self.junk
            self.A(lambda e: e.activation(out=junk[:nt, 0:n], in_=ps[:nt, 0:n], func=AF.Copy), [ps], [junk])
            self.V(lambda e: e.tensor_tensor(out=junk[:nt, 0:n], in0=junk[:nt, 0:n], in1=junk[:nt, 0:n], op=ALU.mult), [junk], [junk])
            self.V(lambda e: e.reduce_sum(out=ss[:nt, 0:1], in_=junk[:nt, 0:n], axis=mybir.AxisListType.X), [junk], [ss])
            self.A(lambda e: e.activation(out=ss[:nt, :], in_=ss[:nt, :], func=AF.Sqrt, bias=RMS_EPS, scale=1.0 / n), [ss], [ss])
            self.V(lambda e: e.reciprocal(out=ss[:nt, :], in_=ss[:nt, :]), [ss], [ss])
            self.V(lambda e: e.scalar_tensor_tensor(out=oap, in0=ps[:nt, 0:n], scalar=ss[:nt, 0:1], in1=gt[:nt, 0:n], op0=ALU.mult, op1=ALU.mult), [ps, ss, gt], [oT])

        def c_dq(t, ps):
            tb = self.tokb.next()
            rms_to(ps, 384, gq_t, tb, tb[:nt, 0:384])

            def dst(pb):
                for c in range(3):
                    self.cp(dqnT[c], dqnT[c][:, t * nt:(t + 1) * nt], pb, pb[:, c * nt:(c + 1) * nt])
            return lambda: self.to_fm(tb, 3, None, dst, t)

        self.proj_tm(l, MIXOFF["dq"], 384, c_dq)
        if self.stop_after == "cdq":
            return
        QPT = [self.palloc() for _ in range(4)]
        slot, wv = self.wget(self.wb["wuq"][l].rearrange("(c p) n -> p c n", p=128), [3, 768], self.wT[("wuq", l)])
        for t in range(ntl):
            ps1 = self.psA.next()
            for c in range(3):
                self.mm(ps1, ps1[:nt, :512], dqnT[c], dqnT[c][:, t * nt:(t + 1) * nt], slot, wv[:, c, 0:512], c == 0, c == 2)
            lat = c_q(t, ps1)
            ps2 = self.psA.next()
            for c in range(3):
                self.mm(ps2, ps2[:nt, :256], dqnT[c], dqnT[c][:, t * nt:(t + 1) * nt], slot, wv[:, c, 512:768], c == 0, c == 2)
            sg = self.stg.next()
            self.cp(sg, sg[:nt, :256], ps2, ps2[:nt, :256], "act")
            self.rope(sg, sg[:nt, :256].rearrange("p (h d) -> p h d", h=8), 8, 16, self.ropeC, kt0 + t)
            tb = self.tokb.next()
            self.V(lambda e, tb=tb: e.memset(tb[:nt, 0:512], 0.0), [], [tb])
            self.cp(tb, tb[:nt, 0:512].rearrange("p (h d) -> p h d", h=8)[:, :, 0:32], sg, sg[:nt, :256].rearrange("p (h d) -> p h d", h=8), "dve")

            def dst(pb, t=t):
                for g in range(4):
                    self.cp(QPT[g], QPT[g][:, t * nt:(t + 1) * nt], pb, pb[:, g * nt:(g + 1) * nt])
            lat()
            self.to_fm(tb, 4, None, dst, t)
        for c in range(3):
            self.pfree(dqnT[c])
        if self.stop_after == "cq":
            return
        ckvT = [self.palloc() for _ in range(2)]

        def c_kv(t, ps):
            kp = self.kp32.next()
            self.cp(kp, kp[:nt, :], ps, ps[:nt, 256:288], "act")
            sg = self.stg.next()
            rms_to(ps, 256, gkv_t, sg, sg[:nt, 0:256])
            self.out_dma("mla_ckv", l, s, tok0 + t * nt, nt, sg, sg[:nt, 0:256])
            tb = self.tokb.next()
            self.cp(tb, tb[:nt, :256], sg, sg[:nt, :256], "dve")
            self.rope(kp, kp[:nt, :].rearrange("p (h d) -> p h d", h=1), 1, 16, self.ropeC, kt0 + t)
            self.out_dma("mla_kpe", l, s, tok0 + t * nt, nt, kp, kp[:nt, :])
            self.V(lambda e, tb=tb: e.memset(tb[:nt, 256:384], 0.0), [], [tb])
            for r in range(2):
                self.cp(tb, tb[:nt, 256 + r * 64:256 + r * 64 + 32], kp, kp[:nt, :], "dve")

            def dst(pb):
                for c in range(2):
                    self.cp(ckvT[c], ckvT[c][:, t * nt:(t + 1) * nt], pb, pb[:, c * nt:(c + 1) * nt])
                self.cp(KPT, KPT[:, pos0 + t * nt:pos0 + (t + 1) * nt], pb, pb[:, 2 * nt:3 * nt])
            return lambda: self.to_fm(tb, 3, None, dst, t)

        self.proj_tm(l, MIXOFF["dkv"], 288, c_kv)
        self.mla_kv(l, ckvT, [(0, NT)], pos0, kt0, nt, ntl)
        for c in range(2):
            self.pfree(ckvT[c])
        if self.stop_after == "ckv":
            return
        Otok = [self.palloc() for _ in range(ntl)]

        def kq_c(h):
            c, b = h // 2, (h % 2) * 64
            return [(KTc, lambda kc0, nk: KTc[:, c, kc0:kc0 + nk], QT, lambda q0, q1: QT[:, h, q0:q1]),
                    (KPT, lambda kc0, nk: KPT[b:b + 64, kc0:kc0 + nk], QPT[c], lambda q0, q1: QPT[c][b:b + 64, q0:q1])]

        self.attention(8, kq_c, Vc, lambda kt, h, nk: Vc[:nk, kt, h, 0:65], 64, 96.0 ** -0.5, False, self.maskd if self.kind == "p" else None, fin_simple)
        for g in range(4):
            self.pfree(QPT[g])
        self.ot_from(Otok, 2)
        if self.stop_after == "mla":
            for cc in range(12):
                self.P.dma("pool", lambda e, cc=cc: e.dma_start(out=d["dbg_ot"][:, cc, :], in_=OT[:, cc, :]), reads=[OT])
            return

        MG = [self.palloc() for _ in range(8)]
        bg_t = self.bg_t
        for b in range(3):
            for half in range(2):
                slot_b, wb = self.wget(self.wb["wbr"][l, b].rearrange("(c p) n -> p c n", p=128)[:, :, half * 512:(half + 1) * 512], [4, 512], self.wT[("wbr", l)])
                slot_g, wgv = self.wget(self.wb["wg"][l].rearrange("(c p) n -> p c n", p=128)[:, :, b * 1024 + half * 512:b * 1024 + (half + 1) * 512], [8, 512], self.wT[("wg", l)])
                for o4 in range(4):
                    oc = half * 4 + o4
                    pgt = self.psA.next()
                    for c in range(8):
                        self.mm(pgt, pgt[:, :NT], slot_g, wgv[:, c, o4 * 128:(o4 + 1) * 128], XT, XT[:, c, :NT], c == 0, c == 7)
                    gsb = self.ftr.next()
                    bcol = b * 8 + oc
                    self.A(lambda e, gsb=gsb, pgt=pgt, bcol=bcol: e.activation(out=gsb[:, :NT], in_=pgt[:, :NT], func=AF.Sigmoid, bias=bg_t[:, bcol:bcol + 1], scale=1.0), [pgt, bg_t], [gsb])
                    pbr = self.psA.next()
                    for kc in range(4):
                        self.mm(pbr, pbr[:, :NT], slot_b, wb[:, kc, o4 * 128:(o4 + 1) * 128], OT, OT[:, b * 4 + kc, :NT], kc == 0, kc == 3)
                    mg = MG[oc]
                    if b == 0:
                        self.V(lambda e, mg=mg, gsb=gsb, pbr=pbr: e.tensor_tensor(out=mg[:, :NT], in0=gsb[:, :NT], in1=pbr[:, :NT], op=ALU.mult), [gsb, pbr], [mg])
                    else:
                        self.V(lambda e, gsb=gsb, pbr=pbr: e.tensor_tensor(out=gsb[:, :NT], in0=gsb[:, :NT], in1=pbr[:, :NT], op=ALU.mult), [gsb, pbr], [gsb])
                        self.V(lambda e, mg=mg, gsb=gsb: e.tensor_tensor(out=mg[:, :NT], in0=mg[:, :NT], in1=gsb[:, :NT], op=ALU.add), [mg, gsb], [mg])
        for nh in range(2):
            slot, wv = self.wget(self.wb["wo"][l].rearrange("(c p) n -> p c n", p=128)[:, :, nh * 512:(nh + 1) * 512], [8, 512], self.wT[("wo", l)])
            for t in range(ntl):
                ps = self.psA.next()
                for kc in range(8):
                    self.mm(ps, ps[:nt, :], MG[kc], MG[kc][:, t * nt:(t + 1) * nt], slot, wv[:, kc, :], kc == 0, kc == 7)
                x = self.x[t]
                xa = x[:nt, nh * 512:(nh + 1) * 512]
                self.V(lambda e, xa=xa, ps=ps: e.scalar_tensor_tensor(out=xa, in0=xa, scalar=ALPHA, in1=ps[:nt, :], op0=ALU.mult, op1=ALU.add), [x, ps], [x])
        for m in MG:
            self.pfree(m)
        self.layernorm(l, 1)

    def mla_kv(self, l, ckvT, ranges, pos0, kt0, nt, ntl):
        d = self.d
        KTc, Vc = self.KTc, self.Vc
        slot, wv = self.wget(self.wb["wuk"][l].rearrange("(c p) n -> p c n", p=128), [2, 512], self.wT[("wuk", l)])
        for (c0, n) in ranges:
            for c in range(4):
                ps = self.psA.next()
                for lc in range(2):
                    self.mm(ps, ps[:, :n], slot, wv[:, lc, c * 128:(c + 1) * 128], ckvT[lc], ckvT[lc][:, c0:c0 + n], lc == 0, lc == 1)
                self.cp(KTc, KTc[:, c, pos0 + c0:pos0 + c0 + n], ps, ps[:, :n])
        slot, wv = self.wget(self.wb["wuv"][l].rearrange("(c p) n -> p c n", p=128), [2, 512], self.wT[("wuv", l)])
        for t in range(ntl):
            ps = self.psA.next()
            for lc in range(2):
                self.mm(ps, ps[:nt, :], ckvT[lc], ckvT[lc][:, t * nt:(t + 1) * nt], slot, wv[:, lc, :], lc == 0, lc == 1)
            tb = self.tokb.next()
            self.cp(tb, tb[:nt, 0:512], ps, ps[:nt, :], "act")
            self.cp(Vc, Vc[:nt, kt0 + t, :, 0:64], tb, tb[:nt, 0:512].rearrange("p (h d) -> p h d", h=8), "dve")

    def ple(self, l, s):
        nt, ntl, NT, pos0 = self.nt, self.ntl, self.NT, self.pos0
        d = self.d
        XT = self.XT
        self.make_XT()
        pt = [self.palloc() for _ in range(2)]
        src = d["ppT" if self.kind == "p" else "psT"]
        for kc in range(2):
            sap = src[l, s, kc, :, self.tok0:self.tok0 + NT]
            self.P.dma("pool", lambda e, kc=kc, sap=sap: e.dma_start(out=pt[kc][:, :NT], in_=sap), writes=[pt[kc]])
        onesb, bhi = self.onesb, self.bhi
        for nh in range(2):
            slot, wv = self.wget(self.wb["wpg"][l].rearrange("(c p) n -> p c n", p=128)[:, :, nh * 512:(nh + 1) * 512], [8, 512], self.wT[("wpg", l)])
            slot2, wp = self.wget(self.wb["wpp"][l].rearrange("(c p) n -> p c n", p=128)[:, :, nh * 512:(nh + 1) * 512], [2, 512], self.wT[("wpp", l)])
            for t in range(ntl):
                pg = self.psA.next()
                for c in range(8):
                    self.mm(pg, pg[:nt, :], XT, XT[:, c, t * nt:(t + 1) * nt], slot, wv[:, c, :], c == 0, False)
                self.mm(pg, pg[:nt, :], onesb, onesb[0:1, :nt], bhi, bhi[0:1, nh * 512:(nh + 1) * 512], False, True)
                gs = self.ftr.next()
                self.A(lambda e, gs=gs, pg=pg: e.activation(out=gs[:nt, :], in_=pg[:nt, :], func=AF.Sigmoid), [pg], [gs])
                pp = self.psA.next()
                for kc in range(2):
                    self.mm(pp, pp[:nt, :], pt[kc], pt[kc][:, t * nt:(t + 1) * nt], slot2, wp[:, kc, :], kc == 0, kc == 1)
                self.V(lambda e, gs=gs, pp=pp: e.tensor_tensor(out=gs[:nt, :], in0=gs[:nt, :], in1=pp[:nt, :], op=ALU.mult), [gs, pp], [gs])
                x = self.x[t]
                xa = x[:nt, nh * 512:(nh + 1) * 512]
                self.V(lambda e, xa=xa, gs=gs: e.scalar_tensor_tensor(out=xa, in0=xa, scalar=ALPHA, in1=gs[:nt, :], op0=ALU.mult, op1=ALU.add), [x, gs], [x])
        for p_ in pt:
            self.pfree(p_)
        self.layernorm(l, 3)

    def block(self, l, kind, s, B):
        self.kind = kind
        if kind == "p":
            self.nt, self.ntl = 128, 4
            self.kt0 = 4 * B
            self.tok0 = 512 * B
            self.pos0 = 512 * B
        else:
            self.nt, self.ntl = 64, 1
            self.kt0 = 8
            self.tok0 = 0
            self.pos0 = 1024
        self.NT = self.nt * self.ntl
        nt, ntl, tok0 = self.nt, self.ntl, self.tok0
        d, P = self.d, self.P
        xmid = self.xmid_p if kind == "p" else self.xmid_s
        xin = d["xp" if kind == "p" else "xs"]
        for t in range(ntl):
            x = self.x[t]
            if l == 0:
                sap = xin[s, tok0 + t * nt:tok0 + (t + 1) * nt, :]
                P.dma("sp", lambda e, x=x, sap=sap: e.dma_start(out=x[:nt, :], in_=sap), writes=[x])
            else:
                sap = xmid[s, tok0 + t * nt:tok0 + (t + 1) * nt, :]
                P.dma("sp", lambda e, x=x, sap=sap: e.dma_start(out=x[:nt, :], in_=sap), writes=[x], extra=list(self.xmid_ev.values()))
        sa = self.stop_after
        self.ffn(l, 1)
        if sa != "ffn1":
            self.mix(l, s)
            if sa is None or sa == "ffn2":
                self.ffn(l, 2)
                if sa is None:
                    self.ple(l, s)
        last = (l == self.L - 1)
        for t in range(ntl):
            x = self.x[t]
            if last:
                self.out_dma("y", None, s, tok0 + t * nt, nt, x, x[:nt, :])
            else:
                dap = xmid[s, tok0 + t * nt:tok0 + (t + 1) * nt, :]
                ev = P.dma("pool", lambda e, x=x, dap=dap: e.dma_start(out=dap, in_=x[:nt, :]), reads=[x])
                self.xmid_ev[id(ev[1])] = ev

    def sample_prep(self, l, s):
        d, P = self.d, self.P
        KTa, Va, KTb, Vb, KPT, lfc = self.KTa, self.Va, self.KTb, self.Vb, self.KPT, self.lfc
        P.dma("pool", lambda e: e.dma_start(out=KTa[:, :, 0:1024], in_=d["cka"][l, s].rearrange("c p n -> p c n")), writes=[KTa])
        P.dma("pool", lambda e: e.dma_start(out=KTb[:, :, 0:1024], in_=d["ckb"][l, s].rearrange("c p n -> p c n")), writes=[KTb])
        P.dma("pool", lambda e: e.dma_start(out=KPT[:, 0:1024], in_=d["ckpT"][l, s]), writes=[KPT])
        for kt in range(8):
            P.dma("pool", lambda e, kt=kt: e.dma_start(out=Va[:, kt, :, 0:64], in_=d["cva"][l, s, 128 * kt:128 * (kt + 1), :].rearrange("p (h d) -> p h d", h=8)), writes=[Va])
            P.dma("pool", lambda e, kt=kt: e.dma_start(out=Vb[:, kt, :, 0:128], in_=d["cvb"][l, s, 128 * kt:128 * (kt + 1), :].rearrange("p (h d) -> p h d", h=4)), writes=[Vb])
        P.dma("sp", lambda e: e.dma_start(out=lfc[:, :, :], in_=d["clf"][l, s].rearrange("(t p) h -> p t h", p=128)), writes=[lfc])
        ck = [[self.palloc() for _ in range(2)] for _ in range(2)]
        for lc in range(2):
            for hf in range(2):
                P.dma("pool", lambda e, lc=lc, hf=hf: e.dma_start(out=ck[lc][hf][:, :], in_=d["cckT"][l, s, lc, :, hf * 512:(hf + 1) * 512]), writes=[ck[lc][hf]])
        for hf in range(2):
            self.mla_kv(l, [ck[0][hf], ck[1][hf]], [(0, 512)], hf * 512, hf * 4, 128, 4)
        for lc in range(2):
            for hf in range(2):
                self.pfree(ck[lc][hf])
        for kt in range(8):
            self.cumsum(kt, lfc, lfc[:, kt, :], 128)

    def build(self):
        self.setup()
        NP, NS = self.NP, self.NS
        carry = self.carry
        for l in range(self.L):
            self.layer_params(l)
            for s in range(NP):
                self.V(lambda e: e.memset(carry[:, :], 0.0), [], [carry])
                for B in range(self.nblk):
                    self.block(l, "p", s, B)
            for s in range(NS):
                self.V(lambda e: e.memset(carry[:, :], 0.0), [], [carry])
                self.kind = "s"
                self.sample_prep(l, s)
                self.block(l, "s", s, 0)
        self.P.wait_all_dma("sp")
        self.P.finish()
        return self.nc


def build_program(NP=4, NS=2, L=2, nblk=4, stop_after=None, WSLOTS=2):
    kb = KB(NP, NS, L, WSLOTS=WSLOTS, stop_after=stop_after)
    kb.nblk = nblk
    nc = kb.build()
    n = {e: len(kb.P.q[e]) for e in ENGINES}
    print("instr counts:", n, "dma sems:", len(kb.P.owners))
    return nc


def _rope_tab(half, d, theta):
    inv = np.exp(np.float32(-math.log(theta)) * np.arange(half, dtype=np.float32) * np.float32(2.0 / d)).astype(np.float32)
    pos = np.arange(2048, dtype=np.float32)
    ang = (pos[:, None] * inv[None, :]).astype(np.float32)
    tab = np.stack([np.cos(ang), np.sin(ang)], axis=1).astype(np.float32)
    return np.ascontiguousarray(tab.reshape(16, 128, 2, half).transpose(1, 0, 2, 3))


def shared_inputs(inp):
    f = np.float32
    A = lambda a: np.ascontiguousarray(a, dtype=f)
    w = inp["w_in_mix"]
    sp = np.cumsum([0, 512, 512, 512, 8, 512, 512, 512, 384, 256, 32])
    seg = {n: w[:, :, sp[i]:sp[i + 1]] for i, n in enumerate(["qa", "ka", "va", "fa", "qb", "kb", "vb", "dq", "dkv", "kr"])}
    wmix = np.concatenate([seg[n] for n in ["qa", "ka", "va", "qb", "kb", "vb", "dq", "dkv", "kr", "fa"]], axis=2)
    wuq = inp["mla_w_uq"].reshape(2, 384, 8, 96)
    wuq = np.concatenate([wuq[..., :64].reshape(2, 384, 512), wuq[..., 64:].reshape(2, 384, 256)], axis=2)
    wukv = inp["mla_w_ukv"]
    k = np.arange(128)
    sh = {
        "w1i": A(inp["ffn1_w_in"]), "w1o": A(inp["ffn1_w_out"]), "w2i": A(inp["ffn2_w_in"]), "w2o": A(inp["ffn2_w_out"]),
        "lng": A(inp["ln_g"]), "lnb": A(inp["ln_b"]), "wmix": A(wmix), "bfg": A(inp["b_forget"]),
        "dlam": A(inp["diff_lambda"].reshape(2, 256)), "dng": A(inp["diff_norm_g"]), "gq": A(inp["mla_q_norm_g"]),
        "wuq": A(wuq), "gkv": A(inp["mla_kv_norm_g"]),
        "wuk": A(wukv[..., :64].reshape(2, 256, 512)), "wuv": A(wukv[..., 64:].reshape(2, 256, 512)),
        "wbr": A(inp["w_branch"]), "wg": A(inp["w_gate"]), "bg": A(inp["b_gate"].reshape(2, 24, 128).transpose(0, 2, 1)),
        "wo": A(inp["w_out"]), "wpg": A(inp["ple_w_gate"]), "bpg": A(inp["ple_b_gate"]), "wpp": A(inp["ple_w_proj"]),
        "c_ident": np.eye(128, dtype=f),
        "c_maskc": np.where(k[:, None] <= k[None, :], 0.0, NEG).astype(f),
        "c_maskd": np.where((k[:, None] // 64) <= (k[None, :] // 64), 0.0, NEG).astype(f),
        "c_tri": (k[:, None] <= k[None, :]).astype(f),
        "c_ropeB": _rope_tab(8, 16, 500000.0), "c_ropeC": _rope_tab(16, 32, 10000.0),
    }
    return sh


def core_inputs(inp, sh, pseqs, sseqs):
    f = np.float32
    A = lambda a: np.ascontiguousarray(a, dtype=f)
    m = dict(sh)
    m["xp"] = A(inp["x_prompt"][pseqs])
    pp = inp["p_prompt"][:, pseqs]
    m["ppT"] = A(pp.transpose(0, 1, 3, 2).reshape(2, len(pseqs), 2, 128, 2048))
    if len(sseqs):
        ns = len(sseqs)
        m["xs"] = A(inp["x_sample"][sseqs])
        m["psT"] = A(inp["p_sample"][:, sseqs].transpose(0, 1, 3, 2).reshape(2, ns, 2, 128, 64))
        m["cka"] = A(inp["cache_fox_k"][:, sseqs].reshape(2, ns, 1024, 4, 128).transpose(0, 1, 3, 4, 2))
        m["cva"] = A(inp["cache_fox_v"][:, sseqs].reshape(2, ns, 1024, 512))
        m["clf"] = A(inp["cache_fox_logf"][:, sseqs])
        m["ckb"] = A(inp["cache_diff_k"][:, sseqs].reshape(2, ns, 1024, 4, 128).transpose(0, 1, 3, 4, 2))
        m["cvb"] = A(inp["cache_diff_v"][:, sseqs].reshape(2, ns, 1024, 512))
        m["cckT"] = A(inp["cache_mla_ckv"][:, sseqs].reshape(2, ns, 1024, 2, 128).transpose(0, 1, 3, 4, 2))
        kp = inp["cache_mla_kpe"][:, sseqs].transpose(0, 1, 3, 2)
        z = np.zeros_like(kp)
        m["ckpT"] = A(np.concatenate([kp, z, kp, z], axis=2))
    return m


_NC_CACHE = {}


def kernel(**inputs):
    inp = {k: np.asarray(v) for k, v in inputs.items()}
    ncores = 8
    NP, NS = 4, 2
    if "full" not in _NC_CACHE:
        _NC_CACHE["full"] = build_program(NP, NS, 2)
    nc = _NC_CACHE["full"]
    sh = shared_inputs(inp)
    in_maps = []
    for c in range(ncores):
        in_maps.append(core_inputs(inp, sh, list(range(NP * c, NP * (c + 1))), list(range(NS * c, NS * (c + 1)))))
    res = run_bass_kernel_spmd(nc, in_maps, core_ids=list(range(ncores)))
    R = res.results

    def cat(name, axis):
        return np.concatenate([np.asarray(r[name]) for r in R], axis=axis)

    y_p = cat("y_p", 0)
    y_s = cat("y_s", 0)
    outs = [y_p, y_s]
    shp = {"fox_k": (8, 64), "fox_v": (8, 64), "fox_logf": (8,), "diff_k": (4, 2, 64), "diff_v": (4, 128), "mla_ckv": (256,), "mla_kpe": (32,)}
    for nm in ["fox_k", "fox_v", "fox_logf", "diff_k", "diff_v", "mla_ckv", "mla_kpe"]:
        a = cat(nm + "_p", 1)
        b = cat(nm + "_s", 1)
        outs.append(a.reshape(2, 32, 2048, *shp[nm]))
        outs.append(b.reshape(2, 16, 64, *shp[nm]))
    return tuple(np.ascontiguousarray(o, dtype=np.float32) for o in outs)
```

```python
import math
from collections import deque
import numpy as np
import concourse.bass as bass
import concourse.mybir as mybir
from concourse.bass_utils import run_bass_kernel_spmd

F32 = mybir.dt.float32
BF16 = mybir.dt.bfloat16
AF = mybir.ActivationFunctionType
ALU = mybir.AluOpType

ENGINES = ["pe", "act", "dve", "pool", "sp"]
ALPHA = 4.0 ** 0.25
LN_EPS = 1e-5
RMS_EPS = 1e-6
NEG = -30000.0


class T:
    __slots__ = ("h", "name", "w", "r", "sem", "cnt", "excl", "base")

    def __init__(self, h, name=""):
        self.base = self
        self.excl = False
        self.h = h
        self.name = name
        self.w = None
        self.r = {}
        self.sem = None
        self.cnt = 0

    def __getitem__(self, k):
        return self.h[k]

    def view(self, h):
        v = T(h, self.name + "_v")
        v.base = self
        return v


class Rec:
    __slots__ = ("fn", "waits", "inc", "dma")

    def __init__(self, fn):
        self.fn = fn
        self.waits = []
        self.inc = False
        self.dma = None


class Prog:
    def __init__(self, nc):
        self.nc = nc
        self.q = {e: [] for e in ENGINES}
        self.seen = {e: {} for e in ENGINES}
        self.esem = {}
        self._ctx = []
        self.owners = []
        for e in ENGINES:
            self.esem[e] = self._sem("s_" + e)

    def _sem(self, name):
        cm = self.nc.semaphore(name)
        s = cm.__enter__()
        self._ctx.append(cm)
        return s

    def sbuf(self, name, shape, dt):
        cm = self.nc.sbuf_tensor(name, list(shape), dt)
        h = cm.__enter__()
        self._ctx.append(cm)
        return T(h, name)

    def psum(self, name, shape, dt):
        cm = self.nc.psum_tensor(name, list(shape), dt)
        h = cm.__enter__()
        self._ctx.append(cm)
        t = T(h, name)
        t.excl = True
        return t

    def _need(self, eng, rec, ev):
        if ev is None:
            return
        if ev[0] == "eng":
            _, e2, idx = ev
            if e2 == eng and eng in ("pe", "sp"):
                return
            key = ("eng", e2)
            if self.seen[eng].get(key, -1) >= idx:
                return
            self.seen[eng][key] = idx
            self.q[e2][idx].inc = True
            rec.waits.append(ev)
        else:
            _, sem, val = ev
            key = ("dma", id(sem))
            if self.seen[eng].get(key, -1) >= val:
                return
            self.seen[eng][key] = val
            rec.waits.append(ev)

    def _deps(self, eng, rec, reads, writes):
        for t in reads:
            self._need(eng, rec, t.w)
            if t.excl:
                for k, ev in t.r.items():
                    if k != eng:
                        self._need(eng, rec, ev)
        for t in writes:
            self._need(eng, rec, t.w)
            for ev in t.r.values():
                self._need(eng, rec, ev)

    def op(self, eng, fn, reads=(), writes=()):
        reads = [t.base for t in reads]
        writes = [t.base for t in writes]
        rec = Rec(fn)
        self._deps(eng, rec, reads, writes)
        idx = len(self.q[eng])
        self.q[eng].append(rec)
        ev = ("eng", eng, idx)
        for t in reads:
            t.r[eng] = ev
        for t in writes:
            t.w = ev
            t.r = {}
        return ev

    def dma(self, queue, fn, reads=(), writes=(), extra=()):
        reads = [t.base for t in reads]
        writes = [t.base for t in writes]
        rec = Rec(fn)
        for ev in extra:
            self._need(queue, rec, ev)
        owner = (list(writes) + list(reads))[0]
        if owner.sem is None:
            owner.sem = {}
            owner.cnt = {}
        if queue not in owner.sem:
            owner.sem[queue] = self._sem("d%s_%s" % (queue[0], owner.name))
            owner.cnt[queue] = 0
            self.owners.append((owner, queue))
        sem = owner.sem[queue]
        for t in reads:
            self._need(queue, rec, t.w)
        for t in writes:
            if not (t.w is not None and t.w[0] == "dma" and t.w[1] is sem):
                self._need(queue, rec, t.w)
            for ev in t.r.values():
                self._need(queue, rec, ev)
        owner.cnt[queue] += 16
        ev = ("dma", sem, owner.cnt[queue])
        rec.dma = (sem, 16)
        self.q[queue].append(rec)
        for t in reads:
            t.r["dma%d%s" % (id(owner), queue)] = ev
        for t in writes:
            t.w = ev
            t.r = {}
        return ev

    def wait_all_dma(self, eng):
        rec = Rec(None)
        for o, qn in self.owners:
            self._need(eng, rec, ("dma", o.sem[qn], o.cnt[qn]))
        self.q[eng].append(rec)

    def finish(self):
        nc = self.nc
        pref = {}
        for e in ENGINES:
            c = 0
            arr = []
            for rec in self.q[e]:
                if rec.inc:
                    c += 1
                arr.append(c)
            pref[e] = arr
        esem = self.esem
        q = self.q

        def run(e, engobj):
            for rec in q[e]:
                for ev in rec.waits:
                    if ev[0] == "eng":
                        engobj.wait_ge(esem[ev[1]], pref[ev[1]][ev[2]])
                    else:
                        engobj.wait_ge(ev[1], ev[2])
                if rec.fn is None:
                    continue
                ins = rec.fn(engobj)
                if rec.dma is not None:
                    ins.then_inc(rec.dma[0], rec.dma[1])
                if rec.inc:
                    ins.then_inc(esem[e], 1)

        with nc.Block() as block:
            @block.tensor
            def _(eng):
                run("pe", eng)

            @block.scalar
            def _(eng):
                run("act", eng)

            @block.vector
            def _(eng):
                run("dve", eng)

            @block.gpsimd
            def _(eng):
                run("pool", eng)

            @block.sync
            def _(eng):
                run("sp", eng)
        for cm in reversed(self._ctx):
            cm.__exit__(None, None, None)
        self._ctx = []


class Ring:
    def __init__(self, tiles):
        self.t = tiles
        self.i = 0

    def next(self):
        t = self.t[self.i % len(self.t)]
        self.i += 1
        return t


OUT_SPECS = [
    ("y", 1024, False), ("fox_k", 512, True), ("fox_v", 512, True), ("fox_logf", 8, True),
    ("diff_k", 512, True), ("diff_v", 512, True), ("mla_ckv", 256, True), ("mla_kpe", 32, True)]

MIXOFF = dict(qa=0, ka=512, va=1024, qb=1536, kb=2048, vb=2560, dq=3072, dkv=3456, kr=3712, fa=3744)


class KB:
    def __init__(self, NP, NS, L=2, WSLOTS=3, stop_after=None):
        self.NP, self.NS, self.L = NP, NS, L
        self.stop_after = stop_after
        nc = bass.Bass("TRN2", target_bir_lowering=False)
        self.nc = nc
        P = Prog(nc)
        self.P = P
        d = {}
        self.d = d

        def din(name, shape):
            d[name] = nc.dram_tensor(name, list(shape), F32, kind="ExternalInput").ap()

        def dout(name, shape):
            d[name] = nc.dram_tensor(name, list(shape), F32, kind="ExternalOutput").ap()

        din("xp", [NP, 2048, 1024])
        din("ppT", [2, NP, 2, 128, 2048])
        if NS:
            din("xs", [NS, 64, 1024])
            din("psT", [2, NS, 2, 128, 64])
            din("cka", [2, NS, 4, 128, 1024])
            din("cva", [2, NS, 1024, 512])
            din("clf", [2, NS, 1024, 8])
            din("ckb", [2, NS, 4, 128, 1024])
            din("cvb", [2, NS, 1024, 512])
            din("cckT", [2, NS, 2, 128, 1024])
            din("ckpT", [2, NS, 128, 1024])
        din("w1i", [2, 1024, 5632]); din("w1o", [2, 2816, 1024])
        din("w2i", [2, 1024, 5632]); din("w2o", [2, 2816, 1024])
        din("lng", [2, 4, 1024]); din("lnb", [2, 4, 1024])
        din("wmix", [2, 1024, 3752]); din("bfg", [2, 8]); din("dlam", [2, 256]); din("dng", [2, 128])
        din("gq", [2, 384]); din("wuq", [2, 384, 768]); din("gkv", [2, 256])
        din("wuk", [2, 256, 512]); din("wuv", [2, 256, 512])
        din("wbr", [2, 3, 512, 1024]); din("wg", [2, 1024, 3072]); din("bg", [2, 128, 24])
        din("wo", [2, 1024, 1024]); din("wpg", [2, 1024, 1024]); din("bpg", [2, 1024]); din("wpp", [2, 256, 1024])
        din("c_ident", [128, 128]); din("c_maskc", [128, 128]); din("c_maskd", [128, 128]); din("c_tri", [128, 128])
        din("c_ropeB", [128, 16, 2, 8]); din("c_ropeC", [128, 16, 2, 16])
        for nm, wd, hasl in OUT_SPECS:
            dout(nm + "_p", ([2] if hasl else []) + [NP, 2048, wd])
            if NS:
                dout(nm + "_s", ([2] if hasl else []) + [NS, 64, wd])
        if stop_after == "mla":
            dout("dbg_ot", [128, 12, 512])
        self.WNAMES = ["w1i", "w1o", "wmix", "wuq", "wuk", "wuv", "wbr", "wg", "wo", "w2i", "w2o", "wpg", "wpp"]
        self.wb = {}
        self.wT = {}
        for nm in self.WNAMES:
            self.wb[nm] = nc.dram_tensor(nm + "_bf", list(d[nm].shape), BF16).ap()
            for l in range(2):
                self.wT[(nm, l)] = T(None, "c_%s%d" % (nm, l))
        self.xmid_p = nc.dram_tensor("xmid_p", [NP, 2048, 1024], F32).ap()
        self.xmid_ev = {}
        if NS:
            self.xmid_s = nc.dram_tensor("xmid_s", [NS, 64, 1024], F32).ap()

        sb = P.sbuf
        self.x = [sb("x%d" % i, [128, 1024], F32) for i in range(4)]
        self.XT = sb("XT", [128, 8, 512], BF16)
        self.QT = sb("QT", [128, 8, 512], BF16)
        self.OT = sb("OT", [128, 12, 512], BF16)
        self.tokb = Ring([sb("tokb%d" % i, [128, 1024], BF16) for i in range(2)])
        self.stg = Ring([sb("stg%d" % i, [128, 512], F32) for i in range(3)])
        self.ftr = self.stg
        self.pool_big = sb("plbig", [128, 12, 512], BF16)
        self.pool_tiles = [T(self.pool_big.h[:, i, :], "pl%d" % i) for i in range(12)]
        self.free = deque(self.pool_tiles)
        self.wring = Ring([sb("wr%d" % i, [128, 4096], BF16) for i in range(WSLOTS)])
        self.gb = sb("gb", [128, 1024], F32)
        self.KTa = sb("KTa", [128, 4, 2048], BF16)
        self.KTb = sb("KTb", [128, 4, 2048], BF16)
        self.KTc = sb("KTc", [128, 4, 2048], BF16)
        self.KPT = sb("KPT", [128, 2048], BF16)
        self.Va = sb("Va", [128, 16, 8, 66], BF16)
        self.Vb = sb("Vb", [128, 16, 4, 130], BF16)
        self.Vc = sb("Vc", [128, 16, 8, 66], BF16)
        self.FK = sb("FK", [128, 16, 8], F32)
        self.BK = sb("BK", [128, 16, 8], F32)
        self.carry = sb("carry", [128, 8], F32)
        self.cref = sb("cref", [128, 8], F32)
        self.lfc = sb("lfc", [128, 8, 8], F32)
        self.bf_t = sb("bf_t", [128, 8], F32)
        self.dng_t = sb("dng_t", [128, 128], F32)
        self.gq_t = sb("gq_t", [128, 384], F32)
        self.gkv_t = sb("gkv_t", [128, 256], F32)
        self.bg_t = sb("bg_t", [128, 24], F32)
        self.bhi = sb("bhi", [1, 1024], BF16)
        self.dl_t = sb("dl_t", [128, 256], F32)
        self.dl2 = sb("dl2", [128, 2, 64], F32)
        self.lam2 = sb("lam2", [128, 2], F32)
        self.neglam = sb("neglam", [128, 1], F32)
        self.ident = sb("ident", [128, 128], BF16)
        self.maskc = sb("maskc", [128, 128], BF16)
        self.maskd = sb("maskd", [128, 128], BF16)
        self.tri = sb("tri", [128, 128], F32)
        self.ones = sb("ones", [128, 128], F32)
        self.onesb = sb("onesb", [1, 128], BF16)
        self.ropeB = sb("ropeB", [128, 16, 2, 8], F32)
        self.ropeC = sb("ropeC", [128, 16, 2, 16], F32)
        self.st = sb("st", [128, 2, 6], F32)
        self.mv = sb("mv", [128, 2], F32)
        self.rstd = sb("rstd", [128, 1], F32)
        self.nmr = sb("nmr", [128, 1], F32)
        self.rc = Ring([sb("rc%d" % i, [128, 1], F32) for i in range(4)])
        self.ss = Ring([sb("ss%d" % i, [128, 1], F32) for i in range(2)])
        self.t1 = [sb("t1_%d" % i, [128, 128], F32) for i in range(4)]
        self.obf = Ring([sb("obf%d" % i, [128, 128], F32) for i in range(2)])
        self.junk = sb("junk", [128, 384], F32)
        self.e8 = Ring([sb("e8_%d" % i, [128, 8], F32) for i in range(2)])
        self.lf8 = Ring([sb("lf8_%d" % i, [128, 8], F32) for i in range(2)])
        self.kp32 = Ring([sb("kp32_%d" % i, [128, 32], F32) for i in range(2)])
        self.rt = [sb("rt%d" % i, [128, 8, 16], F32) for i in range(4)]
        self.psA = Ring([P.psum("psA%d" % i, [128, 512], F32) for i in range(4)])
        self.psS = Ring([P.psum("psS%d" % i, [128, 512], F32) for i in range(2)])
        self.psB = Ring([P.psum("psB%d" % i, [128, 1024], BF16) for i in range(2)])
        self.psS = Ring(self.psS.t + [t_.view(t_.h[:, :].bitcast(F32)) for t_ in self.psB.t])
        self.cp_i = 0
        print("sbuf bytes remaining:", nc.sbuf_bytes_remaining)

    def E(self, fn, reads, writes):
        return self.P.op("pe", fn, reads, writes)

    def A(self, fn, reads, writes):
        return self.P.op("act", fn, reads, writes)

    def V(self, fn, reads, writes):
        return self.P.op("dve", fn, reads, writes)

    def cp(self, oT, oap, iT, iap, eng=None):
        if eng is None:
            eng = "act" if (self.cp_i % 2 == 0) else "dve"
            self.cp_i += 1
        if eng == "act":
            self.A(lambda e: e.activation(out=oap, in_=iap, func=AF.Copy), [iT], [oT])
        else:
            self.V(lambda e: e.tensor_copy(out=oap, in_=iap), [iT], [oT])

    def mm(self, oT, oap, lT, lap, rT, rap, start, stop):
        self.E(lambda e: e.matmul(oap, lhsT=lap, rhs=rap, start=start, stop=stop), [lT, rT], [oT])

    def tr(self, oT, oap, iT, iap, n):
        ident = self.ident
        self.E(lambda e: e.transpose(out=oap, in_=iap, identity=ident[:n, :n]), [iT, ident], [oT])

    def palloc(self):
        return self.free.popleft()

    def pfree(self, t):
        self.free.append(t)

    def wget(self, src, shape, dep):
        slot = self.wring.next()
        n = int(np.prod(shape))
        assert n <= 4096
        dst = slot.h[:src.shape[0], 0:n]
        if len(shape) == 2:
            dst = dst.rearrange("p (a b) -> p a b", a=shape[0])
        elif len(shape) == 3:
            dst = dst.rearrange("p (a b c) -> p a b c", a=shape[0], b=shape[1])
        self.P.dma("sp", lambda e: e.dma_start(out=dst, in_=src), reads=[dep], writes=[slot])
        return slot, dst

    def load_gb(self, src_row):
        gb = self.gb
        self.P.dma("pool", lambda e: e.dma_start(out=gb[:, :], in_=src_row.to_broadcast([128, 1024])), writes=[gb])

    def out_dma(self, name, l, s, tok0, n, sT, sap):
        dst = self.d[name + ("_p" if self.kind == "p" else "_s")]
        dst = dst[l, s, tok0:tok0 + n, :] if l is not None else dst[s, tok0:tok0 + n, :]
        self.P.dma("pool", lambda e: e.dma_start(out=dst, in_=sap), reads=[sT])

    def setup(self):
        P, d = self.P, self.d
        for nm, t in (("c_ident", self.ident), ("c_maskc", self.maskc), ("c_maskd", self.maskd)):
            P.dma("pool", lambda e, nm=nm, t=t: e.dma_start(out=t[:, :], in_=d[nm]), writes=[t])
        for nm, t in (("c_tri", self.tri), ("c_ropeB", self.ropeB), ("c_ropeC", self.ropeC)):
            P.dma("sp", lambda e, nm=nm, t=t: e.dma_start(out=t[:], in_=d[nm]), writes=[t])
        for l in range(self.L):
            for nm in self.WNAMES:
                src = d[nm][l]
                dst = self.wb[nm][l]
                if len(src.shape) == 3:
                    src = src.rearrange("b k n -> (b k) n")
                    dst = dst.rearrange("b k n -> (b k) n")
                P.dma("pool", lambda e, src=src, dst=dst: e.dma_start(out=dst, in_=src, max_dma_last_dim=8192), writes=[self.wT[(nm, l)]])
        self.V(lambda e: e.memset(self.ones[:, :], 1.0), [], [self.ones])
        self.V(lambda e: e.memset(self.onesb[:, :], 1.0), [], [self.onesb])
        self.V(lambda e: e.memset(self.QT[:, :, :], 0.0), [], [self.QT])
        self.V(lambda e: e.memset(self.Va[:, :, :, 64:65], 1.0), [], [self.Va])
        self.V(lambda e: e.memset(self.Vb[:, :, :, 128:129], 1.0), [], [self.Vb])
        self.V(lambda e: e.memset(self.Vc[:, :, :, 64:65], 1.0), [], [self.Vc])

    def layer_params(self, l):
        P, d = self.P, self.d
        ld = lambda t, src: P.dma("sp", lambda e: e.dma_start(out=t[:], in_=src), writes=[t])
        ld(self.bf_t, d["bfg"][l:l + 1, :].to_broadcast([128, 8]))
        ld(self.dng_t, d["dng"][l:l + 1, :].to_broadcast([128, 128]))
        ld(self.gq_t, d["gq"][l:l + 1, :].to_broadcast([128, 384]))
        ld(self.gkv_t, d["gkv"][l:l + 1, :].to_broadcast([128, 256]))
        ld(self.bg_t, d["bg"][l])
        r32 = self.x[0]
        P.dma("sp", lambda e: e.dma_start(out=r32[0:1, :], in_=d["bpg"][l:l + 1, :]), writes=[r32])
        ld(self.dl_t, d["dlam"][l:l + 1, :].to_broadcast([128, 256]))
        lam_init = 0.8 - 0.6 * math.exp(-0.3 * l)
        self.lam_init = lam_init
        self.V(lambda e: e.tensor_scalar(self.dng_t[:, :], self.dng_t[:, :], 1.0 - lam_init, None, ALU.mult), [self.dng_t], [self.dng_t])
        self.V(lambda e: e.tensor_copy(out=self.bhi[:, :], in_=r32[0:1, :]), [r32], [self.bhi])
        dl = self.dl_t
        self.V(lambda e: e.tensor_tensor(out=self.dl2[:, 0, :], in0=dl[:, 0:64], in1=dl[:, 64:128], op=ALU.mult), [dl], [self.dl2])
        self.V(lambda e: e.tensor_tensor(out=self.dl2[:, 1, :], in0=dl[:, 128:192], in1=dl[:, 192:256], op=ALU.mult), [dl, self.dl2], [self.dl2])
        self.V(lambda e: e.reduce_sum(out=self.lam2[:, :], in_=self.dl2[:, :, :], axis=mybir.AxisListType.X), [self.dl2], [self.lam2])
        self.A(lambda e: e.activation(out=self.lam2[:, :], in_=self.lam2[:, :], func=AF.Exp), [self.lam2], [self.lam2])
        self.V(lambda e: e.tensor_tensor(out=self.neglam[:, :], in0=self.lam2[:, 1:2], in1=self.lam2[:, 0:1], op=ALU.subtract), [self.lam2], [self.neglam])
        self.V(lambda e: e.tensor_scalar(self.neglam[:, :], self.neglam[:, :], -lam_init, None, ALU.add), [self.neglam], [self.neglam])

    def make_XT(self):
        nt, ntl = self.nt, self.ntl
        XT = self.XT
        for t in range(ntl):
            tb = self.tokb.next()
            x = self.x[t]
            self.cp(tb, tb[:nt, :], x, x[:nt, :])
            pb = self.psB.next()
            for c in range(8):
                self.tr(pb, pb[:, c * nt:(c + 1) * nt], tb, tb[:nt, c * 128:(c + 1) * 128], nt)
            self.cp(XT, XT[:, :, t * nt:(t + 1) * nt], pb, pb[:, 0:8 * nt].rearrange("p (c n) -> p c n", c=8))

    def layernorm(self, l, k):
        nt, ntl = self.nt, self.ntl
        d = self.d
        gb = self.gb
        bt = self.pool_tiles[8:12]
        for t_ in bt:
            self.free.remove(t_)
        bview = self.pool_big.h[:, 8:12, :].rearrange("p a b -> p (a b)").bitcast(F32)
        self.load_gb(d["lng"][l, k:k + 1, :])
        brow = d["lnb"][l, k:k + 1, :]
        self.P.dma("pool", lambda e: e.dma_start(out=bview, in_=brow.to_broadcast([128, 1024])), writes=bt)
        for t in range(ntl):
            x = self.x[t]
            st, mv, rstd, nmr = self.st, self.mv, self.rstd, self.nmr
            for i in range(2):
                self.V(lambda e, i=i, x=x: e.bn_stats(out=st[:nt, i, :], in_=x[:nt, i * 512:(i + 1) * 512]), [x], [st])
            self.V(lambda e: e.bn_aggr(out=mv[:nt, :], in_=st[:nt, :, :].rearrange("p a b -> p (a b)")), [st], [mv])
            self.A(lambda e: e.activation(out=rstd[:nt, :], in_=mv[:nt, 1:2], func=AF.Sqrt, bias=LN_EPS, scale=1.0), [mv], [rstd])
            self.V(lambda e: e.reciprocal(out=rstd[:nt, :], in_=rstd[:nt, :]), [rstd], [rstd])
            self.V(lambda e: e.tensor_scalar(nmr[:nt, :], mv[:nt, 0:1], -1.0, rstd[:nt, 0:1], ALU.mult, ALU.mult), [mv, rstd], [nmr])
            self.A(lambda e, x=x: e.activation(out=x[:nt, :], in_=x[:nt, :], func=AF.Identity, bias=nmr[:nt, 0:1], scale=rstd[:nt, 0:1]), [x, nmr, rstd], [x])
            self.V(lambda e, x=x: e.tensor_tensor(out=x[:nt, :], in0=x[:nt, :], in1=gb[:nt, :], op=ALU.mult), [x, gb], [x])
            self.P.op("pool", lambda e, x=x: e.tensor_tensor(out=x[:nt, :], in0=x[:nt, :], in1=bview[:nt, :], op=ALU.add), [x] + bt, [x])
        for t_ in bt:
            self.pfree(t_)

    def ffn(self, l, which):
        nt, ntl, NT = self.nt, self.ntl, self.NT
        wn_i = "w1i" if which == 1 else "w2i"
        wn_o = "w1o" if which == 1 else "w2o"
        w_in = self.wb[wn_i][l].rearrange("(c p) (g f) -> p c g f", p=128, g=2)
        w_out = self.wb[wn_o][l].rearrange("(c p) n -> p c n", p=128)
        XT = self.XT
        self.make_XT()
        for half in range(2):
            HT = [self.palloc() for _ in range(11)]
            for jj in range(0, 11, 2):
                nj = min(2, 11 - jj)
                j0 = half * 11 + jj
                slot = self.wring.next()
                wv = slot.h[:, 0:16 * nj * 128].rearrange("p (a b c) -> p a b c", a=8, b=2)
                for g in range(2):
                    self.P.dma("sp", lambda e, g=g, wv=wv, j0=j0, nj=nj: e.dma_start(out=wv[:, :, g, :], in_=w_in[:, :, g, j0 * 128:(j0 + nj) * 128]), reads=[self.wT[(wn_i, l)]], writes=[slot])
                for q in range(nj):
                    pg = self.psA.next()
                    pu = self.psA.next()
                    for g, ps in ((0, pg), (1, pu)):
                        for c in range(8):
                            self.mm(ps, ps[:, :NT], slot, wv[:, c, g, q * 128:(q + 1) * 128], XT, XT[:, c, :NT], c == 0, c == 7)
                    ft = self.ftr.next()
                    self.A(lambda e, ft=ft, pg=pg: e.activation(out=ft[:, :NT], in_=pg[:, :NT], func=AF.Silu), [pg], [ft])
                    h = HT[jj + q]
                    self.V(lambda e, h=h, ft=ft, pu=pu: e.scalar_tensor_tensor(out=h[:, :NT], in0=ft[:, :NT], scalar=0.5, in1=pu[:, :NT], op0=ALU.mult, op1=ALU.mult), [ft, pu], [h])
            for nh in range(2):
                accs = [self.psA.next() for _ in range(ntl)]
                for k0, nk in ((0, 6), (6, 5)):
                    slot, wv = self.wget(w_out[:, half * 11 + k0:half * 11 + k0 + nk, nh * 512:(nh + 1) * 512], [nk, 512], self.wT[(wn_o, l)])
                    for t in range(ntl):
                        for k in range(nk):
                            h = HT[k0 + k]
                            self.mm(accs[t], accs[t][:nt, :], h, h[:, t * nt:(t + 1) * nt], slot, wv[:, k, :], k0 + k == 0, k0 + k == 10)
                for t in range(ntl):
                    x = self.x[t]
                    xa = x[:nt, nh * 512:(nh + 1) * 512]
                    acc = accs[t]
                    if half == 0:
                        self.V(lambda e, xa=xa, acc=acc: e.scalar_tensor_tensor(out=xa, in0=xa, scalar=ALPHA, in1=acc[:nt, :], op0=ALU.mult, op1=ALU.add), [x, acc], [x])
                    else:
                        self.V(lambda e, xa=xa, acc=acc: e.tensor_tensor(out=xa, in0=xa, in1=acc[:nt, :], op=ALU.add), [x, acc], [x])
            for h in HT:
                self.pfree(h)
        self.layernorm(l, 0 if which == 1 else 2)

    def proj_tm(self, l, col0, ncols, consume):
        nt, ntl = self.nt, self.ntl
        XT = self.XT
        src = self.wb["wmix"][l].rearrange("(c p) n -> p c n", p=128)[:, :, col0:col0 + ncols]
        slot, wv = self.wget(src, [8, ncols], self.wT[("wmix", l)])
        pending = None
        for t in range(ntl):
            ps = self.psA.next()
            for c in range(8):
                self.mm(ps, ps[:nt, :ncols], XT, XT[:, c, t * nt:(t + 1) * nt], slot, wv[:, c, :], c == 0, c == 7)
            nxt = consume(t, ps)
            if pending is not None:
                pending()
            pending = nxt
        if pending is not None:
            pending()

    def to_fm(self, tb, nchunks, dT, dap_fn, t, widths=None):
        nt = self.nt
        pb = self.psB.next()
        off = 0
        for c in range(nchunks):
            w = 128 if widths is None else widths[c]
            self.tr(pb, pb[:w, c * nt:(c + 1) * nt], tb, tb[:nt, off:off + w], nt)
            off += w
        dap_fn(pb)

    def rope(self, sg, view, H, half, tab, tile):
        nt = self.nt
        cos = tab[:nt, tile, 0, :].unsqueeze(1).to_broadcast([nt, H, half])
        sin = tab[:nt, tile, 1, :].unsqueeze(1).to_broadcast([nt, H, half])
        x1 = view[:, :, 0:half]
        x2 = view[:, :, half:2 * half]
        a, b, c, dd = [r[:nt, 0:H, 0:half] for r in self.rt]
        rT = self.rt
        self.V(lambda e: e.tensor_tensor(out=a, in0=x1, in1=cos, op=ALU.mult), [sg, tab], [rT[0]])
        self.V(lambda e: e.tensor_tensor(out=b, in0=x2, in1=sin, op=ALU.mult), [sg, tab], [rT[1]])
        self.V(lambda e: e.tensor_tensor(out=c, in0=x2, in1=cos, op=ALU.mult), [sg, tab], [rT[2]])
        self.V(lambda e: e.tensor_tensor(out=dd, in0=x1, in1=sin, op=ALU.mult), [sg, tab], [rT[3]])
        self.V(lambda e: e.tensor_tensor(out=x1, in0=a, in1=b, op=ALU.subtract), [rT[0], rT[1]], [sg])
        self.V(lambda e: e.tensor_tensor(out=x2, in0=c, in1=dd, op=ALU.add), [rT[2], rT[3]], [sg])

    def cumsum(self, kti, lfT, lfap, n):
        ps = self.psA.next()
        tri, ones, FK, carry = self.tri, self.ones, self.FK, self.carry
        self.mm(ps, ps[:n, 0:8], tri, tri[:n, :n], lfT, lfap, True, True)
        self.mm(ps, ps[:, 8:16], ones, ones[:n, :], lfT, lfap, True, True)
        self.V(lambda e: e.tensor_tensor(out=FK[:n, kti, :], in0=ps[:n, 0:8], in1=carry[:n, :], op=ALU.add), [ps, carry], [FK])
        self.V(lambda e: e.tensor_tensor(out=carry[:, :], in0=carry[:, :], in1=ps[:, 8:16], op=ALU.add), [ps, carry], [carry])

    def attention(self, nheads, kq_fn, Vbuf, vap_fn, vdim, scale, use_bias, mask, finish_fn):
        nt, ntl, NT, kt0 = self.nt, self.ntl, self.NT, self.kt0
        nkt = kt0 + ntl
        ident = self.ident
        BK = self.BK
        for h in range(nheads):
            accs = [self.psA.next() for _ in range(ntl)]
            ops = kq_fn(h)

            def emit_s(kt):
                diag = kt >= kt0
                i = kt - kt0 if diag else 0
                nk = nt if diag else 128
                kc0 = kt * 128
                q0 = i * nt if diag else 0
                sp = self.psS.next()
                use_mask = diag and (mask is not None)
                for oi, (KTt, kap, QTt, qap) in enumerate(ops):
                    self.mm(sp, sp[:nk, q0:NT], KTt, kap(kc0, nk), QTt, qap(q0, NT), oi == 0, (oi == len(ops) - 1) and not use_mask)
                if use_mask:
                    self.mm(sp, sp[:nk, q0:q0 + nt], ident, ident[:, :nk], mask, mask[:, :nt], False, True)
                pt = self.palloc()
                if use_bias:
                    bap = BK[:nk, kt, h:h + 1]
                    self.A(lambda e: e.activation(out=pt[:nk, q0:NT], in_=sp[:nk, q0:NT], func=AF.Exp, scale=scale, bias=bap), [sp, BK], [pt])
                else:
                    self.A(lambda e: e.activation(out=pt[:nk, q0:NT], in_=sp[:nk, q0:NT], func=AF.Exp, scale=scale), [sp], [pt])
                return (kt, i if diag else 0, nk, pt)

            def emit_pv(item):
                kt, j0, nk, pt = item
                for j in range(j0, ntl):
                    self.mm(accs[j], accs[j][:nt, 0:vdim + 1], pt, pt[:nk, j * nt:(j + 1) * nt], Vbuf, vap_fn(kt, h, nk), kt == 0, kt == kt0 + j)
                self.pfree(pt)
                if kt >= kt0:
                    finish_fn(h, kt - kt0, accs[kt - kt0])

            items = []
            for kt in range(nkt):
                items.append(emit_s(kt))
                if len(items) > 3:
                    emit_pv(items.pop(0))
            while items:
                emit_pv(items.pop(0))

    def ot_from(self, Otok, b):
        nt, ntl = self.nt, self.ntl
        OT = self.OT
        for j in range(ntl):
            ob = Otok[j]
            self.to_fm(ob, 4, OT, lambda pb, j=j: self.cp(OT, OT[:, b * 4:(b + 1) * 4, j * nt:(j + 1) * nt], pb, pb[:, 0:4 * nt].rearrange("p (c n) -> p c n", c=4)), j)
            self.pfree(ob)

    def mix(self, l, s):
        nt, ntl, NT, kt0, pos0 = self.nt, self.ntl, self.NT, self.kt0, self.pos0
        tok0 = self.tok0
        d = self.d
        XT, QT, OT = self.XT, self.QT, self.OT
        self.make_XT()
        KTa, Va, KTb, Vb, KTc, Vc, KPT = self.KTa, self.Va, self.KTb, self.Vb, self.KTc, self.Vc, self.KPT

        QT4 = QT[:, :, :].rearrange("p (c two) n -> p c two n", two=2)

        def qt_store(pb, t):
            self.cp(QT, QT4[0:64, :, 0, t * nt:(t + 1) * nt], pb, pb[0:64, 0:4 * nt].rearrange("p (c n) -> p c n", c=4), "act")
            self.cp(QT, QT4[64:128, :, 1, t * nt:(t + 1) * nt], pb, pb[64:128, 0:4 * nt].rearrange("p (c n) -> p c n", c=4), "act")

        def c_q(t, ps):
            tb = self.tokb.next()
            self.cp(tb, tb[:nt, :512], ps, ps[:nt, :512])
            return lambda: self.to_fm(tb, 4, QT, lambda pb: qt_store(pb, t), t)

        def c_k(name, KT):
            def f(t, ps):
                sg = self.stg.next()
                self.cp(sg, sg[:nt, :512], ps, ps[:nt, :512], "act")
                if name == "diff_k":
                    self.rope(sg, sg[:nt, :512].rearrange("p (h d) -> p h d", h=8), 8, 8, self.ropeB, kt0 + t)
                self.out_dma(name, l, s, tok0 + t * nt, nt, sg, sg[:nt, :512])
                tb = self.tokb.next()
                self.cp(tb, tb[:nt, :512], sg, sg[:nt, :512], "dve")
                return lambda: self.to_fm(tb, 4, KT, lambda pb: self.cp(KT, KT[:, :, pos0 + t * nt:pos0 + (t + 1) * nt], pb, pb[:, 0:4 * nt].rearrange("p (c n) -> p c n", c=4)), t)
            return f

        def c_v(name, Vb_, H, D):
            def f(t, ps):
                sg = self.stg.next()
                self.cp(sg, sg[:nt, :512], ps, ps[:nt, :512], "act")
                self.out_dma(name, l, s, tok0 + t * nt, nt, sg, sg[:nt, :512])
                self.cp(Vb_, Vb_[:nt, kt0 + t, :, 0:D], sg, sg[:nt, :512].rearrange("p (h d) -> p h d", h=H), "dve")
            return f

        def c_fa(t, ps):
            e8 = self.e8.next()
            lf = self.lf8.next()
            bf_t = self.bf_t
            self.V(lambda e: e.tensor_tensor(out=e8[:nt, :], in0=ps[:nt, 0:8], in1=bf_t[:nt, :], op=ALU.add), [ps, bf_t], [e8])
            self.A(lambda e: e.activation(out=e8[:nt, :], in_=e8[:nt, :], func=AF.Exp, scale=-1.0), [e8], [e8])
            self.A(lambda e: e.activation(out=e8[:nt, :], in_=e8[:nt, :], func=AF.Ln, bias=1.0, scale=1.0), [e8], [e8])
            self.V(lambda e: e.tensor_scalar(lf[:nt, :], e8[:nt, :], -1.0, None, ALU.mult), [e8], [lf])
            self.out_dma("fox_logf", l, s, tok0 + t * nt, nt, lf, lf[:nt, :])
            return lambda: self.cumsum(kt0 + t, lf, lf[:nt, :], nt)

        carry, cref, FK, BK = self.carry, self.cref, self.FK, self.BK
        self.V(lambda e: e.tensor_copy(out=cref[:, :], in_=carry[:, :]), [carry], [cref])
        self.proj_tm(l, MIXOFF["fa"], 8, c_fa)
        for kt in range(kt0 + ntl):
            n = 128 if kt < kt0 else nt
            self.V(lambda e, kt=kt, n=n: e.tensor_tensor(out=BK[:n, kt, :], in0=cref[:n, :], in1=FK[:n, kt, :], op=ALU.subtract), [cref, FK], [BK])
        if self.stop_after == "fa":
            return
        self.proj_tm(l, MIXOFF["qa"], 512, c_q)
        if self.stop_after == "fq":
            return
        self.proj_tm(l, MIXOFF["ka"], 512, c_k("fox_k", KTa))
        if self.stop_after == "fk":
            return
        self.proj_tm(l, MIXOFF["va"], 512, c_v("fox_v", Va, 8, 64))

        if self.stop_after == "fproj":
            return
        Otok = [self.palloc() for _ in range(ntl)]

        def kq_a(h):
            c, b = h // 2, (h % 2) * 64
            return [(KTa, lambda kc0, nk: KTa[:, c, kc0:kc0 + nk], QT, lambda q0, q1: QT[:, h, q0:q1])]

        def fin_simple(h, j, acc):
            rc = self.rc.next()
            ot = Otok[j]
            self.V(lambda e: e.reciprocal(out=rc[:nt, :], in_=acc[:nt, 64:65]), [acc], [rc])
            self.V(lambda e: e.tensor_scalar(ot[:nt, h * 64:(h + 1) * 64], acc[:nt, 0:64], rc[:nt, 0:1], None, ALU.mult), [acc, rc], [ot])

        self.attention(8, kq_a, Va, lambda kt, h, nk: Va[:nk, kt, h, 0:65], 64, 0.125, True, self.maskc, fin_simple)
        self.ot_from(Otok, 0)
        if self.stop_after == "fox":
            return

        def c_qrope(t, ps):
            sg = self.stg.next()
            self.cp(sg, sg[:nt, :512], ps, ps[:nt, :512], "act")
            self.rope(sg, sg[:nt, :512].rearrange("p (h d) -> p h d", h=8), 8, 8, self.ropeB, kt0 + t)
            tb = self.tokb.next()
            self.cp(tb, tb[:nt, :512], sg, sg[:nt, :512], "dve")
            return lambda: self.to_fm(tb, 4, QT, lambda pb: qt_store(pb, t), t)

        self.proj_tm(l, MIXOFF["qb"], 512, c_qrope)
        if self.stop_after == "dq":
            return
        self.proj_tm(l, MIXOFF["kb"], 512, c_k("diff_k", KTb))
        self.proj_tm(l, MIXOFF["vb"], 512, c_v("diff_v", Vb, 4, 128))
        if self.stop_after == "dproj":
            return
        Otok = [self.palloc() for _ in range(ntl)]

        def kq_b(v):
            c, b = v // 2, (v % 2) * 64
            return [(KTb, lambda kc0, nk: KTb[:, c, kc0:kc0 + nk], QT, lambda q0, q1: QT[:, v, q0:q1])]

        neglam, dng_t = self.neglam, self.dng_t

        def fin_b(v, j, acc):
            hh, m = v // 2, v % 2
            rc = self.rc.next()
            self.V(lambda e: e.reciprocal(out=rc[:nt, :], in_=acc[:nt, 128:129]), [acc], [rc])
            t1 = self.t1[j]
            if m == 0:
                self.V(lambda e: e.tensor_scalar(t1[:nt, :], acc[:nt, 0:128], rc[:nt, 0:1], None, ALU.mult), [acc, rc], [t1])
                return
            obf = self.obf.next()
            ss = self.ss.next()
            junk = self.junk
            ot = Otok[j]
            self.V(lambda e: e.tensor_tensor(out=rc[:nt, :], in0=rc[:nt, :], in1=neglam[:nt, :], op=ALU.mult), [rc, neglam], [rc])
            self.V(lambda e: e.scalar_tensor_tensor(out=obf[:nt, :], in0=acc[:nt, 0:128], scalar=rc[:nt, 0:1], in1=t1[:nt, :], op0=ALU.mult, op1=ALU.add), [acc, rc, t1], [obf])
            self.V(lambda e: e.tensor_tensor(out=junk[:nt, 0:128], in0=obf[:nt, :], in1=obf[:nt, :], op=ALU.mult), [obf], [junk])
            self.V(lambda e: e.reduce_sum(out=ss[:nt, 0:1], in_=junk[:nt, 0:128], axis=mybir.AxisListType.X), [junk], [ss])
            self.A(lambda e: e.activation(out=ss[:nt, :], in_=ss[:nt, :], func=AF.Sqrt, bias=RMS_EPS, scale=1.0 / 128), [ss], [ss])
            self.V(lambda e: e.reciprocal(out=ss[:nt, :], in_=ss[:nt, :]), [ss], [ss])
            self.V(lambda e: e.scalar_tensor_tensor(out=ot[:nt, hh * 128:(hh + 1) * 128], in0=obf[:nt, :], scalar=ss[:nt, 0:1], in1=dng_t[:nt, :], op0=ALU.mult, op1=ALU.mult), [obf, ss, dng_t], [ot])

        self.attention(8, kq_b, Vb, lambda kt, v, nk: Vb[:nk, kt, v // 2, 0:129], 128, 0.125, False, self.maskd if self.kind == "p" else None, fin_b)
        self.ot_from(Otok, 1)
        if self.stop_after == "diff":
            return

        dqnT = [self.palloc() for _ in range(3)]
        gq_t, gkv_t = self.gq_t, self.gkv_t

        def rms_to(ps, n, gt, oT, oap):
            ss = self.ss.next()
            junk = self.junk
            self.A(lambda e: e.activation(out=junk[:nt, 0:n], in_=ps[:nt, 0:n], func=AF.Copy), [ps], [junk])
            self.V(lambda e: e.tensor_tensor(out=junk[:nt, 0:n], in0=junk[:nt, 0:n], in1=junk[:nt, 0:n], op=ALU.mult), [junk], [junk])
            self.V(lambda e: e.reduce_sum(out=ss[:nt, 0:1], in_=junk[:nt, 0:n], axis=mybir.AxisListType.X), [junk], [ss])
            self.A(lambda e: e.activation(out=ss[:nt, :], in_=ss[:nt, :], func=AF.Sqrt, bias=RMS_EPS, scale=1.0 / n), [ss], [ss])
            self.V(lambda e: e.reciprocal(out=ss[:nt, :], in_=ss[:nt, :]), [ss], [ss])
            self.V(lambda e: e.scalar_tensor_tensor(out=oap, in0=ps[:nt, 0:n], scalar=ss[:nt, 0:1], in1=gt[:nt, 0:n], op0=ALU.mult, op1=ALU.mult), [ps, ss, gt], [oT])

        def c_dq(t, ps):
            tb = self.tokb.next()
            rms_to(ps, 384, gq_t, tb, tb[:nt, 0:384])

            def dst(pb):
                for c in range(3):
                    self.cp(dqnT[c], dqnT[c][:, t * nt:(t + 1) * nt], pb, pb[:, c * nt:(c + 1) * nt])
            return lambda: self.to_fm(tb, 3, None, dst, t)

        self.proj_tm(l, MIXOFF["dq"], 384, c_dq)
        if self.stop_after == "cdq":
            return
        QPT = [self.palloc() for _ in range(4)]
        slot, wv = self.wget(self.wb["wuq"][l].rearrange("(c p) n -> p c n", p=128), [3, 768], self.wT[("wuq", l)])
        for t in range(ntl):
            ps1 = self.psA.next()
            for c in range(3):
                self.mm(ps1, ps1[:nt, :512], dqnT[c], dqnT[c][:, t * nt:(t + 1) * nt], slot, wv[:, c, 0:512], c == 0, c == 2)
            lat = c_q(t, ps1)
            ps2 = self.psA.next()
            for c in range(3):
                self.mm(ps2, ps2[:nt, :256], dqnT[c], dqnT[c][:, t * nt:(t + 1) * nt], slot, wv[:, c, 512:768], c == 0, c == 2)
            sg = self.stg.next()
            self.cp(sg, sg[:nt, :256], ps2, ps2[:nt, :256], "act")
            self.rope(sg, sg[:nt, :256].rearrange("p (h d) -> p h d", h=8), 8, 16, self.ropeC, kt0 + t)
            tb = self.tokb.next()
            self.V(lambda e, tb=tb: e.memset(tb[:nt, 0:512], 0.0), [], [tb])
            self.cp(tb, tb[:nt, 0:512].rearrange("p (h d) -> p h d", h=8)[:, :, 0:32], sg, sg[:nt, :256].rearrange("p (h d) -> p h d", h=8), "dve")

            def dst(pb, t=t):
                for g in range(4):
                    self.cp(QPT[g], QPT[g][:, t * nt:(t + 1) * nt], pb, pb[:, g * nt:(g + 1) * nt])
            lat()
            self.to_fm(tb, 4, None, dst, t)
        for c in range(3):
            self.pfree(dqnT[c])
        if self.stop_after == "cq":
            return
        ckvT = [self.palloc() for _ in range(2)]

        def c_kv(t, ps):
            kp = self.kp32.next()
            self.cp(kp, kp[:nt, :], ps, ps[:nt, 256:288], "act")
            sg = self.stg.next()
            rms_to(ps, 256, gkv_t, sg, sg[:nt, 0:256])
            self.out_dma("mla_ckv", l, s, tok0 + t * nt, nt, sg, sg[:nt, 0:256])
            tb = self.tokb.next()
            self.cp(tb, tb[:nt, :256], sg, sg[:nt, :256], "dve")
            self.rope(kp, kp[:nt, :].rearrange("p (h d) -> p h d", h=1), 1, 16, self.ropeC, kt0 + t)
            self.out_dma("mla_kpe", l, s, tok0 + t * nt, nt, kp, kp[:nt, :])
            self.V(lambda e, tb=tb: e.memset(tb[:nt, 256:384], 0.0), [], [tb])
            for r in range(2):
                self.cp(tb, tb[:nt, 256 + r * 64:256 + r * 64 + 32], kp, kp[:nt, :], "dve")

            def dst(pb):
                for c in range(2):
                    self.cp(ckvT[c], ckvT[c][:, t * nt:(t + 1) * nt], pb, pb[:, c * nt:(c + 1) * nt])
                self.cp(KPT, KPT[:, pos0 + t * nt:pos0 + (t + 1) * nt], pb, pb[:, 2 * nt:3 * nt])
            return lambda: self.to_fm(tb, 3, None, dst, t)

        self.proj_tm(l, MIXOFF["dkv"], 288, c_kv)
        self.mla_kv(l, ckvT, [(0, NT)], pos0, kt0, nt, ntl)
        for c in range(2):
            self.pfree(ckvT[c])
        if self.stop_after == "ckv":
            return
        Otok = [self.palloc() for _ in range(ntl)]

        def kq_c(h):
            c, b = h // 2, (h % 2) * 64
            return [(KTc, lambda kc0, nk: KTc[:, c, kc0:kc0 + nk], QT, lambda q0, q1: QT[:, h, q0:q1]),
                    (KPT, lambda kc0, nk: KPT[b:b + 64, kc0:kc0 + nk], QPT[c], lambda q0, q1: QPT[c][b:b + 64, q0:q1])]

        self.attention(8, kq_c, Vc, lambda kt, h, nk: Vc[:nk, kt, h, 0:65], 64, 96.0 ** -0.5, False, self.maskd if self.kind == "p" else None, fin_simple)
        for g in range(4):
            self.pfree(QPT[g])
        self.ot_from(Otok, 2)
        if self.stop_after == "mla":
            for cc in range(12):
                self.P.dma("pool", lambda e, cc=cc: e.dma_start(out=d["dbg_ot"][:, cc, :], in_=OT[:, cc, :]), reads=[OT])
            return

        MG = [self.palloc() for _ in range(8)]
        bg_t = self.bg_t
        for b in range(3):
            for half in range(2):
                slot_b, wb = self.wget(self.wb["wbr"][l, b].rearrange("(c p) n -> p c n", p=128)[:, :, half * 512:(half + 1) * 512], [4, 512], self.wT[("wbr", l)])
                slot_g, wgv = self.wget(self.wb["wg"][l].rearrange("(c p) n -> p c n", p=128)[:, :, b * 1024 + half * 512:b * 1024 + (half + 1) * 512], [8, 512], self.wT[("wg", l)])
                for o4 in range(4):
                    oc = half * 4 + o4
                    pgt = self.psA.next()
                    for c in range(8):
                        self.mm(pgt, pgt[:, :NT], slot_g, wgv[:, c, o4 * 128:(o4 + 1) * 128], XT, XT[:, c, :NT], c == 0, c == 7)
                    gsb = self.ftr.next()
                    bcol = b * 8 + oc
                    self.A(lambda e, gsb=gsb, pgt=pgt, bcol=bcol: e.activation(out=gsb[:, :NT], in_=pgt[:, :NT], func=AF.Sigmoid, bias=bg_t[:, bcol:bcol + 1], scale=1.0), [pgt, bg_t], [gsb])
                    pbr = self.psA.next()
                    for kc in range(4):
                        self.mm(pbr, pbr[:, :NT], slot_b, wb[:, kc, o4 * 128:(o4 + 1) * 128], OT, OT[:, b * 4 + kc, :NT], kc == 0, kc == 3)
                    mg = MG[oc]
                    if b == 0:
                        self.V(lambda e, mg=mg, gsb=gsb, pbr=pbr: e.tensor_tensor(out=mg[:, :NT], in0=gsb[:, :NT], in1=pbr[:, :NT], op=ALU.mult), [gsb, pbr], [mg])
                    else:
                        self.V(lambda e, gsb=gsb, pbr=pbr: e.tensor_tensor(out=gsb[:, :NT], in0=gsb[:, :NT], in1=pbr[:, :NT], op=ALU.mult), [gsb, pbr], [gsb])
                        self.V(lambda e, mg=mg, gsb=gsb: e.tensor_tensor(out=mg[:, :NT], in0=mg[:, :NT], in1=gsb[:, :NT], op=ALU.add), [mg, gsb], [mg])
        for nh in range(2):
            slot, wv = self.wget(self.wb["wo"][l].rearrange("(c p) n -> p c n", p=128)[:, :, nh * 512:(nh + 1) * 512], [8, 512], self.wT[("wo", l)])
            for t in range(ntl):
                ps = self.psA.next()
                for kc in range(8):
                    self.mm(ps, ps[:nt, :], MG[kc], MG[kc][:, t * nt:(t + 1) * nt], slot, wv[:, kc, :], kc == 0, kc == 7)
                x = self.x[t]
                xa = x[:nt, nh * 512:(nh + 1) * 512]
                self.V(lambda e, xa=xa, ps=ps: e.scalar_tensor_tensor(out=xa, in0=xa, scalar=ALPHA, in1=ps[:nt, :], op0=ALU.mult, op1=ALU.add), [x, ps], [x])
        for m in MG:
            self.pfree(m)
        self.layernorm(l, 1)

    def mla_kv(self, l, ckvT, ranges, pos0, kt0, nt, ntl):
        d = self.d
        KTc, Vc = self.KTc, self.Vc
        slot, wv = self.wget(self.wb["wuk"][l].rearrange("(c p) n -> p c n", p=128), [2, 512], self.wT[("wuk", l)])
        for (c0, n) in ranges:
            for c in range(4):
                ps = self.psA.next()
                for lc in range(2):
                    self.mm(ps, ps[:, :n], slot, wv[:, lc, c * 128:(c + 1) * 128], ckvT[lc], ckvT[lc][:, c0:c0 + n], lc == 0, lc == 1)
                self.cp(KTc, KTc[:, c, pos0 + c0:pos0 + c0 + n], ps, ps[:, :n])
        slot, wv = self.wget(self.wb["wuv"][l].rearrange("(c p) n -> p c n", p=128), [2, 512], self.wT[("wuv", l)])
        for t in range(ntl):
            ps = self.psA.next()
            for lc in range(2):
                self.mm(ps, ps[:nt, :], ckvT[lc], ckvT[lc][:, t * nt:(t + 1) * nt], slot, wv[:, lc, :], lc == 0, lc == 1)
            tb = self.tokb.next()
            self.cp(tb, tb[:nt, 0:512], ps, ps[:nt, :], "act")
            self.cp(Vc, Vc[:nt, kt0 + t, :, 0:64], tb, tb[:nt, 0:512].rearrange("p (h d) -> p h d", h=8), "dve")

    def ple(self, l, s):
        nt, ntl, NT, pos0 = self.nt, self.ntl, self.NT, self.pos0
        d = self.d
        XT = self.XT
        self.make_XT()
        pt = [self.palloc() for _ in range(2)]
        src = d["ppT" if self.kind == "p" else "psT"]
        for kc in range(2):
            sap = src[l, s, kc, :, self.tok0:self.tok0 + NT]
            self.P.dma("pool", lambda e, kc=kc, sap=sap: e.dma_start(out=pt[kc][:, :NT], in_=sap), writes=[pt[kc]])
        onesb, bhi = self.onesb, self.bhi
        for nh in range(2):
            slot, wv = self.wget(self.wb["wpg"][l].rearrange("(c p) n -> p c n", p=128)[:, :, nh * 512:(nh + 1) * 512], [8, 512], self.wT[("wpg", l)])
            slot2, wp = self.wget(self.wb["wpp"][l].rearrange("(c p) n -> p c n", p=128)[:, :, nh * 512:(nh + 1) * 512], [2, 512], self.wT[("wpp", l)])
            for t in range(ntl):
                pg = self.psA.next()
                for c in range(8):
                    self.mm(pg, pg[:nt, :], XT, XT[:, c, t * nt:(t + 1) * nt], slot, wv[:, c, :], c == 0, False)
                self.mm(pg, pg[:nt, :], onesb, onesb[0:1, :nt], bhi, bhi[0:1, nh * 512:(nh + 1) * 512], False, True)
                gs = self.ftr.next()
                self.A(lambda e, gs=gs, pg=pg: e.activation(out=gs[:nt, :], in_=pg[:nt, :], func=AF.Sigmoid), [pg], [gs])
                pp = self.psA.next()
                for kc in range(2):
                    self.mm(pp, pp[:nt, :], pt[kc], pt[kc][:, t * nt:(t + 1) * nt], slot2, wp[:, kc, :], kc == 0, kc == 1)
                self.V(lambda e, gs=gs, pp=pp: e.tensor_tensor(out=gs[:nt, :], in0=gs[:nt, :], in1=pp[:nt, :], op=ALU.mult), [gs, pp], [gs])
                x = self.x[t]
                xa = x[:nt, nh * 512:(nh + 1) * 512]
                self.V(lambda e, xa=xa, gs=gs: e.scalar_tensor_tensor(out=xa, in0=xa, scalar=ALPHA, in1=gs[:nt, :], op0=ALU.mult, op1=ALU.add), [x, gs], [x])
        for p_ in pt:
            self.pfree(p_)
        self.layernorm(l, 3)

    def block(self, l, kind, s, B):
        self.kind = kind
        if kind == "p":
            self.nt, self.ntl = 128, 4
            self.kt0 = 4 * B
            self.tok0 = 512 * B
            self.pos0 = 512 * B
        else:
            self.nt, self.ntl = 64, 1
            self.kt0 = 8
            self.tok0 = 0
            self.pos0 = 1024
        self.NT = self.nt * self.ntl
        nt, ntl, tok0 = self.nt, self.ntl, self.tok0
        d, P = self.d, self.P
        xmid = self.xmid_p if kind == "p" else self.xmid_s
        xin = d["xp" if kind == "p" else "xs"]
        for t in range(ntl):
            x = self.x[t]
            if l == 0:
                sap = xin[s, tok0 + t * nt:tok0 + (t + 1) * nt, :]
                P.dma("sp", lambda e, x=x, sap=sap: e.dma_start(out=x[:nt, :], in_=sap), writes=[x])
            else:
                sap = xmid[s, tok0 + t * nt:tok0 + (t + 1) * nt, :]
                P.dma("sp", lambda e, x=x, sap=sap: e.dma_start(out=x[:nt, :], in_=sap), writes=[x], extra=list(self.xmid_ev.values()))
        sa = self.stop_after
        self.ffn(l, 1)
        if sa != "ffn1":
            self.mix(l, s)
            if sa is None or sa == "ffn2":
                self.ffn(l, 2)
                if sa is None:
                    self.ple(l, s)
        last = (l == self.L - 1)
        for t in range(ntl):
            x = self.x[t]
            if last:
                self.out_dma("y", None, s, tok0 + t * nt, nt, x, x[:nt, :])
            else:
                dap = xmid[s, tok0 + t * nt:tok0 + (t + 1) * nt, :]
                ev = P.dma("pool", lambda e, x=x, dap=dap: e.dma_start(out=dap, in_=x[:nt, :]), reads=[x])
                self.xmid_ev[id(ev[1])] = ev

    def sample_prep(self, l, s):
        d, P = self.d, self.P
        KTa, Va, KTb, Vb, KPT, lfc = self.KTa, self.Va, self.KTb, self.Vb, self.KPT, self.lfc
        P.dma("pool", lambda e: e.dma_start(out=KTa[:, :, 0:1024], in_=d["cka"][l, s].rearrange("c p n -> p c n")), writes=[KTa])
        P.dma("pool", lambda e: e.dma_start(out=KTb[:, :, 0:1024], in_=d["ckb"][l, s].rearrange("c p n -> p c n")), writes=[KTb])
        P.dma("pool", lambda e: e.dma_start(out=KPT[:, 0:1024], in_=d["ckpT"][l, s]), writes=[KPT])
        for kt in range(8):
            P.dma("pool", lambda e, kt=kt: e.dma_start(out=Va[:, kt, :, 0:64], in_=d["cva"][l, s, 128 * kt:128 * (kt + 1), :].rearrange("p (h d) -> p h d", h=8)), writes=[Va])
            P.dma("pool", lambda e, kt=kt: e.dma_start(out=Vb[:, kt, :, 0:128], in_=d["cvb"][l, s, 128 * kt:128 * (kt + 1), :].rearrange("p (h d) -> p h d", h=4)), writes=[Vb])
        P.dma("sp", lambda e: e.dma_start(out=lfc[:, :, :], in_=d["clf"][l, s].rearrange("(t p) h -> p t h", p=128)), writes=[lfc])
        ck = [[self.palloc() for _ in range(2)] for _ in range(2)]
        for lc in range(2):
            for hf in range(2):
                P.dma("pool", lambda e, lc=lc, hf=hf: e.dma_start(out=ck[lc][hf][:, :], in_=d["cckT"][l, s, lc, :, hf * 512:(hf + 1) * 512]), writes=[ck[lc][hf]])
        for hf in range(2):
            self.mla_kv(l, [ck[0][hf], ck[1][hf]], [(0, 512)], hf * 512, hf * 4, 128, 4)
        for lc in range(2):
            for hf in range(2):
                self.pfree(ck[lc][hf])
        for kt in range(8):
            self.cumsum(kt, lfc, lfc[:, kt, :], 128)

    def build(self):
        self.setup()
        NP, NS = self.NP, self.NS
        carry = self.carry
        for l in range(self.L):
            self.layer_params(l)
            for s in range(NP):
                self.V(lambda e: e.memset(carry[:, :], 0.0), [], [carry])
                for B in range(self.nblk):
                    self.block(l, "p", s, B)
            for s in range(NS):
                self.V(lambda e: e.memset(carry[:, :], 0.0), [], [carry])
                self.kind = "s"
                self.sample_prep(l, s)
                self.block(l, "s", s, 0)
        self.P.wait_all_dma("sp")
        self.P.finish()
        return self.nc


def build_program(NP=4, NS=2, L=2, nblk=4, stop_after=None, WSLOTS=2):
    kb = KB(NP, NS, L, WSLOTS=WSLOTS, stop_after=stop_after)
    kb.nblk = nblk
    nc = kb.build()
    n = {e: len(kb.P.q[e]) for e in ENGINES}
    print("instr counts:", n, "dma sems:", len(kb.P.owners))
    return nc


def _rope_tab(half, d, theta):
    inv = np.exp(np.float32(-math.log(theta)) * np.arange(half, dtype=np.float32) * np.float32(2.0 / d)).astype(np.float32)
    pos = np.arange(2048, dtype=np.float32)
    ang = (pos[:, None] * inv[None, :]).astype(np.float32)
    tab = np.stack([np.cos(ang), np.sin(ang)], axis=1).astype(np.float32)
    return np.ascontiguousarray(tab.reshape(16, 128, 2, half).transpose(1, 0, 2, 3))


def shared_inputs(inp):
    f = np.float32
    A = lambda a: np.ascontiguousarray(a, dtype=f)
    w = inp["w_in_mix"]
    sp = np.cumsum([0, 512, 512, 512, 8, 512, 512, 512, 384, 256, 32])
    seg = {n: w[:, :, sp[i]:sp[i + 1]] for i, n in enumerate(["qa", "ka", "va", "fa", "qb", "kb", "vb", "dq", "dkv", "kr"])}
    wmix = np.concatenate([seg[n] for n in ["qa", "ka", "va", "qb", "kb", "vb", "dq", "dkv", "kr", "fa"]], axis=2)
    wuq = inp["mla_w_uq"].reshape(2, 384, 8, 96)
    wuq = np.concatenate([wuq[..., :64].reshape(2, 384, 512), wuq[..., 64:].reshape(2, 384, 256)], axis=2)
    wukv = inp["mla_w_ukv"]
    k = np.arange(128)
    sh = {
        "w1i": A(inp["ffn1_w_in"]), "w1o": A(inp["ffn1_w_out"]), "w2i": A(inp["ffn2_w_in"]), "w2o": A(inp["ffn2_w_out"]),
        "lng": A(inp["ln_g"]), "lnb": A(inp["ln_b"]), "wmix": A(wmix), "bfg": A(inp["b_forget"]),
        "dlam": A(inp["diff_lambda"].reshape(2, 256)), "dng": A(inp["diff_norm_g"]), "gq": A(inp["mla_q_norm_g"]),
        "wuq": A(wuq), "gkv": A(inp["mla_kv_norm_g"]),
        "wuk": A(wukv[..., :64].reshape(2, 256, 512)), "wuv": A(wukv[..., 64:].reshape(2, 256, 512)),
        "wbr": A(inp["w_branch"]), "wg": A(inp["w_gate"]), "bg": A(inp["b_gate"].reshape(2, 24, 128).transpose(0, 2, 1)),
        "wo": A(inp["w_out"]), "wpg": A(inp["ple_w_gate"]), "bpg": A(inp["ple_b_gate"]), "wpp": A(inp["ple_w_proj"]),
        "c_ident": np.eye(128, dtype=f),
        "c_maskc": np.where(k[:, None] <= k[None, :], 0.0, NEG).astype(f),
        "c_maskd": np.where((k[:, None] // 64) <= (k[None, :] // 64), 0.0, NEG).astype(f),
        "c_tri": (k[:, None] <= k[None, :]).astype(f),
        "c_ropeB": _rope_tab(8, 16, 500000.0), "c_ropeC": _rope_tab(16, 32, 10000.0),
    }
    return sh


def core_inputs(inp, sh, pseqs, sseqs):
    f = np.float32
    A = lambda a: np.ascontiguousarray(a, dtype=f)
    m = dict(sh)
    m["xp"] = A(inp["x_prompt"][pseqs])
    pp = inp["p_prompt"][:, pseqs]
    m["ppT"] = A(pp.transpose(0, 1, 3, 2).reshape(2, len(pseqs), 2, 128, 2048))
    if len(sseqs):
        ns = len(sseqs)
        m["xs"] = A(inp["x_sample"][sseqs])
        m["psT"] = A(inp["p_sample"][:, sseqs].transpose(0, 1, 3, 2).reshape(2, ns, 2, 128, 64))
        m["cka"] = A(inp["cache_fox_k"][:, sseqs].reshape(2, ns, 1024, 4, 128).transpose(0, 1, 3, 4, 2))
        m["cva"] = A(inp["cache_fox_v"][:, sseqs].reshape(2, ns, 1024, 512))
        m["clf"] = A(inp["cache_fox_logf"][:, sseqs])
        m["ckb"] = A(inp["cache_diff_k"][:, sseqs].reshape(2, ns, 1024, 4, 128).transpose(0, 1, 3, 4, 2))
        m["cvb"] = A(inp["cache_diff_v"][:, sseqs].reshape(2, ns, 1024, 512))
        m["cckT"] = A(inp["cache_mla_ckv"][:, sseqs].reshape(2, ns, 1024, 2, 128).transpose(0, 1, 3, 4, 2))
        kp = inp["cache_mla_kpe"][:, sseqs].transpose(0, 1, 3, 2)
        z = np.zeros_like(kp)
        m["ckpT"] = A(np.concatenate([kp, z, kp, z], axis=2))
    return m


_NC_CACHE = {}


def kernel(**inputs):
    inp = {k: np.asarray(v) for k, v in inputs.items()}
    ncores = 8
    NP, NS = 4, 2
    if "full" not in _NC_CACHE:
        _NC_CACHE["full"] = build_program(NP, NS, 2)
    nc = _NC_CACHE["full"]
    sh = shared_inputs(inp)
    in_maps = []
    for c in range(ncores):
        in_maps.append(core_inputs(inp, sh, list(range(NP * c, NP * (c + 1))), list(range(NS * c, NS * (c + 1)))))
    res = run_bass_kernel_spmd(nc, in_maps, core_ids=list(range(ncores)))
    R = res.results

    def cat(name, axis):
        return np.concatenate([np.asarray(r[name]) for r in R], axis=axis)

    y_p = cat("y_p", 0)
    y_s = cat("y_s", 0)
    outs = [y_p, y_s]
    shp = {"fox_k": (8, 64), "fox_v": (8, 64), "fox_logf": (8,), "diff_k": (4, 2, 64), "diff_v": (4, 128), "mla_ckv": (256,), "mla_kpe": (32,)}
    for nm in ["fox_k", "fox_v", "fox_logf", "diff_k", "diff_v", "mla_ckv", "mla_kpe"]:
        a = cat(nm + "_p", 1)
        b = cat(nm + "_s", 1)
        outs.append(a.reshape(2, 32, 2048, *shp[nm]))
        outs.append(b.reshape(2, 16, 64, *shp[nm]))
    return tuple(np.ascontiguousarray(o, dtype=np.float32) for o in outs)
```

```python
import math
from collections import deque
import numpy as np
import concourse.bass as bass
import concourse.mybir as mybir
from concourse.bass_utils import run_bass_kernel_spmd

F32 = mybir.dt.float32
BF16 = mybir.dt.bfloat16
AF = mybir.ActivationFunctionType
ALU = mybir.AluOpType

ENGINES = ["pe", "act", "dve", "pool", "sp"]
ALPHA = 4.0 ** 0.25
LN_EPS = 1e-5
RMS_EPS = 1e-6
NEG = -30000.0


class T:
    __slots__ = ("h", "name", "w", "r", "sem", "cnt", "excl", "base")

    def __init__(self, h, name=""):
        self.base = self
        self.excl = False
        self.h = h
        self.name = name
        self.w = None
        self.r = {}
        self.sem = None
        self.cnt = 0

    def __getitem__(self, k):
        return self.h[k]

    def view(self, h):
        v = T(h, self.name + "_v")
        v.base = self
        return v


class Rec:
    __slots__ = ("fn", "waits", "inc", "dma")

    def __init__(self, fn):
        self.fn = fn
        self.waits = []
        self.inc = False
        self.dma = None


class Prog:
    def __init__(self, nc):
        self.nc = nc
        self.q = {e: [] for e in ENGINES}
        self.seen = {e: {} for e in ENGINES}
        self.esem = {}
        self._ctx = []
        self.owners = []
        for e in ENGINES:
            self.esem[e] = self._sem("s_" + e)

    def _sem(self, name):
        cm = self.nc.semaphore(name)
        s = cm.__enter__()
        self._ctx.append(cm)
        return s

    def sbuf(self, name, shape, dt):
        cm = self.nc.sbuf_tensor(name, list(shape), dt)
        h = cm.__enter__()
        self._ctx.append(cm)
        return T(h, name)

    def psum(self, name, shape, dt):
        cm = self.nc.psum_tensor(name, list(shape), dt)
        h = cm.__enter__()
        self._ctx.append(cm)
        t = T(h, name)
        t.excl = True
        return t

    def _need(self, eng, rec, ev):
        if ev is None:
            return
        if ev[0] == "eng":
            _, e2, idx = ev
            if e2 == eng and eng in ("pe", "sp"):
                return
            key = ("eng", e2)
            if self.seen[eng].get(key, -1) >= idx:
                return
            self.seen[eng][key] = idx
            self.q[e2][idx].inc = True
            rec.waits.append(ev)
        else:
            _, sem, val = ev
            key = ("dma", id(sem))
            if self.seen[eng].get(key, -1) >= val:
                return
            self.seen[eng][key] = val
            rec.waits.append(ev)

    def _deps(self, eng, rec, reads, writes):
        for t in reads:
            self._need(eng, rec, t.w)
            if t.excl:
                for k, ev in t.r.items():
                    if k != eng:
                        self._need(eng, rec, ev)
        for t in writes:
            self._need(eng, rec, t.w)
            for ev in t.r.values():
                self._need(eng, rec, ev)

    def op(self, eng, fn, reads=(), writes=()):
        def flat(ts):
            out = []
            for t in ts:
                if isinstance(t, (list, tuple)):
                    out.extend(x.base for x in t)
                else:
                    out.append(t.base)
            return out
        reads = flat(reads)
        writes = flat(writes)
        rec = Rec(fn)
        self._deps(eng, rec, reads, writes)
        idx = len(self.q[eng])
        self.q[eng].append(rec)
        ev = ("eng", eng, idx)
        for t in reads:
            t.r[eng] = ev
        for t in writes:
            t.w = ev
            t.r = {}
        return ev

    def dma(self, queue, fn, reads=(), writes=(), extra=()):
        reads = [t.base for t in reads]
        writes = [t.base for t in writes]
        rec = Rec(fn)
        for ev in extra:
            self._need(queue, rec, ev)
        owner = (list(writes) + list(reads))[0]
        if owner.sem is None:
            owner.sem = {}
            owner.cnt = {}
        if queue not in owner.sem:
            owner.sem[queue] = self._sem("d%s_%s" % (queue[0], owner.name))
            owner.cnt[queue] = 0
            self.owners.append((owner, queue))
        sem = owner.sem[queue]
        for t in reads:
            self._need(queue, rec, t.w)
        for t in writes:
            if not (t.w is not None and t.w[0] == "dma" and t.w[1] is sem):
                self._need(queue, rec, t.w)
            for ev in t.r.values():
                self._need(queue, rec, ev)
        owner.cnt[queue] += 16
        ev = ("dma", sem, owner.cnt[queue])
        rec.dma = (sem, 16)
        self.q[queue].append(rec)
        for t in reads:
            t.r["dma%d%s" % (id(owner), queue)] = ev
        for t in writes:
            t.w = ev
            t.r = {}
        return ev

    def wait_all_dma(self, eng):
        rec = Rec(None)
        for o, qn in self.owners:
            self._need(eng, rec, ("dma", o.sem[qn], o.cnt[qn]))
        self.q[eng].append(rec)

    def finish(self):
        nc = self.nc
        pref = {}
        for e in ENGINES:
            c = 0
            arr = []
            for rec in self.q[e]:
                if rec.inc:
                    c += 1
                arr.append(c)
            pref[e] = arr
        esem = self.esem
        q = self.q

        def run(e, engobj):
            for rec in q[e]:
                for ev in rec.waits:
                    if ev[0] == "eng":
                        engobj.wait_ge(esem[ev[1]], pref[ev[1]][ev[2]])
                    else:
                        engobj.wait_ge(ev[1], ev[2])
                if rec.fn is None:
                    continue
                ins = rec.fn(engobj)
                if rec.dma is not None:
                    ins.then_inc(rec.dma[0], rec.dma[1])
                if rec.inc:
                    ins.then_inc(esem[e], 1)

        with nc.Block() as block:
            @block.tensor
            def _(eng):
                run("pe", eng)

            @block.scalar
            def _(eng):
                run("act", eng)

            @block.vector
            def _(eng):
                run("dve", eng)

            @block.gpsimd
            def _(eng):
                run("pool", eng)

            @block.sync
            def _(eng):
                run("sp", eng)
        for cm in reversed(self._ctx):
            cm.__exit__(None, None, None)
        self._ctx = []


class Ring:
    def __init__(self, tiles):
        self.t = tiles
        self.i = 0

    def next(self):
        t = self.t[self.i % len(self.t)]
        self.i += 1
        return t


OUT_SPECS = [
    ("y", 1024, False), ("fox_k", 512, True), ("fox_v", 512, True), ("fox_logf", 8, True),
    ("diff_k", 512, True), ("diff_v", 512, True), ("mla_ckv", 256, True), ("mla_kpe", 32, True)]

MIXOFF = dict(qa=0, ka=512, va=1024, qb=1536, kb=2048, vb=2560, dq=3072, dkv=3456, kr=3712, fa=3744)


class KB:
    def __init__(self, NP, NS, L=2, WSLOTS=3, stop_after=None):
        self.NP, self.NS, self.L = NP, NS, L
        self.stop_after = stop_after
        nc = bass.Bass("TRN2", target_bir_lowering=False)
        self.nc = nc
        P = Prog(nc)
        self.P = P
        d = {}
        self.d = d

        def din(name, shape):
            d[name] = nc.dram_tensor(name, list(shape), F32, kind="ExternalInput").ap()

        def dout(name, shape):
            d[name] = nc.dram_tensor(name, list(shape), F32, kind="ExternalOutput").ap()

        din("xp", [NP, 2048, 1024])
        din("ppT", [2, NP, 2, 128, 2048])
        if NS:
            din("xs", [NS, 64, 1024])
            din("psT", [2, NS, 2, 128, 64])
            din("cka", [2, NS, 4, 128, 1024])
            din("cva", [2, NS, 1024, 512])
            din("clf", [2, NS, 1024, 8])
            din("ckb", [2, NS, 4, 128, 1024])
            din("cvb", [2, NS, 1024, 512])
            din("cckT", [2, NS, 2, 128, 1024])
            din("ckpT", [2, NS, 128, 1024])
        din("w1i", [2, 1024, 5632]); din("w1o", [2, 2816, 1024])
        din("w2i", [2, 1024, 5632]); din("w2o", [2, 2816, 1024])
        din("lng", [2, 4, 1024]); din("lnb", [2, 4, 1024])
        din("wmix", [2, 1024, 3752]); din("bfg", [2, 8]); din("dlam", [2, 256]); din("dng", [2, 128])
        din("gq", [2, 384]); din("wuq", [2, 384, 768]); din("gkv", [2, 256])
        din("wuk", [2, 256, 512]); din("wuv", [2, 256, 512])
        din("wbr", [2, 3, 512, 1024]); din("wg", [2, 1024, 3072]); din("bg", [2, 128, 24])
        din("wo", [2, 1024, 1024]); din("wpg", [2, 1024, 1024]); din("bpg", [2, 1024]); din("wpp", [2, 256, 1024])
        din("c_ident", [128, 128]); din("c_maskc", [128, 128]); din("c_maskd", [128, 128]); din("c_tri", [128, 128])
        din("c_ropeB", [128, 16, 2, 8]); din("c_ropeC", [128, 16, 2, 16])
        for nm, wd, hasl in OUT_SPECS:
            dout(nm + "_p", ([2] if hasl else []) + [NP, 2048, wd])
            if NS:
                dout(nm + "_s", ([2] if hasl else []) + [NS, 64, wd])
        if stop_after == "mla":
            dout("dbg_ot", [128, 12, 512])
        self.WNAMES = ["w1i", "w1o", "wmix", "wuq", "wuk", "wuv", "wbr", "wg", "wo", "w2i", "w2o", "wpg", "wpp"]
        self.wb = {}
        self.wT = {}
        for nm in self.WNAMES:
            self.wb[nm] = nc.dram_tensor(nm + "_bf", list(d[nm].shape), BF16).ap()
            for l in range(2):
                self.wT[(nm, l)] = T(None, "c_%s%d" % (nm, l))
        self.xmid_p = nc.dram_tensor("xmid_p", [NP, 2048, 1024], F32).ap()
        self.xmid_ev = {}
        if NS:
            self.xmid_s = nc.dram_tensor("xmid_s", [NS, 64, 1024], F32).ap()

        sb = P.sbuf
        self.x = [sb("x%d" % i, [128, 1024], F32) for i in range(4)]
        self.XT = sb("XT", [128, 8, 512], BF16)
        self.XTt = [T(self.XT.h, "XT_t%d" % i) for i in range(4)]
        self.QT = sb("QT", [128, 8, 512], BF16)
        self.OT = sb("OT", [128, 12, 512], BF16)
        self.tokb = Ring([sb("tokb%d" % i, [128, 1024], BF16) for i in range(2)])
        self.stg = Ring([sb("stg%d" % i, [128, 512], F32) for i in range(3)])
        self.ftr = self.stg
        self.pool_big = sb("plbig", [128, 12, 512], BF16)
        self.pool_tiles = [T(self.pool_big.h[:, i, :], "pl%d" % i) for i in range(12)]
        self.free = deque(self.pool_tiles)
        self.wring = Ring([sb("wr%d" % i, [128, 4096], BF16) for i in range(WSLOTS)])
        self.gb = sb("gb", [128, 1024], F32)
        self.KTa = sb("KTa", [128, 4, 2048], BF16)
        self.KTb = sb("KTb", [128, 4, 2048], BF16)
        self.KTc = sb("KTc", [128, 4, 2048], BF16)
        self.KPT = sb("KPT", [128, 2048], BF16)
        self.Va = sb("Va", [128, 16, 8, 66], BF16)
        self.Vb = sb("Vb", [128, 16, 4, 130], BF16)
        self.Vc = sb("Vc", [128, 16, 8, 66], BF16)
        self.FK = sb("FK", [128, 16, 8], F32)
        self.BK = sb("BK", [128, 16, 8], F32)
        self.carry = sb("carry", [128, 8], F32)
        self.cref = sb("cref", [128, 8], F32)
        self.lfc = sb("lfc", [128, 8, 8], F32)
        self.bf_t = sb("bf_t", [128, 8], F32)
        self.dng_t = sb("dng_t", [128, 128], F32)
        self.gq_t = sb("gq_t", [128, 384], F32)
        self.gkv_t = sb("gkv_t", [128, 256], F32)
        self.bg_t = sb("bg_t", [128, 24], F32)
        self.bhi = sb("bhi", [1, 1024], BF16)
        self.dl_t = sb("dl_t", [128, 256], F32)
        self.dl2 = sb("dl2", [128, 2, 64], F32)
        self.lam2 = sb("lam2", [128, 2], F32)
        self.neglam = sb("neglam", [128, 1], F32)
        self.ident = sb("ident", [128, 128], BF16)
        self.maskc = sb("maskc", [128, 128], BF16)
        self.maskd = sb("maskd", [128, 128], BF16)
        self.tri = sb("tri", [128, 128], F32)
        self.ones = sb("ones", [128, 128], F32)
        self.onesb = sb("onesb", [1, 128], BF16)
        self.ropeB = sb("ropeB", [128, 16, 2, 8], F32)
        self.ropeC = sb("ropeC", [128, 16, 2, 16], F32)
        self.st = sb("st", [128, 2, 6], F32)
        self.mv = sb("mv", [128, 2], F32)
        self.rstd = sb("rstd", [128, 1], F32)
        self.nmr = sb("nmr", [128, 1], F32)
        self.rc = Ring([sb("rc%d" % i, [128, 1], F32) for i in range(4)])
        self.ss = Ring([sb("ss%d" % i, [128, 1], F32) for i in range(2)])
        self.t1 = [sb("t1_%d" % i, [128, 128], F32) for i in range(4)]
        self.obf = Ring([sb("obf%d" % i, [128, 128], F32) for i in range(2)])
        self.junk = sb("junk", [128, 384], F32)
        self.e8 = Ring([sb("e8_%d" % i, [128, 8], F32) for i in range(2)])
        self.lf8 = Ring([sb("lf8_%d" % i, [128, 8], F32) for i in range(2)])
        self.kp32 = Ring([sb("kp32_%d" % i, [128, 32], F32) for i in range(2)])
        self.rt = [sb("rt%d" % i, [128, 8, 16], F32) for i in range(4)]
        self.psA = Ring([P.psum("psA%d" % i, [128, 512], F32) for i in range(4)])
        self.psS = Ring([P.psum("psS%d" % i, [128, 512], F32) for i in range(2)])
        self.psB = Ring([P.psum("psB%d" % i, [128, 1024], BF16) for i in range(2)])
        self.psS = Ring(self.psS.t + [t_.view(t_.h[:, :].bitcast(F32)) for t_ in self.psB.t])
        self.cp_i = 0
        print("sbuf bytes remaining:", nc.sbuf_bytes_remaining)

    def E(self, fn, reads, writes):
        return self.P.op("pe", fn, reads, writes)

    def A(self, fn, reads, writes):
        return self.P.op("act", fn, reads, writes)

    def V(self, fn, reads, writes):
        return self.P.op("dve", fn, reads, writes)

    def cp(self, oT, oap, iT, iap, eng=None):
        if eng is None:
            eng = "act" if (self.cp_i % 2 == 0) else "dve"
            self.cp_i += 1
        if eng == "act":
            self.A(lambda e: e.activation(out=oap, in_=iap, func=AF.Copy), [iT], [oT])
        else:
            self.V(lambda e: e.tensor_copy(out=oap, in_=iap), [iT], [oT])

    def mm(self, oT, oap, lT, lap, rT, rap, start, stop):
        self.E(lambda e: e.matmul(oap, lhsT=lap, rhs=rap, start=start, stop=stop), [lT, rT], [oT])

    def tr(self, oT, oap, iT, iap, n):
        ident = self.ident
        self.E(lambda e: e.transpose(out=oap, in_=iap, identity=ident[:n, :n]), [iT, ident], [oT])

    def palloc(self):
        return self.free.popleft()

    def pfree(self, t):
        self.free.append(t)

    def wget(self, src, shape, dep):
        slot = self.wring.next()
        n = int(np.prod(shape))
        assert n <= 4096
        dst = slot.h[:src.shape[0], 0:n]
        if len(shape) == 2:
            dst = dst.rearrange("p (a b) -> p a b", a=shape[0])
        elif len(shape) == 3:
            dst = dst.rearrange("p (a b c) -> p a b c", a=shape[0], b=shape[1])
        self.P.dma("sp", lambda e: e.dma_start(out=dst, in_=src), reads=[dep], writes=[slot])
        return slot, dst

    def load_gb(self, src_row):
        gb = self.gb
        self.P.dma("pool", lambda e: e.dma_start(out=gb[:, :], in_=src_row.to_broadcast([128, 1024])), writes=[gb])

    def out_dma(self, name, l, s, tok0, n, sT, sap):
        dst = self.d[name + ("_p" if self.kind == "p" else "_s")]
        dst = dst[l, s, tok0:tok0 + n, :] if l is not None else dst[s, tok0:tok0 + n, :]
        self.P.dma("pool", lambda e: e.dma_start(out=dst, in_=sap), reads=[sT])

    def setup(self):
        P, d = self.P, self.d
        for nm, t in (("c_ident", self.ident), ("c_maskc", self.maskc), ("c_maskd", self.maskd)):
            P.dma("pool", lambda e, nm=nm, t=t: e.dma_start(out=t[:, :], in_=d[nm]), writes=[t])
        for nm, t in (("c_tri", self.tri), ("c_ropeB", self.ropeB), ("c_ropeC", self.ropeC)):
            P.dma("sp", lambda e, nm=nm, t=t: e.dma_start(out=t[:], in_=d[nm]), writes=[t])
        for l in range(self.L):
            for nm in self.WNAMES:
                src = d[nm][l]
                dst = self.wb[nm][l]
                if len(src.shape) == 3:
                    src = src.rearrange("b k n -> (b k) n")
                    dst = dst.rearrange("b k n -> (b k) n")
                P.dma("pool", lambda e, src=src, dst=dst: e.dma_start(out=dst, in_=src, max_dma_last_dim=8192), writes=[self.wT[(nm, l)]])
        self.V(lambda e: e.memset(self.ones[:, :], 1.0), [], [self.ones])
        self.V(lambda e: e.memset(self.onesb[:, :], 1.0), [], [self.onesb])
        self.V(lambda e: e.memset(self.QT[:, :, :], 0.0), [], [self.QT])
        self.V(lambda e: e.memset(self.Va[:, :, :, 64:65], 1.0), [], [self.Va])
        self.V(lambda e: e.memset(self.Vb[:, :, :, 128:129], 1.0), [], [self.Vb])
        self.V(lambda e: e.memset(self.Vc[:, :, :, 64:65], 1.0), [], [self.Vc])

    def layer_params(self, l):
        P, d = self.P, self.d
        ld = lambda t, src: P.dma("sp", lambda e: e.dma_start(out=t[:], in_=src), writes=[t])
        ld(self.bf_t, d["bfg"][l:l + 1, :].to_broadcast([128, 8]))
        ld(self.dng_t, d["dng"][l:l + 1, :].to_broadcast([128, 128]))
        ld(self.gq_t, d["gq"][l:l + 1, :].to_broadcast([128, 384]))
        ld(self.gkv_t, d["gkv"][l:l + 1, :].to_broadcast([128, 256]))
        ld(self.bg_t, d["bg"][l])
        r32 = self.x[0]
        P.dma("sp", lambda e: e.dma_start(out=r32[0:1, :], in_=d["bpg"][l:l + 1, :]), writes=[r32])
        ld(self.dl_t, d["dlam"][l:l + 1, :].to_broadcast([128, 256]))
        lam_init = 0.8 - 0.6 * math.exp(-0.3 * l)
        self.lam_init = lam_init
        self.V(lambda e: e.tensor_scalar(self.dng_t[:, :], self.dng_t[:, :], 1.0 - lam_init, None, ALU.mult), [self.dng_t], [self.dng_t])
        self.V(lambda e: e.tensor_copy(out=self.bhi[:, :], in_=r32[0:1, :]), [r32], [self.bhi])
        dl = self.dl_t
        self.V(lambda e: e.tensor_tensor(out=self.dl2[:, 0, :], in0=dl[:, 0:64], in1=dl[:, 64:128], op=ALU.mult), [dl], [self.dl2])
        self.V(lambda e: e.tensor_tensor(out=self.dl2[:, 1, :], in0=dl[:, 128:192], in1=dl[:, 192:256], op=ALU.mult), [dl, self.dl2], [self.dl2])
        self.V(lambda e: e.reduce_sum(out=self.lam2[:, :], in_=self.dl2[:, :, :], axis=mybir.AxisListType.X), [self.dl2], [self.lam2])
        self.A(lambda e: e.activation(out=self.lam2[:, :], in_=self.lam2[:, :], func=AF.Exp), [self.lam2], [self.lam2])
        self.V(lambda e: e.tensor_tensor(out=self.neglam[:, :], in0=self.lam2[:, 1:2], in1=self.lam2[:, 0:1], op=ALU.subtract), [self.lam2], [self.neglam])
        self.V(lambda e: e.tensor_scalar(self.neglam[:, :], self.neglam[:, :], -lam_init, None, ALU.add), [self.neglam], [self.neglam])

    def make_XT(self):
        nt, ntl = self.nt, self.ntl
        XT = self.XT
        for t in range(ntl):
            tb = self.tokb.next()
            x = self.x[t]
            self.cp(tb, tb[:nt, :], x, x[:nt, :])
            pb = self.psB.next()
            for c in range(8):
                self.tr(pb, pb[:, c * nt:(c + 1) * nt], tb, tb[:nt, c * 128:(c + 1) * 128], nt)
            self.cp(self.XTt[t], XT[:, :, t * nt:(t + 1) * nt], pb, pb[:, 0:8 * nt].rearrange("p (c n) -> p c n", c=8))

    def layernorm(self, l, k):
        nt, ntl = self.nt, self.ntl
        d = self.d
        gb = self.gb
        bt = self.pool_tiles[8:12]
        for t_ in bt:
            self.free.remove(t_)
        bview = self.pool_big.h[:, 8:12, :].rearrange("p a b -> p (a b)").bitcast(F32)
        self.load_gb(d["lng"][l, k:k + 1, :])
        brow = d["lnb"][l, k:k + 1, :]
        self.P.dma("pool", lambda e: e.dma_start(out=bview, in_=brow.to_broadcast([128, 1024])), writes=bt)
        for t in range(ntl):
            x = self.x[t]
            st, mv, rstd, nmr = self.st, self.mv, self.rstd, self.nmr
            for i in range(2):
                self.V(lambda e, i=i, x=x: e.bn_stats(out=st[:nt, i, :], in_=x[:nt, i * 512:(i + 1) * 512]), [x], [st])
            self.V(lambda e: e.bn_aggr(out=mv[:nt, :], in_=st[:nt, :, :].rearrange("p a b -> p (a b)")), [st], [mv])
            self.A(lambda e: e.activation(out=rstd[:nt, :], in_=mv[:nt, 1:2], func=AF.Sqrt, bias=LN_EPS, scale=1.0), [mv], [rstd])
            self.V(lambda e: e.reciprocal(out=rstd[:nt, :], in_=rstd[:nt, :]), [rstd], [rstd])
            self.V(lambda e: e.tensor_scalar(nmr[:nt, :], mv[:nt, 0:1], -1.0, rstd[:nt, 0:1], ALU.mult, ALU.mult), [mv, rstd], [nmr])
            self.A(lambda e, x=x: e.activation(out=x[:nt, :], in_=x[:nt, :], func=AF.Identity, bias=nmr[:nt, 0:1], scale=rstd[:nt, 0:1]), [x, nmr, rstd], [x])
            self.V(lambda e, x=x: e.tensor_tensor(out=x[:nt, :], in0=x[:nt, :], in1=gb[:nt, :], op=ALU.mult), [x, gb], [x])
            self.V(lambda e, x=x: e.tensor_tensor(out=x[:nt, :], in0=x[:nt, :], in1=bview[:nt, :], op=ALU.add), [x] + bt, [x])
        for t_ in bt:
            self.pfree(t_)

    def ffn(self, l, which):
        nt, ntl, NT = self.nt, self.ntl, self.NT
        wn_i = "w1i" if which == 1 else "w2i"
        wn_o = "w1o" if which == 1 else "w2o"
        w_in = self.wb[wn_i][l].rearrange("(c p) (g f) -> p c g f", p=128, g=2)
        w_out = self.wb[wn_o][l].rearrange("(c p) n -> p c n", p=128)
        XT = self.XT
        self.make_XT()
        for half in range(2):
            HT = [self.palloc() for _ in range(11)]
            for jj in range(0, 11, 2):
                nj = min(2, 11 - jj)
                j0 = half * 11 + jj
                slot = self.wring.next()
                wv = slot.h[:, 0:16 * nj * 128].rearrange("p (a b c) -> p a b c", a=8, b=2)
                for g in range(2):
                    self.P.dma("sp", lambda e, g=g, wv=wv, j0=j0, nj=nj: e.dma_start(out=wv[:, :, g, :], in_=w_in[:, :, g, j0 * 128:(j0 + nj) * 128]), reads=[self.wT[(wn_i, l)]], writes=[slot])
                for q in range(nj):
                    pg = self.psA.next()
                    pu = self.psA.next()
                    if half == 0 and jj == 0:
                        for t in range(ntl):
                            for g, ps in ((0, pg), (1, pu)):
                                for c in range(8):
                                    self.mm(ps, ps[:, t * nt:(t + 1) * nt], slot, wv[:, c, g, q * 128:(q + 1) * 128], self.XTt[t], XT[:, c, t * nt:(t + 1) * nt], c == 0, c == 7)
                    else:
                        for g, ps in ((0, pg), (1, pu)):
                            for c in range(8):
                                self.mm(ps, ps[:, :NT], slot, wv[:, c, g, q * 128:(q + 1) * 128], self.XTt, XT[:, c, :NT], c == 0, c == 7)
                    ft = self.ftr.next()
                    self.A(lambda e, ft=ft, pg=pg: e.activation(out=ft[:, :NT], in_=pg[:, :NT], func=AF.Silu), [pg], [ft])
                    h = HT[jj + q]
                    self.V(lambda e, h=h, ft=ft, pu=pu: e.scalar_tensor_tensor(out=h[:, :NT], in0=ft[:, :NT], scalar=0.5, in1=pu[:, :NT], op0=ALU.mult, op1=ALU.mult), [ft, pu], [h])
            for nh in range(2):
                accs = [self.psA.next() for _ in range(ntl)]
                for k0, nk in ((0, 6), (6, 5)):
                    slot, wv = self.wget(w_out[:, half * 11 + k0:half * 11 + k0 + nk, nh * 512:(nh + 1) * 512], [nk, 512], self.wT[(wn_o, l)])
                    for t in range(ntl):
                        for k in range(nk):
                            h = HT[k0 + k]
                            self.mm(accs[t], accs[t][:nt, :], h, h[:, t * nt:(t + 1) * nt], slot, wv[:, k, :], k0 + k == 0, k0 + k == 10)
                for t in range(ntl):
                    x = self.x[t]
                    xa = x[:nt, nh * 512:(nh + 1) * 512]
                    acc = accs[t]
                    if half == 0:
                        self.V(lambda e, xa=xa, acc=acc: e.scalar_tensor_tensor(out=xa, in0=xa, scalar=ALPHA, in1=acc[:nt, :], op0=ALU.mult, op1=ALU.add), [x, acc], [x])
                    else:
                        self.V(lambda e, xa=xa, acc=acc: e.tensor_tensor(out=xa, in0=xa, in1=acc[:nt, :], op=ALU.add), [x, acc], [x])
            for h in HT:
                self.pfree(h)
        self.layernorm(l, 0 if which == 1 else 2)

    def proj_tm(self, l, col0, ncols, consume):
        nt, ntl = self.nt, self.ntl
        XT = self.XT
        src = self.wb["wmix"][l].rearrange("(c p) n -> p c n", p=128)[:, :, col0:col0 + ncols]
        slot, wv = self.wget(src, [8, ncols], self.wT[("wmix", l)])
        pending = None
        for t in range(ntl):
            ps = self.psA.next()
            for c in range(8):
                self.mm(ps, ps[:nt, :ncols], self.XTt[t], XT[:, c, t * nt:(t + 1) * nt], slot, wv[:, c, :], c == 0, c == 7)
            nxt = consume(t, ps)
            if pending is not None:
                pending()
            pending = nxt
        if pending is not None:
            pending()

    def to_fm(self, tb, nchunks, dT, dap_fn, t, widths=None):
        nt = self.nt
        pb = self.psB.next()
        off = 0
        for c in range(nchunks):
            w = 128 if widths is None else widths[c]
            self.tr(pb, pb[:w, c * nt:(c + 1) * nt], tb, tb[:nt, off:off + w], nt)
            off += w
        dap_fn(pb)

    def rope(self, sg, view, H, half, tab, tile):
        nt = self.nt
        cos = tab[:nt, tile, 0, :].unsqueeze(1).to_broadcast([nt, H, half])
        sin = tab[:nt, tile, 1, :].unsqueeze(1).to_broadcast([nt, H, half])
        x1 = view[:, :, 0:half]
        x2 = view[:, :, half:2 * half]
        a, b, c, dd = [r[:nt, 0:H, 0:half] for r in self.rt]
        rT = self.rt
        self.V(lambda e: e.tensor_tensor(out=a, in0=x1, in1=cos, op=ALU.mult), [sg, tab], [rT[0]])
        self.V(lambda e: e.tensor_tensor(out=b, in0=x2, in1=sin, op=ALU.mult), [sg, tab], [rT[1]])
        self.V(lambda e: e.tensor_tensor(out=c, in0=x2, in1=cos, op=ALU.mult), [sg, tab], [rT[2]])
        self.V(lambda e: e.tensor_tensor(out=dd, in0=x1, in1=sin, op=ALU.mult), [sg, tab], [rT[3]])
        self.V(lambda e: e.tensor_tensor(out=x1, in0=a, in1=b, op=ALU.subtract), [rT[0], rT[1]], [sg])
        self.V(lambda e: e.tensor_tensor(out=x2, in0=c, in1=dd, op=ALU.add), [rT[2], rT[3]], [sg])

    def cumsum(self, kti, lfT, lfap, n):
        ps = self.psA.next()
        tri, ones, FK, carry = self.tri, self.ones, self.FK, self.carry
        self.mm(ps, ps[:n, 0:8], tri, tri[:n, :n], lfT, lfap, True, True)
        self.mm(ps, ps[:, 8:16], ones, ones[:n, :], lfT, lfap, True, True)
        self.V(lambda e: e.tensor_tensor(out=FK[:n, kti, :], in0=ps[:n, 0:8], in1=carry[:n, :], op=ALU.add), [ps, carry], [FK])
        self.V(lambda e: e.tensor_tensor(out=carry[:, :], in0=carry[:, :], in1=ps[:, 8:16], op=ALU.add), [ps, carry], [carry])

    def attention(self, nheads, kq_fn, Vbuf, vap_fn, vdim, scale, use_bias, mask, finish_fn):
        nt, ntl, NT, kt0 = self.nt, self.ntl, self.NT, self.kt0
        nkt = kt0 + ntl
        ident = self.ident
        BK = self.BK
        for h in range(nheads):
            accs = [self.psA.next() for _ in range(ntl)]
            ops = kq_fn(h)

            def emit_s(kt):
                diag = kt >= kt0
                i = kt - kt0 if diag else 0
                nk = nt if diag else 128
                kc0 = kt * 128
                q0 = i * nt if diag else 0
                sp = self.psS.next()
                use_mask = diag and (mask is not None)
                for oi, (KTt, kap, QTt, qap) in enumerate(ops):
                    self.mm(sp, sp[:nk, q0:NT], KTt, kap(kc0, nk), QTt, qap(q0, NT), oi == 0, (oi == len(ops) - 1) and not use_mask)
                if use_mask:
                    self.mm(sp, sp[:nk, q0:q0 + nt], ident, ident[:, :nk], mask, mask[:, :nt], False, True)
                pt = self.palloc()
                if use_bias:
                    bap = BK[:nk, kt, h:h + 1]
                    self.A(lambda e: e.activation(out=pt[:nk, q0:NT], in_=sp[:nk, q0:NT], func=AF.Exp, scale=scale, bias=bap), [sp, BK], [pt])
                else:
                    self.A(lambda e: e.activation(out=pt[:nk, q0:NT], in_=sp[:nk, q0:NT], func=AF.Exp, scale=scale), [sp], [pt])
                return (kt, i if diag else 0, nk, pt)

            def emit_pv(item):
                kt, j0, nk, pt = item
                for j in range(j0, ntl):
                    self.mm(accs[j], accs[j][:nt, 0:vdim + 1], pt, pt[:nk, j * nt:(j + 1) * nt], Vbuf, vap_fn(kt, h, nk), kt == 0, kt == kt0 + j)
                self.pfree(pt)
                if kt >= kt0:
                    finish_fn(h, kt - kt0, accs[kt - kt0])

            items = []
            for kt in range(nkt):
                items.append(emit_s(kt))
                if len(items) > 3:
                    emit_pv(items.pop(0))
            while items:
                emit_pv(items.pop(0))

    def ot_from(self, Otok, b):
        nt, ntl = self.nt, self.ntl
        OT = self.OT
        for j in range(ntl):
            ob = Otok[j]
            self.to_fm(ob, 4, OT, lambda pb, j=j: self.cp(OT, OT[:, b * 4:(b + 1) * 4, j * nt:(j + 1) * nt], pb, pb[:, 0:4 * nt].rearrange("p (c n) -> p c n", c=4)), j)
            self.pfree(ob)

    def mix(self, l, s):
        nt, ntl, NT, kt0, pos0 = self.nt, self.ntl, self.NT, self.kt0, self.pos0
        tok0 = self.tok0
        d = self.d
        XT, QT, OT = self.XT, self.QT, self.OT
        self.make_XT()
        KTa, Va, KTb, Vb, KTc, Vc, KPT = self.KTa, self.Va, self.KTb, self.Vb, self.KTc, self.Vc, self.KPT

        QT4 = QT[:, :, :].rearrange("p (c two) n -> p c two n", two=2)

        def qt_store(pb, t):
            self.cp(QT, QT4[0:64, :, 0, t * nt:(t + 1) * nt], pb, pb[0:64, 0:4 * nt].rearrange("p (c n) -> p c n", c=4), "act")
            self.cp(QT, QT4[64:128, :, 1, t * nt:(t + 1) * nt], pb, pb[64:128, 0:4 * nt].rearrange("p (c n) -> p c n", c=4), "act")

        def c_q(t, ps):
            tb = self.tokb.next()
            self.cp(tb, tb[:nt, :512], ps, ps[:nt, :512])
            return lambda: self.to_fm(tb, 4, QT, lambda pb: qt_store(pb, t), t)

        def c_k(name, KT):
            def f(t, ps):
                sg = self.stg.next()
                self.cp(sg, sg[:nt, :512], ps, ps[:nt, :512], "act")
                if name == "diff_k":
                    self.rope(sg, sg[:nt, :512].rearrange("p (h d) -> p h d", h=8), 8, 8, self.ropeB, kt0 + t)
                self.out_dma(name, l, s, tok0 + t * nt, nt, sg, sg[:nt, :512])
                tb = self.tokb.next()
                self.cp(tb, tb[:nt, :512], sg, sg[:nt, :512], "dve")
                return lambda: self.to_fm(tb, 4, KT, lambda pb: self.cp(KT, KT[:, :, pos0 + t * nt:pos0 + (t + 1) * nt], pb, pb[:, 0:4 * nt].rearrange("p (c n) -> p c n", c=4)), t)
            return f

        def c_v(name, Vb_, H, D):
            def f(t, ps):
                sg = self.stg.next()
                self.cp(sg, sg[:nt, :512], ps, ps[:nt, :512], "act")
                self.out_dma(name, l, s, tok0 + t * nt, nt, sg, sg[:nt, :512])
                self.cp(Vb_, Vb_[:nt, kt0 + t, :, 0:D], sg, sg[:nt, :512].rearrange("p (h d) -> p h d", h=H), "dve")
            return f

        def c_fa(t, ps):
            e8 = self.e8.next()
            lf = self.lf8.next()
            bf_t = self.bf_t
            self.V(lambda e: e.tensor_tensor(out=e8[:nt, :], in0=ps[:nt, 0:8], in1=bf_t[:nt, :], op=ALU.add), [ps, bf_t], [e8])
            self.A(lambda e: e.activation(out=e8[:nt, :], in_=e8[:nt, :], func=AF.Exp, scale=-1.0), [e8], [e8])
            self.A(lambda e: e.activation(out=e8[:nt, :], in_=e8[:nt, :], func=AF.Ln, bias=1.0, scale=1.0), [e8], [e8])
            self.V(lambda e: e.tensor_scalar(lf[:nt, :], e8[:nt, :], -1.0, None, ALU.mult), [e8], [lf])
            self.out_dma("fox_logf", l, s, tok0 + t * nt, nt, lf, lf[:nt, :])
            return lambda: self.cumsum(kt0 + t, lf, lf[:nt, :], nt)

        carry, cref, FK, BK = self.carry, self.cref, self.FK, self.BK
        self.V(lambda e: e.tensor_copy(out=cref[:, :], in_=carry[:, :]), [carry], [cref])
        self.proj_tm(l, MIXOFF["fa"], 8, c_fa)
        for kt in range(kt0 + ntl):
            n = 128 if kt < kt0 else nt
            self.V(lambda e, kt=kt, n=n: e.tensor_tensor(out=BK[:n, kt, :], in0=cref[:n, :], in1=FK[:n, kt, :], op=ALU.subtract), [cref, FK], [BK])
        if self.stop_after == "fa":
            return
        self.proj_tm(l, MIXOFF["qa"], 512, c_q)
        if self.stop_after == "fq":
            return
        self.proj_tm(l, MIXOFF["ka"], 512, c_k("fox_k", KTa))
        if self.stop_after == "fk":
            return
        self.proj_tm(l, MIXOFF["va"], 512, c_v("fox_v", Va, 8, 64))

        if self.stop_after == "fproj":
            return
        Otok = [self.palloc() for _ in range(ntl)]

        def kq_a(h):
            c, b = h // 2, (h % 2) * 64
            return [(KTa, lambda kc0, nk: KTa[:, c, kc0:kc0 + nk], QT, lambda q0, q1: QT[:, h, q0:q1])]

        def fin_simple(h, j, acc):
            rc = self.rc.next()
            ot = Otok[j]
            self.V(lambda e: e.reciprocal(out=rc[:nt, :], in_=acc[:nt, 64:65]), [acc], [rc])
            self.V(lambda e: e.tensor_scalar(ot[:nt, h * 64:(h + 1) * 64], acc[:nt, 0:64], rc[:nt, 0:1], None, ALU.mult), [acc, rc], [ot])

        self.attention(8, kq_a, Va, lambda kt, h, nk: Va[:nk, kt, h, 0:65], 64, 0.125, True, self.maskc, fin_simple)
        self.ot_from(Otok, 0)
        if self.stop_after == "fox":
            return

        def c_qrope(t, ps):
            sg = self.stg.next()
            self.cp(sg, sg[:nt, :512], ps, ps[:nt, :512], "act")
            self.rope(sg, sg[:nt, :512].rearrange("p (h d) -> p h d", h=8), 8, 8, self.ropeB, kt0 + t)
            tb = self.tokb.next()
            self.cp(tb, tb[:nt, :512], sg, sg[:nt, :512], "dve")
            return lambda: self.to_fm(tb, 4, QT, lambda pb: qt_store(pb, t), t)

        self.proj_tm(l, MIXOFF["qb"], 512, c_qrope)
        if self.stop_after == "dq":
            return
        self.proj_tm(l, MIXOFF["kb"], 512, c_k("diff_k", KTb))
        self.proj_tm(l, MIXOFF["vb"], 512, c_v("diff_v", Vb, 4, 128))
        if self.stop_after == "dproj":
            return
        Otok = [self.palloc() for _ in range(ntl)]

        def kq_b(v):
            c, b = v // 2, (v % 2) * 64
            return [(KTb, lambda kc0, nk: KTb[:, c, kc0:kc0 + nk], QT, lambda q0, q1: QT[:, v, q0:q1])]

        neglam, dng_t = self.neglam, self.dng_t

        def fin_b(v, j, acc):
            hh, m = v // 2, v % 2
            rc = self.rc.next()
            self.V(lambda e: e.reciprocal(out=rc[:nt, :], in_=acc[:nt, 128:129]), [acc], [rc])
            t1 = self.t1[j]
            if m == 0:
                self.V(lambda e: e.tensor_scalar(t1[:nt, :], acc[:nt, 0:128], rc[:nt, 0:1], None, ALU.mult), [acc, rc], [t1])
                return
            obf = self.obf.next()
            ss = self.ss.next()
            junk = self.junk
            ot = Otok[j]
            self.V(lambda e: e.tensor_tensor(out=rc[:nt, :], in0=rc[:nt, :], in1=neglam[:nt, :], op=ALU.mult), [rc, neglam], [rc])
            self.V(lambda e: e.scalar_tensor_tensor(out=obf[:nt, :], in0=acc[:nt, 0:128], scalar=rc[:nt, 0:1], in1=t1[:nt, :], op0=ALU.mult, op1=ALU.add), [acc, rc, t1], [obf])
            self.V(lambda e: e.tensor_tensor(out=junk[:nt, 0:128], in0=obf[:nt, :], in1=obf[:nt, :], op=ALU.mult), [obf], [junk])
            self.V(lambda e: e.reduce_sum(out=ss[:nt, 0:1], in_=junk[:nt, 0:128], axis=mybir.AxisListType.X), [junk], [ss])
            self.A(lambda e: e.activation(out=ss[:nt, :], in_=ss[:nt, :], func=AF.Sqrt, bias=RMS_EPS, scale=1.0 / 128), [ss], [ss])
            self.V(lambda e: e.reciprocal(out=ss[:nt, :], in_=ss[:nt, :]), [ss], [ss])
            self.V(lambda e: e.scalar_tensor_tensor(out=ot[:nt, hh * 128:(hh + 1) * 128], in0=obf[:nt, :], scalar=ss[:nt, 0:1], in1=dng_t[:nt, :], op0=ALU.mult, op1=ALU.mult), [obf, ss, dng_t], [ot])

        self.attention(8, kq_b, Vb, lambda kt, v, nk: Vb[:nk, kt, v // 2, 0:129], 128, 0.125, False, self.maskd if self.kind == "p" else None, fin_b)
        self.ot_from(Otok, 1)
        if self.stop_after == "diff":
            return

        dqnT = [self.palloc() for _ in range(3)]
        gq_t, gkv_t = self.gq_t, self.gkv_t

        def rms_to(ps, n, gt, oT, oap):
            ss = self.ss.next()
            junk = self.junk
            self.A(lambda e: e.activation(out=junk[:nt, 0:n], in_=ps[:nt, 0:n], func=AF.Copy), [ps], [junk])
            self.V(lambda e: e.tensor_tensor(out=junk[:nt, 0:n], in0=junk[:nt, 0:n], in1=junk[:nt, 0:n], op=ALU.mult), [junk], [junk])
            self.V(lambda e: e.reduce_sum(out=ss[:nt, 0:1], in_=junk[:nt, 0:n], axis=mybir.AxisListType.X), [junk], [ss])
            self.A(lambda e: e.activation(out=ss[:nt, :], in_=ss[:nt, :], func=AF.Sqrt, bias=RMS_EPS, scale=1.0 / n), [ss], [ss])
            self.V(lambda e: e.reciprocal(out=ss[:nt, :], in_=ss[:nt, :]), [ss], [ss])
            self.V(lambda e: e.scalar_tensor_tensor(out=oap, in0=ps[:nt, 0:n], scalar=ss[:nt, 0:1], in1=gt[:nt, 0:n], op0=ALU.mult, op1=ALU.mult), [ps, ss, gt], [oT])

        def c_dq(t, ps):
            tb = self.tokb.next()
            rms_to(ps, 384, gq_t, tb, tb[:nt, 0:384])

            def dst(pb):
                for c in range(3):
                    self.cp(dqnT[c], dqnT[c][:, t * nt:(t + 1) * nt], pb, pb[:, c * nt:(c + 1) * nt])
            return lambda: self.to_fm(tb, 3, None, dst, t)

        self.proj_tm(l, MIXOFF["dq"], 384, c_dq)
        if self.stop_after == "cdq":
            return
        QPT = [self.palloc() for _ in range(4)]
        slot, wv = self.wget(self.wb["wuq"][l].rearrange("(c p) n -> p c n", p=128), [3, 768], self.wT[("wuq", l)])
        for t in range(ntl):
            ps1 = self.psA.next()
            for c in range(3):
                self.mm(ps1, ps1[:nt, :512], dqnT[c], dqnT[c][:, t * nt:(t + 1) * nt], slot, wv[:, c, 0:512], c == 0, c == 2)
            lat = c_q(t, ps1)
            ps2 = self.psA.next()
            for c in range(3):
                self.mm(ps2, ps2[:nt, :256], dqnT[c], dqnT[c][:, t * nt:(t + 1) * nt], slot, wv[:, c, 512:768], c == 0, c == 2)
            sg = self.stg.next()
            self.cp(sg, sg[:nt, :256], ps2, ps2[:nt, :256], "act")
            self.rope(sg, sg[:nt, :256].rearrange("p (h d) -> p h d", h=8), 8, 16, self.ropeC, kt0 + t)
            tb = self.tokb.next()
            self.V(lambda e, tb=tb: e.memset(tb[:nt, 0:512], 0.0), [], [tb])
            self.cp(tb, tb[:nt, 0:512].rearrange("p (h d) -> p h d", h=8)[:, :, 0:32], sg, sg[:nt, :256].rearrange("p (h d) -> p h d", h=8), "dve")

            def dst(pb, t=t):
                for g in range(4):
                    self.cp(QPT[g], QPT[g][:, t * nt:(t + 1) * nt], pb, pb[:, g * nt:(g + 1) * nt])
            lat()
            self.to_fm(tb, 4, None, dst, t)
        for c in range(3):
            self.pfree(dqnT[c])
        if self.stop_after == "cq":
            return
        ckvT = [self.palloc() for _ in range(2)]

        def c_kv(t, ps):
            kp = self.kp32.next()
            self.cp(kp, kp[:nt, :], ps, ps[:nt, 256:288], "act")
            sg = self.stg.next()
            rms_to(ps, 256, gkv_t, sg, sg[:nt, 0:256])
            self.out_dma("mla_ckv", l, s, tok0 + t * nt, nt, sg, sg[:nt, 0:256])
            tb = self.tokb.next()
            self.cp(tb, tb[:nt, :256], sg, sg[:nt, :256], "dve")
            self.rope(kp, kp[:nt, :].rearrange("p (h d) -> p h d", h=1), 1, 16, self.ropeC, kt0 + t)
            self.out_dma("mla_kpe", l, s, tok0 + t * nt, nt, kp, kp[:nt, :])
            self.V(lambda e, tb=tb: e.memset(tb[:nt, 256:384], 0.0), [], [tb])
            for r in range(2):
                self.cp(tb, tb[:nt, 256 + r * 64:256 + r * 64 + 32], kp, kp[:nt, :], "dve")

            def dst(pb):
                for c in range(2):
                    self.cp(ckvT[c], ckvT[c][:, t * nt:(t + 1) * nt], pb, pb[:, c * nt:(c + 1) * nt])
                self.cp(KPT, KPT[:, pos0 + t * nt:pos0 + (t + 1) * nt], pb, pb[:, 2 * nt:3 * nt])
            return lambda: self.to_fm(tb, 3, None, dst, t)

        self.proj_tm(l, MIXOFF["dkv"], 288, c_kv)
        self.mla_kv(l, ckvT, [(0, NT)], pos0, kt0, nt, ntl)
        for c in range(2):
            self.pfree(ckvT[c])
        if self.stop_after == "ckv":
            return
        Otok = [self.palloc() for _ in range(ntl)]

        def kq_c(h):
            c, b = h // 2, (h % 2) * 64
            return [(KTc, lambda kc0, nk: KTc[:, c, kc0:kc0 + nk], QT, lambda q0, q1: QT[:, h, q0:q1]),
                    (KPT, lambda kc0, nk: KPT[b:b + 64, kc0:kc0 + nk], QPT[c], lambda q0, q1: QPT[c][b:b + 64, q0:q1])]

        self.attention(8, kq_c, Vc, lambda kt, h, nk: Vc[:nk, kt, h, 0:65], 64, 96.0 ** -0.5, False, self.maskd if self.kind == "p" else None, fin_simple)
        for g in range(4):
            self.pfree(QPT[g])
        self.ot_from(Otok, 2)
        if self.stop_after == "mla":
            for cc in range(12):
                self.P.dma("pool", lambda e, cc=cc: e.dma_start(out=d["dbg_ot"][:, cc, :], in_=OT[:, cc, :]), reads=[OT])
            return

        MG = [self.palloc() for _ in range(8)]
        bg_t = self.bg_t
        for b in range(3):
            for half in range(2):
                slot_b, wb = self.wget(self.wb["wbr"][l, b].rearrange("(c p) n -> p c n", p=128)[:, :, half * 512:(half + 1) * 512], [4, 512], self.wT[("wbr", l)])
                slot_g, wgv = self.wget(self.wb["wg"][l].rearrange("(c p) n -> p c n", p=128)[:, :, b * 1024 + half * 512:b * 1024 + (half + 1) * 512], [8, 512], self.wT[("wg", l)])
                for o4 in range(4):
                    oc = half * 4 + o4
                    pgt = self.psA.next()
                    for c in range(8):
                        self.mm(pgt, pgt[:, :NT], slot_g, wgv[:, c, o4 * 128:(o4 + 1) * 128], self.XTt, XT[:, c, :NT], c == 0, c == 7)
                    gsb = self.ftr.next()
                    bcol = b * 8 + oc
                    self.A(lambda e, gsb=gsb, pgt=pgt, bcol=bcol: e.activation(out=gsb[:, :NT], in_=pgt[:, :NT], func=AF.Sigmoid, bias=bg_t[:, bcol:bcol + 1], scale=1.0), [pgt, bg_t], [gsb])
                    pbr = self.psA.next()
                    for kc in range(4):
                        self.mm(pbr, pbr[:, :NT], slot_b, wb[:, kc, o4 * 128:(o4 + 1) * 128], OT, OT[:, b * 4 + kc, :NT], kc == 0, kc == 3)
                    mg = MG[oc]
                    if b == 0:
                        self.V(lambda e, mg=mg, gsb=gsb, pbr=pbr: e.tensor_tensor(out=mg[:, :NT], in0=gsb[:, :NT], in1=pbr[:, :NT], op=ALU.mult), [gsb, pbr], [mg])
                    else:
                        self.V(lambda e, gsb=gsb, pbr=pbr: e.tensor_tensor(out=gsb[:, :NT], in0=gsb[:, :NT], in1=pbr[:, :NT], op=ALU.mult), [gsb, pbr], [gsb])
                        self.V(lambda e, mg=mg, gsb=gsb: e.tensor_tensor(out=mg[:, :NT], in0=mg[:, :NT], in1=gsb[:, :NT], op=ALU.add), [mg, gsb], [mg])
        for nh in range(2):
            slot, wv = self.wget(self.wb["wo"][l].rearrange("(c p) n -> p c n", p=128)[:, :, nh * 512:(nh + 1) * 512], [8, 512], self.wT[("wo", l)])
            for t in range(ntl):
                ps = self.psA.next()
                for kc in range(8):
                    self.mm(ps, ps[:nt, :], MG[kc], MG[kc][:, t * nt:(t + 1) * nt], slot, wv[:, kc, :], kc == 0, kc == 7)
                x = self.x[t]
                xa = x[:nt, nh * 512:(nh + 1) * 512]
                self.V(lambda e, xa=xa, ps=ps: e.scalar_tensor_tensor(out=xa, in0=xa, scalar=ALPHA, in1=ps[:nt, :], op0=ALU.mult, op1=ALU.add), [x, ps], [x])
        for m in MG:
            self.pfree(m)
        self.layernorm(l, 1)

    def mla_kv(self, l, ckvT, ranges, pos0, kt0, nt, ntl):
        d = self.d
        KTc, Vc = self.KTc, self.Vc
        slot, wv = self.wget(self.wb["wuk"][l].rearrange("(c p) n -> p c n", p=128), [2, 512], self.wT[("wuk", l)])
        for (c0, n) in ranges:
            for c in range(4):
                ps = self.psA.next()
                for lc in range(2):
                    self.mm(ps, ps[:, :n], slot, wv[:, lc, c * 128:(c + 1) * 128], ckvT[lc], ckvT[lc][:, c0:c0 + n], lc == 0, lc == 1)
                self.cp(KTc, KTc[:, c, pos0 + c0:pos0 + c0 + n], ps, ps[:, :n])
        slot, wv = self.wget(self.wb["wuv"][l].rearrange("(c p) n -> p c n", p=128), [2, 512], self.wT[("wuv", l)])
        for t in range(ntl):
            ps = self.psA.next()
            for lc in range(2):
                self.mm(ps, ps[:nt, :], ckvT[lc], ckvT[lc][:, t * nt:(t + 1) * nt], slot, wv[:, lc, :], lc == 0, lc == 1)
            tb = self.tokb.next()
            self.cp(tb, tb[:nt, 0:512], ps, ps[:nt, :], "act")
            self.cp(Vc, Vc[:nt, kt0 + t, :, 0:64], tb, tb[:nt, 0:512].rearrange("p (h d) -> p h d", h=8), "dve")

    def ple(self, l, s):
        nt, ntl, NT, pos0 = self.nt, self.ntl, self.NT, self.pos0
        d = self.d
        XT = self.XT
        self.make_XT()
        pt = [self.palloc() for _ in range(2)]
        src = d["ppT" if self.kind == "p" else "psT"]
        for kc in range(2):
            sap = src[l, s, kc, :, self.tok0:self.tok0 + NT]
            self.P.dma("pool", lambda e, kc=kc, sap=sap: e.dma_start(out=pt[kc][:, :NT], in_=sap), writes=[pt[kc]])
        onesb, bhi = self.onesb, self.bhi
        for nh in range(2):
            slot, wv = self.wget(self.wb["wpg"][l].rearrange("(c p) n -> p c n", p=128)[:, :, nh * 512:(nh + 1) * 512], [8, 512], self.wT[("wpg", l)])
            slot2, wp = self.wget(self.wb["wpp"][l].rearrange("(c p) n -> p c n", p=128)[:, :, nh * 512:(nh + 1) * 512], [2, 512], self.wT[("wpp", l)])
            for t in range(ntl):
                pg = self.psA.next()
                for c in range(8):
                    self.mm(pg, pg[:nt, :], self.XTt[t], XT[:, c, t * nt:(t + 1) * nt], slot, wv[:, c, :], c == 0, False)
                self.mm(pg, pg[:nt, :], onesb, onesb[0:1, :nt], bhi, bhi[0:1, nh * 512:(nh + 1) * 512], False, True)
                gs = self.ftr.next()
                self.A(lambda e, gs=gs, pg=pg: e.activation(out=gs[:nt, :], in_=pg[:nt, :], func=AF.Sigmoid), [pg], [gs])
                pp = self.psA.next()
                for kc in range(2):
                    self.mm(pp, pp[:nt, :], pt[kc], pt[kc][:, t * nt:(t + 1) * nt], slot2, wp[:, kc, :], kc == 0, kc == 1)
                self.V(lambda e, gs=gs, pp=pp: e.tensor_tensor(out=gs[:nt, :], in0=gs[:nt, :], in1=pp[:nt, :], op=ALU.mult), [gs, pp], [gs])
                x = self.x[t]
                xa = x[:nt, nh * 512:(nh + 1) * 512]
                self.V(lambda e, xa=xa, gs=gs: e.scalar_tensor_tensor(out=xa, in0=xa, scalar=ALPHA, in1=gs[:nt, :], op0=ALU.mult, op1=ALU.add), [x, gs], [x])
        for p_ in pt:
            self.pfree(p_)
        self.layernorm(l, 3)

    def block(self, l, kind, s, B):
        self.kind = kind
        if kind == "p":
            self.nt, self.ntl = 128, 4
            self.kt0 = 4 * B
            self.tok0 = 512 * B
            self.pos0 = 512 * B
        else:
            self.nt, self.ntl = 64, 1
            self.kt0 = 8
            self.tok0 = 0
            self.pos0 = 1024
        self.NT = self.nt * self.ntl
        nt, ntl, tok0 = self.nt, self.ntl, self.tok0
        d, P = self.d, self.P
        xmid = self.xmid_p if kind == "p" else self.xmid_s
        xin = d["xp" if kind == "p" else "xs"]
        for t in range(ntl):
            x = self.x[t]
            if l == 0:
                sap = xin[s, tok0 + t * nt:tok0 + (t + 1) * nt, :]
                P.dma("sp", lambda e, x=x, sap=sap: e.dma_start(out=x[:nt, :], in_=sap), writes=[x])
            else:
                sap = xmid[s, tok0 + t * nt:tok0 + (t + 1) * nt, :]
                P.dma("sp", lambda e, x=x, sap=sap: e.dma_start(out=x[:nt, :], in_=sap), writes=[x], extra=list(self.xmid_ev.values()))
        sa = self.stop_after
        self.ffn(l, 1)
        if sa != "ffn1":
            self.mix(l, s)
            if sa is None or sa == "ffn2":
                self.ffn(l, 2)
                if sa is None:
                    self.ple(l, s)
        last = (l == self.L - 1)
        for t in range(ntl):
            x = self.x[t]
            if last:
                self.out_dma("y", None, s, tok0 + t * nt, nt, x, x[:nt, :])
            else:
                dap = xmid[s, tok0 + t * nt:tok0 + (t + 1) * nt, :]
                ev = P.dma("pool", lambda e, x=x, dap=dap: e.dma_start(out=dap, in_=x[:nt, :]), reads=[x])
                self.xmid_ev[id(ev[1])] = ev

    def sample_prep(self, l, s):
        d, P = self.d, self.P
        KTa, Va, KTb, Vb, KPT, lfc = self.KTa, self.Va, self.KTb, self.Vb, self.KPT, self.lfc
        P.dma("pool", lambda e: e.dma_start(out=KTa[:, :, 0:1024], in_=d["cka"][l, s].rearrange("c p n -> p c n")), writes=[KTa])
        P.dma("pool", lambda e: e.dma_start(out=KTb[:, :, 0:1024], in_=d["ckb"][l, s].rearrange("c p n -> p c n")), writes=[KTb])
        P.dma("pool", lambda e: e.dma_start(out=KPT[:, 0:1024], in_=d["ckpT"][l, s]), writes=[KPT])
        for kt in range(8):
            P.dma("pool", lambda e, kt=kt: e.dma_start(out=Va[:, kt, :, 0:64], in_=d["cva"][l, s, 128 * kt:128 * (kt + 1), :].rearrange("p (h d) -> p h d", h=8)), writes=[Va])
            P.dma("pool", lambda e, kt=kt: e.dma_start(out=Vb[:, kt, :, 0:128], in_=d["cvb"][l, s, 128 * kt:128 * (kt + 1), :].rearrange("p (h d) -> p h d", h=4)), writes=[Vb])
        P.dma("sp", lambda e: e.dma_start(out=lfc[:, :, :], in_=d["clf"][l, s].rearrange("(t p) h -> p t h", p=128)), writes=[lfc])
        ck = [[self.palloc() for _ in range(2)] for _ in range(2)]
        for lc in range(2):
            for hf in range(2):
                P.dma("pool", lambda e, lc=lc, hf=hf: e.dma_start(out=ck[lc][hf][:, :], in_=d["cckT"][l, s, lc, :, hf * 512:(hf + 1) * 512]), writes=[ck[lc][hf]])
        for hf in range(2):
            self.mla_kv(l, [ck[0][hf], ck[1][hf]], [(0, 512)], hf * 512, hf * 4, 128, 4)
        for lc in range(2):
            for hf in range(2):
                self.pfree(ck[lc][hf])
        for kt in range(8):
            self.cumsum(kt, lfc, lfc[:, kt, :], 128)

    def build(self):
        self.setup()
        NP, NS = self.NP, self.NS
        carry = self.carry
        for l in range(self.L):
            self.layer_params(l)
            for s in range(NP):
                self.V(lambda e: e.memset(carry[:, :], 0.0), [], [carry])
                for B in range(self.nblk):
                    self.block(l, "p", s, B)
            for s in range(NS):
                self.V(lambda e: e.memset(carry[:, :], 0.0), [], [carry])
                self.kind = "s"
                self.sample_prep(l, s)
                self.block(l, "s", s, 0)
        self.P.wait_all_dma("sp")
        self.P.finish()
        return self.nc


def build_program(NP=4, NS=2, L=2, nblk=4, stop_after=None, WSLOTS=2):
    kb = KB(NP, NS, L, WSLOTS=WSLOTS, stop_after=stop_after)
    kb.nblk = nblk
    nc = kb.build()
    n = {e: len(kb.P.q[e]) for e in ENGINES}
    print("instr counts:", n, "dma sems:", len(kb.P.owners))
    return nc


def _rope_tab(half, d, theta):
    inv = np.exp(np.float32(-math.log(theta)) * np.arange(half, dtype=np.float32) * np.float32(2.0 / d)).astype(np.float32)
    pos = np.arange(2048, dtype=np.float32)
    ang = (pos[:, None] * inv[None, :]).astype(np.float32)
    tab = np.stack([np.cos(ang), np.sin(ang)], axis=1).astype(np.float32)
    return np.ascontiguousarray(tab.reshape(16, 128, 2, half).transpose(1, 0, 2, 3))


def shared_inputs(inp):
    f = np.float32
    A = lambda a: np.ascontiguousarray(a, dtype=f)
    w = inp["w_in_mix"]
    sp = np.cumsum([0, 512, 512, 512, 8, 512, 512, 512, 384, 256, 32])
    seg = {n: w[:, :, sp[i]:sp[i + 1]] for i, n in enumerate(["qa", "ka", "va", "fa", "qb", "kb", "vb", "dq", "dkv", "kr"])}
    wmix = np.concatenate([seg[n] for n in ["qa", "ka", "va", "qb", "kb", "vb", "dq", "dkv", "kr", "fa"]], axis=2)
    wuq = inp["mla_w_uq"].reshape(2, 384, 8, 96)
    wuq = np.concatenate([wuq[..., :64].reshape(2, 384, 512), wuq[..., 64:].reshape(2, 384, 256)], axis=2)
    wukv = inp["mla_w_ukv"]
    k = np.arange(128)
    sh = {
        "w1i": A(inp["ffn1_w_in"]), "w1o": A(inp["ffn1_w_out"]), "w2i": A(inp["ffn2_w_in"]), "w2o": A(inp["ffn2_w_out"]),
        "lng": A(inp["ln_g"]), "lnb": A(inp["ln_b"]), "wmix": A(wmix), "bfg": A(inp["b_forget"]),
        "dlam": A(inp["diff_lambda"].reshape(2, 256)), "dng": A(inp["diff_norm_g"]), "gq": A(inp["mla_q_norm_g"]),
        "wuq": A(wuq), "gkv": A(inp["mla_kv_norm_g"]),
        "wuk": A(wukv[..., :64].reshape(2, 256, 512)), "wuv": A(wukv[..., 64:].reshape(2, 256, 512)),
        "wbr": A(inp["w_branch"]), "wg": A(inp["w_gate"]), "bg": A(inp["b_gate"].reshape(2, 24, 128).transpose(0, 2, 1)),
        "wo": A(inp["w_out"]), "wpg": A(inp["ple_w_gate"]), "bpg": A(inp["ple_b_gate"]), "wpp": A(inp["ple_w_proj"]),
        "c_ident": np.eye(128, dtype=f),
        "c_maskc": np.where(k[:, None] <= k[None, :], 0.0, NEG).astype(f),
        "c_maskd": np.where((k[:, None] // 64) <= (k[None, :] // 64), 0.0, NEG).astype(f),
        "c_tri": (k[:, None] <= k[None, :]).astype(f),
        "c_ropeB": _rope_tab(8, 16, 500000.0), "c_ropeC": _rope_tab(16, 32, 10000.0),
    }
    return sh


def core_inputs(inp, sh, pseqs, sseqs):
    f = np.float32
    A = lambda a: np.ascontiguousarray(a, dtype=f)
    m = dict(sh)
    m["xp"] = A(inp["x_prompt"][pseqs])
    pp = inp["p_prompt"][:, pseqs]
    m["ppT"] = A(pp.transpose(0, 1, 3, 2).reshape(2, len(pseqs), 2, 128, 2048))
    if len(sseqs):
        ns = len(sseqs)
        m["xs"] = A(inp["x_sample"][sseqs])
        m["psT"] = A(inp["p_sample"][:, sseqs].transpose(0, 1, 3, 2).reshape(2, ns, 2, 128, 64))
        m["cka"] = A(inp["cache_fox_k"][:, sseqs].reshape(2, ns, 1024, 4, 128).transpose(0, 1, 3, 4, 2))
        m["cva"] = A(inp["cache_fox_v"][:, sseqs].reshape(2, ns, 1024, 512))
        m["clf"] = A(inp["cache_fox_logf"][:, sseqs])
        m["ckb"] = A(inp["cache_diff_k"][:, sseqs].reshape(2, ns, 1024, 4, 128).transpose(0, 1, 3, 4, 2))
        m["cvb"] = A(inp["cache_diff_v"][:, sseqs].reshape(2, ns, 1024, 512))
        m["cckT"] = A(inp["cache_mla_ckv"][:, sseqs].reshape(2, ns, 1024, 2, 128).transpose(0, 1, 3, 4, 2))
        kp = inp["cache_mla_kpe"][:, sseqs].transpose(0, 1, 3, 2)
        z = np.zeros_like(kp)
        m["ckpT"] = A(np.concatenate([kp, z, kp, z], axis=2))
    return m


_NC_CACHE = {}


def kernel(**inputs):
    inp = {k: np.asarray(v) for k, v in inputs.items()}
    ncores = 8
    NP, NS = 4, 2
    if "full" not in _NC_CACHE:
        _NC_CACHE["full"] = build_program(NP, NS, 2)
    nc = _NC_CACHE["full"]
    sh = shared_inputs(inp)
    in_maps = []
    for c in range(ncores):
        in_maps.append(core_inputs(inp, sh, list(range(NP * c, NP * (c + 1))), list(range(NS * c, NS * (c + 1)))))
    res = run_bass_kernel_spmd(nc, in_maps, core_ids=list(range(ncores)))
    R = res.results

    def cat(name, axis):
        return np.concatenate([np.asarray(r[name]) for r in R], axis=axis)

    y_p = cat("y_p", 0)
    y_s = cat("y_s", 0)
    outs = [y_p, y_s]
    shp = {"fox_k": (8, 64), "fox_v": (8, 64), "fox_logf": (8,), "diff_k": (4, 2, 64), "diff_v": (4, 128), "mla_ckv": (256,), "mla_kpe": (32,)}
    for nm in ["fox_k", "fox_v", "fox_logf", "diff_k", "diff_v", "mla_ckv", "mla_kpe"]:
        a = cat(nm + "_p", 1)
        b = cat(nm + "_s", 1)
        outs.append(a.reshape(2, 32, 2048, *shp[nm]))
        outs.append(b.reshape(2, 16, 64, *shp[nm]))
    return tuple(np.ascontiguousarray(o, dtype=np.float32) for o in outs)
```
